# Optimizing a Trainium2 kernel written in Bass

```python
import jax, jax.numpy as jnp
from jax import lax
import numpy as np


D_MODEL = 1024
BATCH = 4
SEQ = 8192
DEPTH = 1

HEAD_DIM = 64
ROPE_THETA = 10000.0
EPS = 1e-6
ATTN_SCALE = HEAD_DIM ** -0.5
Q_BLOCK = 128

DSA_HEADS = 8
DSA_Q_RANK = 256
DSA_KV_RANK = 128
IDX_HEADS = 8
IDX_DIM = 64
IDX_SCALE = IDX_HEADS ** -0.5 * IDX_DIM ** -0.5
DSA_TOPK_MAX = 256

NSA_HEADS = 8
NSA_KV_HEADS = 2
NSA_GROUP = NSA_HEADS // NSA_KV_HEADS
CMP_BLOCK = 32
CMP_STRIDE = 16
SLC_BLOCK = 64
SLC_COUNT = 16
WINDOW = 512
FORCE_BONUS = 1e4

MIX_WIDTH = DSA_HEADS * HEAD_DIM + NSA_HEADS * HEAD_DIM
IN_SIZES = (DSA_Q_RANK, DSA_KV_RANK, IDX_DIM, IDX_HEADS,
            NSA_HEADS * HEAD_DIM, 3 * 2 * NSA_KV_HEADS * HEAD_DIM, NSA_HEADS * 3)
IN_WIDTH = sum(IN_SIZES)

MEM_LEN = 256
XATTN_HEADS = 4
XATTN_HEAD_DIM = 128

D_FF = 4 * D_MODEL

kernel_name = "hybrid_dsa_nsa_memory_block"


def _rms_norm(x, g):
    xf = x.astype(jnp.float32)
    y = xf * lax.rsqrt(jnp.mean(xf * xf, axis=-1, keepdims=True) + EPS)
    return (y * g.astype(jnp.float32)).astype(x.dtype)


def _layer_norm(x, g, b):
    xf = x.astype(jnp.float32)
    mu = jnp.mean(xf, axis=-1, keepdims=True)
    var = jnp.mean(jnp.square(xf - mu), axis=-1, keepdims=True)
    y = (xf - mu) * lax.rsqrt(var + EPS) * g.astype(jnp.float32) + b.astype(jnp.float32)
    return y.astype(x.dtype)


def _rope(x, pos):
    half = x.shape[-1] // 2
    inv = ROPE_THETA ** (-jnp.arange(half, dtype=jnp.float32) / half)
    ang = pos.astype(jnp.float32)[:, None] * inv[None, :]
    cos = jnp.cos(ang)[:, None, :]
    sin = jnp.sin(ang)[:, None, :]
    xf = x.astype(jnp.float32)
    x1, x2 = xf[..., :half], xf[..., half:]
    return jnp.concatenate([x1 * cos - x2 * sin, x1 * sin + x2 * cos], axis=-1).astype(x.dtype)


def _masked_softmax(s, mask):
    s = jnp.where(mask, s, -jnp.inf)
    m = jnp.max(s, axis=-1, keepdims=True)
    m = jnp.where(jnp.isfinite(m), m, 0.0)
    p = jnp.where(mask, jnp.exp(s - m), 0.0)
    return p / jnp.maximum(jnp.sum(p, axis=-1, keepdims=True), 1e-30)


def _hybrid_mixer(h, w_in, dsa_cq_g, dsa_ckv_g, w_dsa_uq, w_dsa_ukv, w_idx_q, idx_k_ln_g, idx_k_ln_b,
                  dsa_qn_g, dsa_kn_g, nsa_cmp_pe, w_nsa_cmp, nsa_qn_g, nsa_kn_g, w_out):
    B, L, _ = h.shape
    G, R, D = NSA_KV_HEADS, NSA_GROUP, HEAD_DIM
    pos = jnp.arange(L, dtype=jnp.int32)
    splits = [int(v) for v in np.cumsum(IN_SIZES)[:-1]]
    proj = h @ w_in
    c_q, c_kv, k_idx, w_idx, q_nsa, kv_nsa, g_nsa = jnp.split(proj, splits, axis=-1)

    c_q = _rms_norm(c_q, dsa_cq_g)
    c_kv = _rms_norm(c_kv, dsa_ckv_g)
    q_d = _rope(_rms_norm((c_q @ w_dsa_uq).reshape(B, L, DSA_HEADS, D), dsa_qn_g), pos)
    k_d, v_d = jnp.split(c_kv @ w_dsa_ukv, 2, axis=-1)
    k_d = _rope(_rms_norm(k_d, dsa_kn_g)[:, :, None, :], pos)[:, :, 0, :]
    q_i = _rope((c_q @ w_idx_q).reshape(B, L, IDX_HEADS, IDX_DIM), pos)
    k_i = _rope(_layer_norm(k_idx, idx_k_ln_g, idx_k_ln_b)[:, :, None, :], pos)[:, :, 0, :]
    w_i = w_idx * IDX_SCALE
    k_dsa = min(DSA_TOPK_MAX, L // 4)

    q_n = _rope(_rms_norm(q_nsa.reshape(B, L, NSA_HEADS, D), nsa_qn_g), pos)
    kv = kv_nsa.reshape(B, L, 3, 2, G, D)
    k_br, v_br = kv[:, :, :, 0], kv[:, :, :, 1]
    k_slc = _rope(_rms_norm(k_br[:, :, 1], nsa_kn_g[1]), pos)
    k_win = _rope(_rms_norm(k_br[:, :, 2], nsa_kn_g[2]), pos)
    v_slc, v_win = v_br[:, :, 1], v_br[:, :, 2]

    n_cmp = (L - CMP_BLOCK) // CMP_STRIDE + 1
    tok = jnp.arange(n_cmp)[:, None] * CMP_STRIDE + jnp.arange(CMP_BLOCK)[None, :]
    cmp_end = tok[:, -1]
    k_blocks = k_br[:, :, 0][:, tok] + nsa_cmp_pe[0][:, None, :]
    v_blocks = v_br[:, :, 0][:, tok] + nsa_cmp_pe[1][:, None, :]
    k_c = jnp.einsum('bnlgd,lde->bnge', k_blocks, w_nsa_cmp[0])
    v_c = jnp.einsum('bnlgd,lde->bnge', v_blocks, w_nsa_cmp[1])
    k_c = _rope(_rms_norm(k_c, nsa_kn_g[0]), cmp_end)

    n_slc = L // SLC_BLOCK
    n_sel = min(SLC_COUNT, n_slc)
    ci = jnp.arange(n_cmp)[:, None] * CMP_STRIDE
    sj = jnp.arange(n_slc)[None, :] * SLC_BLOCK
    overlap = ((ci < sj + SLC_BLOCK) & (ci + CMP_BLOCK > sj)).astype(jnp.float32)
    k_slc_blk = k_slc.reshape(B, n_slc, SLC_BLOCK, G, D).transpose(0, 3, 1, 2, 4)
    v_slc_blk = v_slc.reshape(B, n_slc, SLC_BLOCK, G, D).transpose(0, 3, 1, 2, 4)

    k_win_pad = jnp.pad(k_win, ((0, 0), (WINDOW, 0), (0, 0), (0, 0)))
    v_win_pad = jnp.pad(v_win, ((0, 0), (WINDOW, 0), (0, 0), (0, 0)))

    gates = jax.nn.sigmoid(g_nsa.astype(jnp.float32)).astype(h.dtype).reshape(B, L, NSA_HEADS, 3)

    n_qb = L // Q_BLOCK

    def blocks(a):
        return jnp.moveaxis(a.reshape((B, n_qb, Q_BLOCK) + a.shape[2:]), 1, 0)

    bidx3 = jnp.arange(B)[:, None, None]
    bidx4 = jnp.arange(B)[:, None, None, None]
    gidx4 = jnp.arange(G)[None, :, None, None]
    j_slc = jnp.arange(n_slc)

    def step(args):
        qb, qd, qi, wi, qn, gt = args
        t = qb * Q_BLOCK + jnp.arange(Q_BLOCK, dtype=jnp.int32)

        idx_score = jnp.einsum('bqhs,bqh->bqs',
                               jax.nn.relu(jnp.einsum('bqhd,bsd->bqhs', qi, k_i)), wi).astype(jnp.float32)
        idx_score = jnp.where(pos[None, None, :] <= t[None, :, None], idx_score, -jnp.inf)
        _, sel = lax.top_k(idx_score, k_dsa)
        kg = k_d[bidx3, sel]
        vg = v_d[bidx3, sel]
        s = jnp.einsum('bqhd,bqkd->bhqk', qd, kg).astype(jnp.float32) * ATTN_SCALE
        p = _masked_softmax(s, (sel <= t[None, :, None])[:, None])
        o_dsa = jnp.einsum('bhqk,bqkd->bqhd', p.astype(vg.dtype), vg)

        qg = qn.reshape(B, Q_BLOCK, G, R, D)
        s = jnp.einsum('bqgrd,bngd->bgrqn', qg, k_c).astype(jnp.float32) * ATTN_SCALE
        p_cmp = _masked_softmax(s, cmp_end[None, :] <= t[:, None])
        o_cmp = jnp.einsum('bgrqn,bngd->bqgrd', p_cmp.astype(v_c.dtype), v_c)
        blk_score = jnp.einsum('bgrqn,nj->bgqj', p_cmp, overlap)
        tb = t // SLC_BLOCK
        forced = (j_slc[None, :] == 0) | (j_slc[None, :] == tb[:, None]) | (j_slc[None, :] == tb[:, None] - 1)
        admissible = j_slc[None, :] * SLC_BLOCK <= t[:, None]
        blk_score = jnp.where(admissible, blk_score + jnp.where(forced, FORCE_BONUS, 0.0), -jnp.inf)
        _, sel_blk = lax.top_k(blk_score, n_sel)
        kb = k_slc_blk[bidx4, gidx4, sel_blk].reshape(B, G, Q_BLOCK, n_sel * SLC_BLOCK, D)
        vb = v_slc_blk[bidx4, gidx4, sel_blk].reshape(B, G, Q_BLOCK, n_sel * SLC_BLOCK, D)
        kpos = (sel_blk[..., None] * SLC_BLOCK + jnp.arange(SLC_BLOCK)).reshape(B, G, Q_BLOCK, n_sel * SLC_BLOCK)
        s = jnp.einsum('bqgrd,bgqkd->bgrqk', qg, kb).astype(jnp.float32) * ATTN_SCALE
        p = _masked_softmax(s, (kpos <= t[None, None, :, None])[:, :, None])
        o_slc = jnp.einsum('bgrqk,bgqkd->bqgrd', p.astype(vb.dtype), vb)
        kw = lax.dynamic_slice_in_dim(k_win_pad, qb * Q_BLOCK, WINDOW + Q_BLOCK, axis=1)
        vw = lax.dynamic_slice_in_dim(v_win_pad, qb * Q_BLOCK, WINDOW + Q_BLOCK, axis=1)
        wpos = qb * Q_BLOCK - WINDOW + jnp.arange(WINDOW + Q_BLOCK, dtype=jnp.int32)
        wmask = (wpos[None, :] <= t[:, None]) & (wpos[None, :] > t[:, None] - WINDOW) & (wpos[None, :] >= 0)
        s = jnp.einsum('bqgrd,bkgd->bgrqk', qg, kw).astype(jnp.float32) * ATTN_SCALE
        p = _masked_softmax(s, wmask)
        o_win = jnp.einsum('bgrqk,bkgd->bqgrd', p.astype(vw.dtype), vw)
        gt = gt.reshape(B, Q_BLOCK, G, R, 3)
        o_nsa = o_cmp * gt[..., 0:1] + o_slc * gt[..., 1:2] + o_win * gt[..., 2:3]

        return jnp.concatenate([o_dsa.reshape(B, Q_BLOCK, -1), o_nsa.reshape(B, Q_BLOCK, -1)], axis=-1)

    xs = (jnp.arange(n_qb, dtype=jnp.int32), blocks(q_d), blocks(q_i), blocks(w_i), blocks(q_n), blocks(gates))
    o = lax.map(step, xs)
    o = jnp.moveaxis(o, 0, 1).reshape(B, L, MIX_WIDTH)
    return o @ w_out


def _memory_xattn(h, mem, mem_norm_g, w_xq, w_xk, w_xv, xq_norm_g, xk_norm_g, w_xo):
    B, L, _ = h.shape
    M = mem.shape[1]
    m = _rms_norm(mem, mem_norm_g)
    q = _rms_norm((h @ w_xq).reshape(B, L, XATTN_HEADS, XATTN_HEAD_DIM), xq_norm_g)
    k = _rms_norm((m @ w_xk).reshape(B, M, XATTN_HEADS, XATTN_HEAD_DIM), xk_norm_g)
    v = (m @ w_xv).reshape(B, M, XATTN_HEADS, XATTN_HEAD_DIM)
    s = jnp.einsum('bqhd,bmhd->bhqm', q, k).astype(jnp.float32) * (XATTN_HEAD_DIM ** -0.5)
    p = jax.nn.softmax(s, axis=-1).astype(v.dtype)
    o = jnp.einsum('bhqm,bmhd->bqhd', p, v).reshape(B, L, XATTN_HEADS * XATTN_HEAD_DIM)
    return o @ w_xo


def _sq_relu_mlp(h, w_ff_in, w_ff_out):
    return jnp.square(jax.nn.relu(h @ w_ff_in)) @ w_ff_out


def setup_inputs(seed: int = 0) -> dict:
    key = jax.random.key(seed)
    D = HEAD_DIM
    specs = [
        ("norm1_g", (D_MODEL,), "g", 1),
        ("w_in", (D_MODEL, IN_WIDTH), "w", D_MODEL),
        ("dsa_cq_g", (DSA_Q_RANK,), "g", 1),
        ("dsa_ckv_g", (DSA_KV_RANK,), "g", 1),
        ("w_dsa_uq", (DSA_Q_RANK, DSA_HEADS * D), "w", DSA_Q_RANK),
        ("w_dsa_ukv", (DSA_KV_RANK, 2 * D), "w", DSA_KV_RANK),
        ("w_idx_q", (DSA_Q_RANK, IDX_HEADS * IDX_DIM), "w", DSA_Q_RANK),
        ("idx_k_ln_g", (IDX_DIM,), "g", 1),
        ("idx_k_ln_b", (IDX_DIM,), "b", 1),
        ("dsa_qn_g", (D,), "g", 1),
        ("dsa_kn_g", (D,), "g", 1),
        ("nsa_cmp_pe", (2, CMP_BLOCK, D), "b", 1),
        ("w_nsa_cmp", (2, CMP_BLOCK, D, D), "w", CMP_BLOCK * D),
        ("nsa_qn_g", (D,), "g", 1),
        ("nsa_kn_g", (3, D), "g", 1),
        ("w_out", (MIX_WIDTH, D_MODEL), "w", MIX_WIDTH),
        ("norm2_g", (D_MODEL,), "g", 1),
        ("mem_norm_g", (D_MODEL,), "g", 1),
        ("w_xq", (D_MODEL, XATTN_HEADS * XATTN_HEAD_DIM), "w", D_MODEL),
        ("w_xk", (D_MODEL, XATTN_HEADS * XATTN_HEAD_DIM), "w", D_MODEL),
        ("w_xv", (D_MODEL, XATTN_HEADS * XATTN_HEAD_DIM), "w", D_MODEL),
        ("xq_norm_g", (XATTN_HEAD_DIM,), "g", 1),
        ("xk_norm_g", (XATTN_HEAD_DIM,), "g", 1),
        ("w_xo", (XATTN_HEADS * XATTN_HEAD_DIM, D_MODEL), "w", XATTN_HEADS * XATTN_HEAD_DIM),
        ("norm3_g", (D_MODEL,), "g", 1),
        ("w_ff_in", (D_MODEL, D_FF), "w", D_MODEL),
        ("w_ff_out", (D_FF, D_MODEL), "w", D_FF),
    ]
    keys = jax.random.split(key, len(specs) + 2)
    out = {
        "x": jax.random.normal(keys[0], (BATCH, SEQ, D_MODEL), jnp.float32),
        "mem": jax.random.normal(keys[1], (BATCH, MEM_LEN, D_MODEL), jnp.float32),
    }
    for k, (name, shape, kind, fan) in zip(keys[2:], specs):
        z = jax.random.normal(k, (DEPTH,) + shape, jnp.float32)
        if kind == "w":
            out[name] = z * (fan ** -0.5)
        elif kind == "g":
            out[name] = 1.0 + 0.02 * z
        else:
            out[name] = 0.02 * z
    return out


def reference(x, mem, norm1_g, w_in, dsa_cq_g, dsa_ckv_g, w_dsa_uq, w_dsa_ukv, w_idx_q, idx_k_ln_g, idx_k_ln_b,
              dsa_qn_g, dsa_kn_g, nsa_cmp_pe, w_nsa_cmp, nsa_qn_g, nsa_kn_g, w_out, norm2_g, mem_norm_g,
              w_xq, w_xk, w_xv, xq_norm_g, xk_norm_g, w_xo, norm3_g, w_ff_in, w_ff_out):
    for l in range(DEPTH):
        x = x + _hybrid_mixer(_rms_norm(x, norm1_g[l]), w_in[l], dsa_cq_g[l], dsa_ckv_g[l], w_dsa_uq[l],
                              w_dsa_ukv[l], w_idx_q[l], idx_k_ln_g[l], idx_k_ln_b[l], dsa_qn_g[l], dsa_kn_g[l],
                              nsa_cmp_pe[l], w_nsa_cmp[l], nsa_qn_g[l], nsa_kn_g[l], w_out[l])
        x = x + _memory_xattn(_rms_norm(x, norm2_g[l]), mem, mem_norm_g[l], w_xq[l], w_xk[l], w_xv[l],
                              xq_norm_g[l], xk_norm_g[l], w_xo[l])
        x = x + _sq_relu_mlp(_rms_norm(x, norm3_g[l]), w_ff_in[l], w_ff_out[l])
    return x
```

```python
import numpy as np
from contextlib import ExitStack
import concourse.bass as bass
import concourse.mybir as mybir
from concourse.bass_utils import run_bass_kernel_spmd

F32 = mybir.dt.float32
BF16 = mybir.dt.bfloat16
ALU = mybir.AluOpType
AF = mybir.ActivationFunctionType
AX = mybir.AxisListType

SAME_ENGINE_SYNC = True
SEM_CHUNK = 16000
NSLOT = 8

D_MODEL = 1024
SEQ = 8192
NQB = 32
NKT = 64
NST = 16
NIT = 16
NEG = -1.0e30
EPS = 1e-6
IDX_SCALE = 8 ** -0.5 * 64 ** -0.5
TOPK = 256


class Prog:
    ENGS = ["pe", "act", "dve", "pool", "sp"]
    BLK = {"pe": "tensor", "act": "scalar", "dve": "vector", "pool": "gpsimd", "sp": "sync"}

    def __init__(self, nc):
        self.nc = nc
        self.ops = []
        self.lastw = {}
        self.readers = {}

    def op(self, eng, fn, r=(), w=(), dma=False):
        i = len(self.ops)
        w = list(w) + [k for k in r if k.startswith("pb") and k not in w]
        deps = set()
        for k in r:
            if k in self.lastw:
                deps.add(self.lastw[k])
        for k in w:
            if k in self.lastw:
                deps.add(self.lastw[k])
            for x in self.readers.get(k, {}).values():
                deps.add(x)
        for k in w:
            self.lastw[k] = i
            self.readers[k] = {}
        for k in r:
            if k not in w:
                d = self.readers.setdefault(k, {})
                d[("dma", i) if dma else eng] = i
        fd = set()
        for d in deps:
            o = self.ops[d]
            if (not dma) and (not o["dma"]) and o["eng"] == eng:
                if eng == "pe" or not SAME_ENGINE_SYNC:
                    continue
            fd.add(d)
        self.ops.append(dict(eng=eng, fn=fn, deps=fd, dma=dma, sig=None))
        return i

    def barrier(self):
        last = {}
        dmas = []
        for i, o in enumerate(self.ops):
            if o["fn"] is None:
                continue
            if o["dma"]:
                dmas.append(i)
            else:
                last[o["eng"]] = i
        deps = set(last.values()) | set(dmas)
        for e in self.ENGS:
            self.ops.append(dict(eng=e, fn=None, deps=set(deps), dma=False, sig=None))
        self.lastw = {}
        self.readers = {}

    def emit(self):
        nc = self.nc
        ops = self.ops
        needs = [False] * len(ops)
        for o in ops:
            for d in o["deps"]:
                needs[d] = True
        cnt = {e: 0 for e in self.ENGS}
        dcnt = {e: 0 for e in self.ENGS}
        slot_uses = {}
        for i, o in enumerate(ops):
            e = o["eng"]
            if o["fn"] is None:
                continue
            if o["dma"]:
                k = dcnt[e]
                dcnt[e] += 1
                slot = (e, k % NSLOT)
                slot_uses[slot] = slot_uses.get(slot, 0) + 1
                o["sig"] = ("d", slot, 16 * slot_uses[slot])
            elif needs[i]:
                cnt[e] += 1
                c = cnt[e] - 1
                o["sig"] = ("c", (e, c // SEM_CHUNK), c % SEM_CHUNK + 1)
        with ExitStack() as st:
            sems = {}
            for o in ops:
                if o["sig"] is not None:
                    kk = (o["sig"][0], o["sig"][1])
                    if kk not in sems:
                        sems[kk] = st.enter_context(nc.semaphore(f"s{kk[0]}_{kk[1][0]}_{kk[1][1]}"))
            block = st.enter_context(nc.Block())
            for e in self.ENGS:
                myops = [o for o in ops if o["eng"] == e]

                def body(eng, e=e, myops=myops):
                    waited = {}

                    def wait(sig):
                        kind, key, val = sig
                        kk = (kind, key)
                        if waited.get(kk, 0) >= val:
                            return
                        eng.wait_ge(sems[kk], val)
                        waited[kk] = val
                    for o in myops:
                        for d in sorted(o["deps"]):
                            wait(ops[d]["sig"])
                        if o["fn"] is None:
                            continue
                        if o["dma"] and o["sig"][2] > 16:
                            wait(("d", o["sig"][1], o["sig"][2] - 16))
                        ins = o["fn"](eng)
                        if o["sig"] is not None:
                            ins.then_inc(sems[(o["sig"][0], o["sig"][1])], 16 if o["dma"] else 1)
                    if e == "sp":
                        for slot, uses in slot_uses.items():
                            wait(("d", slot, 16 * uses))
                        for ee in self.ENGS:
                            if cnt[ee] > 0:
                                c = cnt[ee] - 1
                                wait(("c", (ee, c // SEM_CHUNK), c % SEM_CHUNK + 1))
                getattr(block, self.BLK[e])(body)


class Arena:
    def __init__(self, ap_f32, size_f32):
        self.ap = ap_f32
        self.size = size_f32
        self.off = 0
        self.peak = 0

    def alloc(self, n, dt=F32):
        nb = n * (4 if dt == F32 else 2)
        nf = (nb + 3) // 4
        assert self.off + nf <= self.size, f"SBUF arena overflow: {self.off}+{nf}>{self.size}"
        v = self.ap[:, self.off:self.off + nf]
        self.off += nf
        self.peak = max(self.peak, self.off)
        if dt != F32:
            v = v.bitcast(dt)
        return v

    def mark(self):
        return self.off

    def release(self, m):
        self.off = m


C_CQ = (0, 256)
C_CKV = (256, 384)
C_KIDX = (384, 448)
C_WIDX = (448, 456)
C_QNSA = (456, 968)
C_CMPK = (968, 1096)
C_CMPV = (1096, 1224)
C_SLCK = (1224, 1352)
C_SLCV = (1352, 1480)
C_WINK = (1480, 1608)
C_WINV = (1608, 1736)
C_GNSA = (1736, 1760)
W1Q_GROUPS = [C_CQ, C_WIDX, C_GNSA, C_QNSA]
W1K_GROUPS = [C_SLCK, C_SLCV, C_WINK, C_WINV, C_CKV, C_KIDX]
W1C_GROUPS = [C_CMPK, C_CMPV]

import os
LEVEL = float(os.environ.get('K1_LEVEL', '99'))
K1_NST = int(os.environ.get('K1_NST', '16'))


def build_program(dbg=None, nqb=NQB, phases=(0, 1, 2, 3), qlist=None, dbg_i=-1):
    dbg = dbg or {}
    nc = bass.Bass("TRN2", target_bir_lowering=False)

    def din(name, shape, dt=F32):
        return nc.dram_tensor(name, list(shape), dt, kind="ExternalInput").ap()

    xT = din("xT", [1024, SEQ])
    xqT = din("xqT", [1024, NQB * 128])
    xq = din("xq", [NQB * 128, 1024])
    memT = din("memT", [1024, 256])
    W = {}
    for nm, shp in [("norm1_g", [1024]), ("w_in", [1024, 1760]), ("dsa_cq_g", [256]), ("dsa_ckv_g", [128]),
                    ("w_dsa_uq", [256, 512]), ("w_dsa_ukv", [128, 128]), ("w_idx_q", [256, 512]),
                    ("idx_k_ln_g", [64]), ("idx_k_ln_b", [64]), ("dsa_qn_g", [64]), ("dsa_kn_g", [64]),
                    ("nsa_cmp_pe", [2, 32, 64]), ("w_nsa_cmp", [2, 32, 64, 64]), ("nsa_qn_g", [64]),
                    ("nsa_kn_g", [3, 64]), ("w_out", [1024, 1024]), ("norm2_g", [1024]), ("mem_norm_g", [1024]),
                    ("w_xq", [1024, 512]), ("w_xk", [1024, 512]), ("w_xv", [1024, 512]), ("xq_norm_g", [128]),
                    ("xk_norm_g", [128]), ("w_xo", [512, 1024]), ("norm3_g", [1024]), ("w_ff_in", [1024, 4096]),
                    ("w_ff_out", [4096, 1024])]:
        W[nm] = din(nm, shp)
    c_ident = din("c_ident", [128, 128])
    c_ropeK = din("c_ropeK", [SEQ, 64])
    c_ropeQ = din("c_ropeQ", [NQB * 128, 64])
    c_ropeC = din("c_ropeC", [512, 64])
    c_cmask = din("c_cmask", [128, 256])
    c_causT = din("c_causT", [128, 256])
    c_winT = din("c_winT", [128, 768])
    c_tq = din("c_tq", [128, 1])
    c_cn = din("c_cn", [128, 512])
    c_fbias = din("c_fbias", [NQB * 128, 128])
    c_ovl = din("c_ovl", [512, 128])
    c_pow2 = din("c_pow2", [128, NIT + 1])
    out = nc.dram_tensor("out", [NQB * 128, 1024], F32, kind="ExternalOutput").ap()
    scrO = nc.dram_tensor("scrO", [NQB, 128, 1024], BF16, kind="Internal").ap()
    scrKw = nc.dram_tensor("scrKw", [128, SEQ], BF16, kind="Internal").ap()
    scrVW = nc.dram_tensor("scrVW", [NKT, 128, 132], BF16, kind="Internal").ap()
    dbg_out = {}
    for nm, shp in dbg.items():
        dbg_out[nm] = nc.dram_tensor("dbg_" + nm, list(shp), F32, kind="ExternalOutput").ap()

    P = Prog(nc)
    with ExitStack() as st:
        ARENA_F32 = 53000
        arena_t = st.enter_context(nc.sbuf_tensor("arena", [128, ARENA_F32], F32))
        A = Arena(arena_t[:], ARENA_F32)
        psall = st.enter_context(nc.psum_tensor("psall", [128, 4096], F32))[:]
        pb = [psall[:, k * 512:(k + 1) * 512] for k in range(8)]
        pbh = [p.bitcast(BF16) for p in pb]

        def dma(out_, in_, r=(), w=(), eng="sp", **kw):
            P.op(eng, lambda e: e.dma_start(out=out_, in_=in_, **kw), r=r, w=w, dma=True)

        def tt(eng, out_, in0, in1, op, r=(), w=()):
            P.op(eng, lambda e: e.tensor_tensor(out=out_, in0=in0, in1=in1, op=op), r=r, w=w)

        def ts(eng, out_, in0, s1, s2, op0, op1=None, r=(), w=(), accum=None):
            kw = {}
            if op1 is not None:
                kw["op1"] = op1
            if accum is not None:
                kw["accum_out"] = accum
            P.op(eng, lambda e: e.tensor_scalar(out=out_, in0=in0, scalar1=s1, scalar2=s2, op0=op0, **kw), r=r, w=w)

        def cp(eng, out_, in_, r=(), w=()):
            if eng == "act":
                P.op(eng, lambda e: e.copy(out=out_, in_=in_), r=r, w=w)
            else:
                P.op(eng, lambda e: e.tensor_copy(out=out_, in_=in_), r=r, w=w)

        def act(out_, in_, func, r=(), w=(), **kw):
            P.op("act", lambda e: e.activation(out=out_, in_=in_, func=func, **kw), r=r, w=w)

        def mm(out_, lhsT, rhs, start, stop, r=(), w=(), **kw):
            P.op("pe", lambda e: e.matmul(out_, lhsT=lhsT, rhs=rhs, start=start, stop=stop, **kw), r=r, w=w)

        def tp(out_, in_, r=(), w=()):
            P.op("pe", lambda e: e.transpose(out=out_, in_=in_, identity=ident_bf), r=list(r) + ["ident"], w=w)

        def memset(eng, ap, val, w=()):
            P.op(eng, lambda e: e.memset(ap, val), w=w)

        def red(out_, in_, op, r=(), w=()):
            P.op("dve", lambda e: e.tensor_reduce(out=out_, in_=in_, axis=AX.X, op=op), r=r, w=w)

        def rstd_from_ss(ss_ap, n, inv_n, key, eng_r=()):
            ts("dve", ss_ap, ss_ap, inv_n, EPS, ALU.mult, ALU.add, r=[key], w=[key])
            act(ss_ap, ss_ap, AF.Sqrt, r=[key], w=[key])
            P.op("dve", lambda e: e.reciprocal(out=ss_ap, in_=ss_ap), r=[key], w=[key])

        def dump(name, ap_sb):
            if name not in dbg_out:
                return
            dst = dbg_out[name]
            n = ap_sb.shape[1]
            CH = 256
            for ci, off in enumerate(range(0, n, CH)):
                w_ = min(CH, n - off)
                tmpb = dbg_tmp[ci % 2]
                cp("dve", tmpb[:, 0:w_], ap_sb[:, off:off + w_], r=["*"], w=[f"dbgtmp{ci % 2}"])
                dma(dst[:, off:off + w_], tmpb[:, 0:w_], r=[f"dbgtmp{ci % 2}"])

        uid = [0]

        def uk(prefix="j"):
            uid[0] += 1
            return f"{prefix}{uid[0]}"

        def zero_bank(k):
            mm(pb[k], zeros_bf[0:1, 0:128], zeros_bf[0:1, 0:512], True, False, r=["zeros"], w=[f"pb{k}"], skip_group_check=True)

        ident_f = A.alloc(128)
        ident_bf = A.alloc(128, BF16)
        ones_bf = A.alloc(128, BF16)
        zeros_bf = A.alloc(512, BF16)
        neghalf = A.alloc(32)
        dbg_tmp = [A.alloc(256) for _ in range(2)] if dbg else None
        dma(ident_f, c_ident, w=["identf"])
        cp("dve", ident_bf, ident_f, r=["identf"], w=["ident"])
        memset("pool", ones_bf, 1.0, w=["ones"])
        memset("pool", zeros_bf, 0.0, w=["zeros"])
        memset("pool", neghalf, -0.5, w=["neghalf"])

        mark_resident0 = A.mark()
        KT = A.alloc(2 * SEQ, BF16).rearrange("p (a t) -> p a t", a=2)
        VD = A.alloc(NKT * 66, BF16).rearrange("p (t c) -> p t c", c=66)
        VS = A.alloc(NKT * 132, BF16).rearrange("p (t g c) -> p t g c", g=2, c=66)
        KCT = A.alloc(512, BF16)
        VC = A.alloc(4 * 2 * 194, BF16).rearrange("p (t g c) -> p t g c", g=2, c=194)
        mark_resident = A.mark()

        m0 = A.mark()
        W1k = A.alloc(8 * 704, BF16).rearrange("p (c n) -> p c n", c=8)
        W1c = A.alloc(8 * 256, BF16).rearrange("p (c n) -> p c n", c=8)
        Wukv = A.alloc(128, BF16)
        Wc = A.alloc(2 * 32 * 64, BF16).rearrange("p (k l e) -> p k l e", k=2, l=32)
        biasKV = A.alloc(128)
        G6 = A.alloc(6 * 64).rearrange("p (h d) -> p h d", h=6)
        B1 = A.alloc(64)
        GC = A.alloc(64)
        g1 = A.alloc(8)
        gckv = A.alloc(1)
        m1 = A.mark()
        stg = A.alloc(4 * 1760).rearrange("p (c n) -> p c n", c=4)
        stg2 = stg.rearrange("p c n -> p (c n)")

        def bvec(dst, src_vec, n, key):
            dma(dst, src_vec.rearrange("(o n) -> o n", o=1).partition_broadcast(128), w=[key])

        w_in_v = W["w_in"].rearrange("(c p) n -> p c n", p=128)

        def load_w_in(groups, dstW, key, gvec, gkey):
            for hf in range(2):
                col = 0
                for (a, b) in groups:
                    dma(stg[:, :, col:col + (b - a)], w_in_v[:, hf * 4:(hf + 1) * 4, a:b], w=["stg"])
                    col += b - a
                for c4 in range(4):
                    c = hf * 4 + c4
                    if c % 2 == 0:
                        act(dstW[:, c, :], stg[:, c4, 0:col], AF.Copy, scale=gvec[:, c:c + 1], r=["stg", gkey], w=[key])
                    else:
                        ts("dve", dstW[:, c, :], stg[:, c4, 0:col], gvec[:, c:c + 1], None, ALU.mult, r=["stg", gkey], w=[key])

        dma(g1, W["norm1_g"].rearrange("(c p) -> p c", p=128), w=["g1"], allow_slow_non_contiguous=True)
        dma(gckv, W["dsa_ckv_g"].rearrange("(c p) -> p c", p=128), w=["gckv"], allow_slow_non_contiguous=True)
        load_w_in(W1K_GROUPS, W1k, "W1k", g1, "g1")
        load_w_in(W1C_GROUPS, W1c, "W1c", g1, "g1")
        P.barrier()
        s_ukv = stg2[:, 2048:2176]
        dma(s_ukv, W["w_dsa_ukv"], w=["sukv"])
        ts("dve", Wukv, s_ukv, gckv[:, 0:1], None, ALU.mult, r=["sukv", "gckv"], w=["Wukv"])
        s_wc = stg2[:, 2176:2176 + 4096].rearrange("p (k l e) -> p k l e", k=2, l=32)
        wc_src = W["w_nsa_cmp"].rearrange("k l d e -> d k l e")
        for hf in range(2):
            for kv in range(2):
                dma(s_wc[hf * 64:(hf + 1) * 64, kv], wc_src[:, kv], w=["swc"])
        cp("act", Wc, s_wc, r=["swc"], w=["Wc"])
        s_pe = stg2[:, 6400:6464]
        dma(s_pe[0:64, :], W["nsa_cmp_pe"].rearrange("k l d -> (k l) d"), w=["spe"])
        pe_bf = A.alloc(64, BF16)
        cp("dve", pe_bf[0:64, :], s_pe[0:64, :], r=["spe"], w=["pebf"])
        P.op("pe", lambda e: e.transpose(out=pbh[6][0:64, 0:64], in_=pe_bf[0:64, 0:64], identity=ident_bf[0:64, 0:64]), r=["pebf", "ident"], w=["pb6"])
        peT = A.alloc(64, BF16)
        cp("act", peT[0:64, :], pbh[6][0:64, 0:64], r=["pb6"], w=["peT"])
        peB = A.alloc(64 * 128, BF16).rearrange("p (a m) -> p a m", a=64)
        cp("dve", peB[0:64], peT[0:64, :].unsqueeze(2).to_broadcast([64, 64, 128]), r=["peT"], w=["peB"])
        for kv in range(2):
            for l in range(32):
                mm(pb[7][:, kv * 64:(kv + 1) * 64], peB[0:64, kv * 32 + l, :], Wc[0:64, kv, l, :], l == 0, l == 31,
                   r=["peB", "Wc"], w=["pb7"])
        cp("dve", biasKV, pb[7][:, 0:128], r=["pb7"], w=["biasKV"])
        bvec(G6[:, 0, :], W["dsa_kn_g"], 64, "G6")
        bvec(G6[:, 1, :], W["idx_k_ln_g"], 64, "G6")
        for hh, row in [(2, 1), (3, 1), (4, 2), (5, 2)]:
            dma(G6[:, hh, :], W["nsa_kn_g"][row:row + 1, :].partition_broadcast(128), w=["G6"])
        dma(GC, W["nsa_kn_g"][0:1, :].partition_broadcast(128), w=["GC"])
        bvec(B1, W["idx_k_ln_b"], 64, "B1")
        memset("pool", VD[:, :, 64:66], 1.0, w=["VDones"])
        memset("pool", VS[:, :, :, 64:66], 1.0, w=["VSones"])
        memset("pool", VC[:, :, :, 64:66], 1.0, w=["VCones"])
        P.barrier()
        A.release(m1)

        cmpT = A.alloc(2 * (SEQ + 32), BF16).rearrange("p (k t) -> p k t", k=2)
        memset("pool", cmpT[:, :, SEQ:SEQ + 32], 0.0, w=["cmpTpad"])
        m1b = A.mark()
        xs = [A.alloc(8 * 512).rearrange("p (c t) -> p c t", c=8)] * 2
        hT = [A.alloc(8 * 512, BF16).rearrange("p (c t) -> p c t", c=8)] * 2
        xsq = [A.alloc(8 * 512, BF16).rearrange("p (c t) -> p c t", c=8)] * 2
        rbc = A.alloc(512)
        rtok = A.alloc(4)
        pk = [A.alloc(704) for _ in range(2)]
        ropeK = [A.alloc(64) for _ in range(2)]
        KH = [A.alloc(6 * 64).rearrange("p (h d) -> p h d", h=6) for _ in range(2)]
        KHsq2 = [A.alloc(6 * 64).rearrange("p (h d) -> p h d", h=6) for _ in range(2)]
        KHsq = KHsq2[0]
        KHb = [A.alloc(6 * 64, BF16).rearrange("p (h d) -> p h d", h=6) for _ in range(2)]
        kss = A.alloc(8)
        kss2 = [A.alloc(8) for _ in range(2)]
        kmu2 = [A.alloc(1) for _ in range(2)]
        ckvss2 = [A.alloc(1) for _ in range(2)]
        ckvn2 = [A.alloc(128, BF16) for _ in range(2)]
        ckvnT2 = [A.alloc(128, BF16) for _ in range(2)]
        rt = [A.alloc(6 * 32).rearrange("p (h d) -> p h d", h=6) for _ in range(4)]
        kwst = [A.alloc(128, BF16) for _ in range(2)]
        vwst = [A.alloc(132, BF16).rearrange("p (g c) -> p g c", g=2) for _ in range(2)]
        for q_ in range(2):
            memset("pool", vwst[q_][:, :, 64:66], 1.0, w=[f"vwones{q_}"])
        xT_v = xT.rearrange("(c p) t -> p c t", p=128)

        for stI in range(min(NST, K1_NST) if 1 in phases else 0):
            b = 0
            t0 = stI * 512
            dma(xs[b], xT_v[:, :, t0:t0 + 512], w=[f"xs{b}"])
            cp("act", hT[b], xs[b], r=[f"xs{b}"], w=[f"hT{b}"])
            tt("pool", xsq[b], xs[b], xs[b], ALU.mult, r=[f"xs{b}"], w=[f"xsq{b}"])
            for c in range(8):
                mm(pb[7], ones_bf, xsq[b][:, c, :], c == 0, c == 7, r=["ones", f"xsq{b}"], w=["pb7"])
            ts("dve", rbc, pb[7], 1.0 / 1024, EPS, ALU.mult, ALU.add, r=["pb7"], w=["rbc"])
            act(rbc, rbc, AF.Sqrt, r=["rbc"], w=["rbc"])
            P.op("dve", lambda e: e.reciprocal(out=rbc, in_=rbc), r=["rbc"], w=["rbc"])
            for j in range(4):
                for c in range(8):
                    mm(pb[6][:, 2 * j:2 * j + 2], xsq[b][:, c, j * 128:(j + 1) * 128], ones_bf[:, 0:2], c == 0, c == 7,
                       r=["ones", f"xsq{b}"], w=["pb6"])
            ts("dve", rtok, pb[6][:, 0:8:2], 1.0 / 1024, EPS, ALU.mult, ALU.add, r=["pb6"], w=["rtok"])
            act(rtok, rtok, AF.Sqrt, r=["rtok"], w=["rtok"])
            P.op("dve", lambda e: e.reciprocal(out=rtok, in_=rtok), r=["rtok"], w=["rtok"])
            for kv in range(2 if LEVEL >= 2 else 0):
                for c in range(8):
                    mm(pb[4 + kv], W1c[:, c, kv * 128:(kv + 1) * 128], hT[b][:, c, :], c == 0, c == 7,
                       r=["W1c", f"hT{b}"], w=[f"pb{4 + kv}"])
                tt("dve", cmpT[:, kv, t0:t0 + 512], pb[4 + kv], rbc, ALU.mult, r=[f"pb{4 + kv}", "rbc"], w=["cmpT"])
            def ktile(stI, j, q, b=b):
                tile = stI * 4 + j
                tsl = slice(j * 128, (j + 1) * 128)
                gsl = slice(tile * 128, (tile + 1) * 128)
                ubank = 2 if q == 0 else 5
                sq_, css_, cn_, cT_, mu_, ss_ = KHsq2[q], ckvss2[q], ckvn2[q], ckvnT2[q], kmu2[q], kss2[q]
                ta, tb = rt[2 * q], rt[2 * q + 1]
                dma(ropeK[q], c_ropeK[gsl, :], w=[f"ropeK{q}"])
                for c in range(8):
                    mm(pb[0], hT[b][:, c, tsl], W1k[:, c, 0:512], c == 0, c == 7, r=["W1k", f"hT{b}"], w=["pb0"])
                for c in range(8):
                    mm(pb[1][:, 0:192], hT[b][:, c, tsl], W1k[:, c, 512:704], c == 0, c == 7, r=["W1k", f"hT{b}"], w=["pb1"])
                act(pk[q][:, 0:512], pb[0], AF.Copy, scale=rtok[:, j:j + 1], r=["pb0", "rtok"], w=[f"pk{q}"])
                act(pk[q][:, 512:704], pb[1][:, 0:192], AF.Copy, scale=rtok[:, j:j + 1], r=["pb1", "rtok"], w=[f"pk{q}"])
                yield
                act(sq_.rearrange("p h d -> p (h d)")[:, 0:128], pk[q][:, 512:640], AF.Square, accum_out=css_, r=[f"pk{q}"], w=[f"KHsq{q}", f"ckvss{q}"])
                rstd_from_ss(css_, 1, 1.0 / 128, f"ckvss{q}")
                ts("dve", cn_, pk[q][:, 512:640], css_[:, 0:1], None, ALU.mult, r=[f"pk{q}", f"ckvss{q}"], w=[f"ckvn{q}"])
                yield
                tp(pbh[6][:, 128:256], cn_, r=[f"ckvn{q}"], w=["pb6"])
                cp("act", cT_, pbh[6][:, 128:256], r=["pb6"], w=[f"ckvnT{q}"])
                yield
                mm(pb[ubank][:, 0:128], cT_, Wukv, True, True, r=[f"ckvnT{q}", "Wukv"], w=[f"pb{ubank}"])
                kh = KH[q]
                cp("dve", kh[:, 0, :], pb[ubank][:, 0:64], r=[f"pb{ubank}"], w=[f"KH{q}"])
                cp("act", VD[:, tile, 0:64], pb[ubank][:, 64:128], r=[f"pb{ubank}"], w=[f"VD{tile}"])
                yield
                cp("act", kh[:, 2:4, :], pk[q][:, 0:128].rearrange("p (h d) -> p h d", h=2), r=[f"pk{q}"], w=[f"KH{q}"])
                cp("act", kh[:, 4:6, :], pk[q][:, 256:384].rearrange("p (h d) -> p h d", h=2), r=[f"pk{q}"], w=[f"KH{q}"])
                cp("act", VS[:, tile, :, 0:64], pk[q][:, 128:256].rearrange("p (g d) -> p g d", g=2), r=[f"pk{q}"], w=[f"VS{tile}"])
                cp("act", vwst[q][:, :, 0:64], pk[q][:, 384:512].rearrange("p (g d) -> p g d", g=2), r=[f"pk{q}", f"vwones{q}"], w=[f"vwst{q}"])
                dma(scrVW[tile], vwst[q].rearrange("p g c -> p (g c)"), r=[f"vwst{q}"], w=[f"scrVW{tile}"])
                red(mu_, pk[q][:, 640:704], ALU.add, r=[f"pk{q}"], w=[f"kmu{q}"])
                ts("dve", mu_, mu_, -1.0 / 64, None, ALU.mult, r=[f"kmu{q}"], w=[f"kmu{q}"])
                ts("dve", kh[:, 1, :], pk[q][:, 640:704], mu_[:, 0:1], None, ALU.add, r=[f"kmu{q}", f"pk{q}"], w=[f"KH{q}"])
                yield
                tt("dve", sq_, kh, kh, ALU.mult, r=[f"KH{q}"], w=[f"KHsq{q}"])
                red(ss_[:, 0:6], sq_, ALU.add, r=[f"KHsq{q}"], w=[f"kss{q}"])
                rstd_from_ss(ss_[:, 0:6], 6, 1.0 / 64, f"kss{q}")
                yield
                tt("dve", kh, kh, ss_[:, 0:6].unsqueeze(2).to_broadcast([128, 6, 64]), ALU.mult, r=[f"KH{q}", f"kss{q}"], w=[f"KH{q}"])
                tt("dve", kh, kh, G6, ALU.mult, r=[f"KH{q}", "G6"], w=[f"KH{q}"])
                tt("dve", kh[:, 1, :], kh[:, 1, :], B1, ALU.add, r=[f"KH{q}", "B1"], w=[f"KH{q}"])
                yield
                cosb = ropeK[q][:, 0:32].unsqueeze(1).to_broadcast([128, 6, 32])
                sinb = ropeK[q][:, 32:64].unsqueeze(1).to_broadcast([128, 6, 32])
                x1 = kh[:, :, 0:32]
                x2 = kh[:, :, 32:64]
                tt("dve", ta, x1, cosb, ALU.mult, r=[f"KH{q}", f"ropeK{q}"], w=[f"rta{q}"])
                tt("dve", tb, x2, sinb, ALU.mult, r=[f"KH{q}", f"ropeK{q}"], w=[f"rtb{q}"])
                tt("dve", KHb[q][:, :, 0:32], ta, tb, ALU.subtract, r=[f"rta{q}", f"rtb{q}"], w=[f"KHb{q}"])
                yield
                tt("dve", ta, x1, sinb, ALU.mult, r=[f"KH{q}", f"ropeK{q}"], w=[f"rta{q}"])
                tt("dve", tb, x2, cosb, ALU.mult, r=[f"KH{q}", f"ropeK{q}"], w=[f"rtb{q}"])
                tt("dve", KHb[q][:, :, 32:64], ta, tb, ALU.add, r=[f"rta{q}", f"rtb{q}"], w=[f"KHb{q}"])
                yield
                khb2 = KHb[q].rearrange("p (a two) d -> p a (two d)", two=2)
                for a3 in range(3):
                    tp(pbh[3][:, a3 * 128:(a3 + 1) * 128], khb2[:, a3, :], r=[f"KHb{q}"], w=["pb3"])
                cp("act", KT[:, :, gsl], pbh[3][:, 0:256].rearrange("p (a t) -> p a t", a=2), r=["pb3"], w=[f"KT{tile}"])
                cp("dve", kwst[q], pbh[3][:, 256:384], r=["pb3"], w=[f"kwst{q}"])
                dma(scrKw[:, gsl], kwst[q], r=[f"kwst{q}"], w=[f"scrKw{tile}"])
                yield

            if LEVEL >= 3:
                for jp in range(2):
                    ga_, gb_ = ktile(stI, 2 * jp, 0), ktile(stI, 2 * jp + 1, 1)
                    la_ = lb_ = True
                    while la_ or lb_:
                        if la_:
                            try:
                                next(ga_)
                            except StopIteration:
                                la_ = False
                        if lb_:
                            try:
                                next(gb_)
                            except StopIteration:
                                lb_ = False

        if 1 in phases and LEVEL >= 7:
            P.barrier()
            A.release(m1b)
            rt = [A.alloc(6 * 32).rearrange("p (h d) -> p h d", h=6) for _ in range(4)]
            kss = A.alloc(8)
            ropeC = A.alloc(4 * 64).rearrange("p (t d) -> p t d", t=4)
            dma(ropeC, c_ropeC.rearrange("(t p) d -> p t d", p=128), w=["ropeC"])
            ovl_f = A.alloc(4 * 128).rearrange("p (t j) -> p t j", t=4)
            dma(ovl_f, c_ovl.rearrange("(t p) j -> p t j", p=128), w=["ovlf"])
            for g in range(2):
                cp("dve", VC[:, :, g, 66:194], ovl_f, r=["ovlf"], w=["VC"])
            kc = A.alloc(2 * 64).rearrange("p (g d) -> p g d", g=2)
            kcsq = A.alloc(2 * 64).rearrange("p (g d) -> p g d", g=2)
            kcb = A.alloc(2 * 64, BF16)
            for nb in range(4):
                for kv in range(2):
                    for g in range(2):
                        bank = pb[0] if g == 0 else pb[2]
                        for l in range(32):
                            s0 = nb * 2048 + l
                            mm(bank[:, kv * 64:(kv + 1) * 64], cmpT[g * 64:(g + 1) * 64, kv, s0:s0 + 16 * 127 + 1:16],
                               Wc[g * 64:(g + 1) * 64, kv, l, :], l == 0, l == 31, r=["cmpT", "cmpTpad", "Wc"], w=["pb0" if g == 0 else "pb2"])
                bk = biasKV.rearrange("p (k e) -> p k e", k=2)
                for g in range(2):
                    bank = pb[0] if g == 0 else pb[2]
                    bkey = "pb0" if g == 0 else "pb2"
                    tt("dve", kc[:, g, :], bank[:, 0:64], bk[:, 0, :], ALU.add, r=[bkey, "biasKV"], w=["kc"])
                    tt("dve", VC[:, nb, g, 0:64], bank[:, 64:128], bk[:, 1, :], ALU.add, r=[bkey, "biasKV"], w=["VC"])
                tt("dve", kcsq, kc, kc, ALU.mult, r=["kc"], w=["kcsq"])
                red(kss[:, 6:8], kcsq, ALU.add, r=["kcsq"], w=["kss2"])
                ts("dve", kss[:, 6:8], kss[:, 6:8], 1.0 / 64, EPS, ALU.mult, ALU.add, r=["kss2"], w=["kss2"])
                act(kss[:, 6:8], kss[:, 6:8], AF.Sqrt, r=["kss2"], w=["kss2"])
                P.op("dve", lambda e: e.reciprocal(out=kss[:, 6:8], in_=kss[:, 6:8]), r=["kss2"], w=["kss2"])
                tt("dve", kc, kc, kss[:, 6:8].unsqueeze(2).to_broadcast([128, 2, 64]), ALU.mult, r=["kc", "kss2"], w=["kc"])
                tt("dve", kc, kc, GC.unsqueeze(1).to_broadcast([128, 2, 64]), ALU.mult, r=["kc", "GC"], w=["kc"])
                cosb = ropeC[:, nb, 0:32].unsqueeze(1).to_broadcast([128, 2, 32])
                sinb = ropeC[:, nb, 32:64].unsqueeze(1).to_broadcast([128, 2, 32])
                kcb3 = kcb.rearrange("p (g d) -> p g d", g=2)
                tt("dve", rt[0][:, 0:2, :], kc[:, :, 0:32], cosb, ALU.mult, r=["kc", "ropeC"], w=["rt0"])
                tt("dve", rt[1][:, 0:2, :], kc[:, :, 32:64], sinb, ALU.mult, r=["kc", "ropeC"], w=["rt1"])
                tt("dve", rt[2][:, 0:2, :], kc[:, :, 0:32], sinb, ALU.mult, r=["kc", "ropeC"], w=["rt2"])
                tt("dve", rt[3][:, 0:2, :], kc[:, :, 32:64], cosb, ALU.mult, r=["kc", "ropeC"], w=["rt3"])
                tt("dve", kcb3[:, :, 0:32], rt[0][:, 0:2, :], rt[1][:, 0:2, :], ALU.subtract, r=["rt0", "rt1"], w=["kcb"])
                tt("dve", kcb3[:, :, 32:64], rt[2][:, 0:2, :], rt[3][:, 0:2, :], ALU.add, r=["rt2", "rt3"], w=["kcb"])
                tp(pbh[6][:, 0:128], kcb, r=["kcb"], w=["pb6"])
                cp("act", KCT[:, nb * 128:(nb + 1) * 128], pbh[6][:, 0:128], r=["pb6"], w=["KCT"])
            P.barrier()
            dump("KT", KT.rearrange("p a t -> p (a t)"))
            dump("VD", VD.rearrange("p t c -> p (t c)"))
            dump("VS", VS.rearrange("p t g c -> p (t g c)"))
            dump("KCT", KCT)
            dump("VC", VC.rearrange("p t g c -> p (t g c)"))
        P.barrier()
        A.release(m0)

        if 2 in phases:
            W1q = A.alloc(8 * 800, BF16).rearrange("p (c n) -> p c n", c=8)
            Wq2 = A.alloc(2 * 1024, BF16).rearrange("p (c n) -> p c n", c=2)
            GQ = A.alloc(24 * 64).rearrange("p (h d) -> p h d", h=24)
            g1 = A.alloc(8)
            gcq = A.alloc(2)
            cmask = A.alloc(256)
            causT = A.alloc(256, BF16).rearrange("p (k q) -> p k q", k=2)
            winT = A.alloc(768, BF16).rearrange("p (k q) -> p k q", k=6)
            tq = A.alloc(1)
            cn = A.alloc(512)
            pow2 = A.alloc(NIT + 1)
            m2 = A.mark()
            stg = A.alloc(4 * 1760).rearrange("p (c n) -> p c n", c=4)
            stg2 = stg.rearrange("p c n -> p (c n)")
            dma(g1, W["norm1_g"].rearrange("(c p) -> p c", p=128), w=["g1"], allow_slow_non_contiguous=True)
            dma(gcq, W["dsa_cq_g"].rearrange("(c p) -> p c", p=128), w=["gcq"], allow_slow_non_contiguous=True)
            load_w_in(W1Q_GROUPS, W1q, "W1q", g1, "g1")
            P.barrier()
            s_q2 = stg2[:, 0:2048].rearrange("p (c n) -> p c n", c=2)
            dma(s_q2[:, :, 0:512], W["w_dsa_uq"].rearrange("(c p) n -> p c n", p=128), w=["sq2"])
            dma(s_q2[:, :, 512:1024], W["w_idx_q"].rearrange("(c p) n -> p c n", p=128), w=["sq2"])
            for c in range(2):
                ts("dve", Wq2[:, c, :], s_q2[:, c, :], gcq[:, c:c + 1], None, ALU.mult, r=["sq2", "gcq"], w=["Wq2"])
            s_c = stg2[:, 2048:2048 + 1024]
            dma(s_c[:, 0:256], c_causT, w=["sc"])
            dma(s_c[:, 256:1024], c_winT, w=["sc"])
            ts("dve", causT.rearrange("p k q -> p (k q)"), s_c[:, 0:256], -1.0, 30000.0, ALU.add, ALU.mult, r=["sc"], w=["causT"])
            ts("dve", winT.rearrange("p k q -> p (k q)"), s_c[:, 256:1024], -1.0, 30000.0, ALU.add, ALU.mult, r=["sc"], w=["winT"])
            dma(cmask, c_cmask, w=["cmask"])
            dma(tq, c_tq, w=["tq"])
            dma(cn, c_cn, w=["cn"])
            dma(pow2, c_pow2, w=["pow2"])
            for pr in range(8):
                bvec(GQ[:, 2 * pr, :], W["dsa_qn_g"], 64, "GQ")
                memset("pool", GQ[:, 2 * pr + 1, :], 1.0, w=["GQ"])
            for hh in range(16, 24):
                bvec(GQ[:, hh, :], W["nsa_qn_g"], 64, "GQ")
            GQp = GQ.rearrange("p (a two) d -> p a two d", two=2)
            ts("dve", GQp[:, 0:8, 0, :], GQp[:, 0:8, 0, :], 0.125, None, ALU.mult, r=["GQ"], w=["GQ"])
            ts("dve", GQ[:, 16:24, :], GQ[:, 16:24, :], 0.125, None, ALU.mult, r=["GQ"], w=["GQ"])
            P.barrier()
            A.release(m2)

            score = A.alloc(SEQ)
            MR = A.alloc(SEQ, BF16)
            xq_flat = A.alloc(8 * 128)
            xq_s = xq_flat.rearrange("p (c t) -> p c t", c=8)
            mkj = xq_flat.bitcast(BF16)
            hq = A.alloc(8 * 128, BF16).rearrange("p (c t) -> p c t", c=8)
            xsqq = A.alloc(8 * 128, BF16).rearrange("p (c t) -> p c t", c=8)
            pq = A.alloc(800)
            cqn = A.alloc(256, BF16)
            cqnT = A.alloc(256, BF16).rearrange("p (c t) -> p c t", c=2)
            QH = A.alloc(24 * 64).rearrange("p (h d) -> p h d", h=24)
            QHsq = A.alloc(24 * 64).rearrange("p (h d) -> p h d", h=24)
            QHb = A.alloc(24 * 64, BF16).rearrange("p (h d) -> p h d", h=24)
            QTd2 = [A.alloc(8 * 128, BF16).rearrange("p (a t) -> p a t", a=8) for _ in range(2)]
            QTi = A.alloc(8 * 128, BF16).rearrange("p (a t) -> p a t", a=8)
            QTn2 = [[A.alloc(4 * 128, BF16).rearrange("p (a t) -> p a t", a=4) for _ in range(2)] for _ in range(2)]
            for par_ in range(2):
                memset("pool", QTd2[par_][64:128], 0.0, w=[f"QTd{par_}"])
                memset("pool", QTn2[par_][0][64:128], 0.0, w=[f"QTn{par_}"])
                memset("pool", QTn2[par_][1][0:64], 0.0, w=[f"QTn{par_}"])
            memset("pool", QTi[0:64], 0.0, w=["QTi"])
            ropeQ = A.alloc(64)
            rbuf = [A.alloc(512, BF16) for _ in range(2)]
            pbuf2 = [A.alloc(1024, BF16) for _ in range(2)]
            mexp = [[A.alloc(512, BF16) for _ in range(2)] for _ in range(2)]
            mcT = A.alloc(4 * 128, BF16).rearrange("p (k q) -> p k q", k=4)
            bm = [A.alloc(128, BF16) for _ in range(2)]
            bs = [A.alloc(128) for _ in range(2)]
            bs2 = A.alloc(128)
            blk = [A.alloc(128) for _ in range(2)]
            fb = A.alloc(128)
            Onsa = A.alloc(8 * 64).rearrange("p (h d) -> p h d", h=8)
            tmpO = A.alloc(4 * 64).rearrange("p (h d) -> p h d", h=4)
            Ob = A.alloc(1024, BF16)
            OT = A.alloc(1024, BF16)
            kw_t = A.alloc(768, BF16)
            vw_t = A.alloc(6 * 132, BF16).rearrange("p (t g c) -> p t g c", t=6, g=2)
            sm = A.alloc(64)
            rq, cqss, wabs, wsgn, gat, qss = sm[:, 0:1], sm[:, 1:2], sm[:, 2:10], sm[:, 10:18], sm[:, 18:42], A.alloc(24)
            lo, hi, d0, mid, t1, cnttot, thr16 = (A.alloc(1) for _ in range(7))
            cnt = A.alloc(4)
            cntA = A.alloc(1)
            junkA = QH.rearrange("p h d -> p (h d)").bitcast(BF16)
            Hh = A.alloc(NIT + 1)
            Hh2 = A.alloc(NIT + 1)
            m8 = A.alloc(16)
            rsum = A.alloc(8)
            cco = A.alloc(8)
            xqT_v = xqT.rearrange("(c p) t -> p c t", p=128)
            print("phase2 arena used (KiB):", A.off * 4 / 1024)
            def rope(src, dstb, nh, ropet, keys_r, key_w):
                cosb = ropet[:, 0:32].unsqueeze(1).to_broadcast([128, nh, 32])
                sinb = ropet[:, 32:64].unsqueeze(1).to_broadcast([128, nh, 32])
                ta = QHsq[:, 0:nh, 0:32]
                tb = QHsq[:, 0:nh, 32:64]
                x1 = src[:, :, 0:32]
                x2 = src[:, :, 32:64]
                tt("dve", ta, x1, cosb, ALU.mult, r=keys_r, w=["QHsqa"])
                tt("pool", tb, x2, sinb, ALU.mult, r=keys_r, w=["QHsqb"])
                tt("dve", dstb[:, :, 0:32], ta, tb, ALU.subtract, r=["QHsqa", "QHsqb"], w=[key_w])
                tt("dve", ta, x1, sinb, ALU.mult, r=keys_r, w=["QHsqa"])
                tt("pool", tb, x2, cosb, ALU.mult, r=keys_r, w=["QHsqb"])
                tt("dve", dstb[:, :, 32:64], ta, tb, ALU.add, r=["QHsqa", "QHsqb"], w=[key_w])

            def fold_branch(banks, br, first):
                for (bk, h0, nh, stride) in banks:
                    accv = pb[bk][:, 0:nh * stride].rearrange("p (h c) -> p h c", c=stride)
                    ts("dve", rsum[:, 0:nh], accv[:, :, 64], 1e-30, None, ALU.max, r=[f"pb{bk}"], w=["rsum"])
                    P.op("dve", lambda e, nh=nh: e.reciprocal(out=rsum[:, 0:nh], in_=rsum[:, 0:nh]), r=["rsum"], w=["rsum"])
                    gv = gat.rearrange("p (h b) -> p h b", b=3)[:, h0:h0 + nh, br]
                    tt("dve", cco[:, 0:nh], rsum[:, 0:nh], gv, ALU.mult, r=["rsum", "gat"], w=["cco"])
                    cb = cco[:, 0:nh].unsqueeze(2).to_broadcast([128, nh, 64])
                    if first:
                        tt("dve", Onsa[:, h0:h0 + nh, :], accv[:, :, 0:64], cb, ALU.mult, r=[f"pb{bk}", "cco"], w=["Onsa"])
                    else:
                        tt("dve", tmpO[:, 0:nh, :], accv[:, :, 0:64], cb, ALU.mult, r=[f"pb{bk}", "cco"], w=["tmpO"])
                        tt("dve", Onsa[:, h0:h0 + nh, :], Onsa[:, h0:h0 + nh, :], tmpO[:, 0:nh, :], ALU.add, r=["tmpO", "Onsa"], w=["Onsa"])

            MNEG = -30000.0
            NPB = 4
            PVLAG = 2
            pctr = [0]

            def attend(kb_list, g_list, kplane_fn, v_fn, mask_fn, qsel, acc_fn, kkeys=(), vkeys=(), bank_fn=None):
                units = [(kb, g) for kb in kb_list for g in g_list]
                pairs = [units[k:k + 2] for k in range(0, len(units), 2)]
                pend = []
                for pi_, pr in enumerate(pairs):
                    pslot = pctr[0] % 2
                    pctr[0] += 1
                    b0 = 2 * pslot
                    pbf2 = pbuf2[pslot]
                    for ui, (kb, g) in enumerate(pr):
                        bank = b0 + ui
                        sb = pb[bank]
                        mk_ap, mk_keys = mask_fn(kb, g)
                        mm(sb, kplane_fn(kb, g), qsel(g), True, False, r=list(kkeys), w=[f"pb{bank}"], skip_group_check=True)
                        mm(sb, mk_ap, ident_bf.unsqueeze(1).to_broadcast([128, 4, 128]), False, True, r=["ident"] + mk_keys, w=[f"pb{bank}"], skip_group_check=True)
                    n_ = len(pr)
                    act(pbf2[:, 0:512 * n_], psall[:, b0 * 512:(b0 + n_) * 512], AF.Exp, r=[f"pb{b0 + k}" for k in range(n_)], w=[f"pbuf{pslot}"])
                    if len(pend) >= 1:
                        pend.pop(0)()
                    def do_pv(pr=pr, pbf2=pbf2, pslot=pslot, last=(pi_ == len(pairs) - 1)):
                        for ui, (kb, g) in enumerate(pr):
                            vv = v_fn(kb, g)
                            for r_ in range(4):
                                o_ap, okey = acc_fn(g, r_)
                                mm(o_ap, pbf2[:, ui * 512 + r_ * 128:ui * 512 + (r_ + 1) * 128], vv, False, last and ui == len(pr) - 1,
                                   r=[f"pbuf{pslot}"] + list(vkeys), w=[okey], skip_group_check=True)
                    pend.append(do_pv)
                    yield
                while pend:
                    pend.pop(0)()
                yield

            def make_stages(i):
                    L = 256 * (i + 1)
                    nkb = L // 128
                    nck = (L + 511) // 512
                    tsl = slice(i * 128, (i + 1) * 128)
                    par = i % 2
                    QTd = QTd2[par]
                    QTn = QTn2[par]
                    kQd = f"QTd{par}"
                    kQn = f"QTn{par}"

                    def stageA():
                        dma(xq_s, xqT_v[:, :, tsl], w=["xq_s"])
                        dma(ropeQ, c_ropeQ[tsl, :], w=["ropeQ"])
                        dma(fb, c_fbias[tsl, :], w=["fb"])
                        cp("act", hq, xq_s, r=["xq_s"], w=["hq"])
                        tt("pool", xsqq, xq_s, xq_s, ALU.mult, r=["xq_s"], w=["xsqq"])
                        for c in range(8):
                            mm(pb[7][:, 0:2], xsqq[:, c, :], ones_bf[:, 0:2], c == 0, c == 7, r=["ones", "xsqq"], w=["pb7"])
                        cp("dve", rq, pb[7][:, 0:1], r=["pb7"], w=["rq"])
                        rstd_from_ss(rq, 1, 1.0 / 1024, "rq")
                        for c in range(8):
                            mm(pb[6], hq[:, c, :], W1q[:, c, 288:800], c == 0, c == 7, r=["hq", "W1q"], w=["pb6"])
                        for c in range(8):
                            mm(pb[7][:, 0:288], hq[:, c, :], W1q[:, c, 0:288], c == 0, c == 7, r=["hq", "W1q"], w=["pb7"])
                        act(pq[:, 288:800], pb[6], AF.Copy, scale=rq, r=["pb6", "rq"], w=["pqn"])
                        act(pq[:, 0:288], pb[7][:, 0:288], AF.Copy, scale=rq, r=["pb7", "rq"], w=["pqa"])
                        yield
                        act(QHsq.rearrange("p h d -> p (h d)")[:, 0:256], pq[:, 0:256], AF.Square, accum_out=cqss, r=["pqa"], w=["QHsqa", "QHsqb", "cqss"])
                        rstd_from_ss(cqss, 1, 1.0 / 256, "cqss")
                        ts("dve", cqn, pq[:, 0:256], cqss, None, ALU.mult, r=["pqa", "cqss"], w=["cqn"])
                        for c2 in range(2):
                            tp(pbh[6][:, c2 * 128:(c2 + 1) * 128], cqn[:, c2 * 128:(c2 + 1) * 128], r=["cqn"], w=["pb6"])
                        cp("act", cqnT, pbh[6][:, 0:256].rearrange("p (c t) -> p c t", c=2), r=["pb6"], w=["cqnT"])
                        for c2 in range(2):
                            mm(pb[6], cqnT[:, c2, :], Wq2[:, c2, 0:512], c2 == 0, c2 == 1, r=["cqnT", "Wq2"], w=["pb6"])
                        for c2 in range(2):
                            mm(pb[7], cqnT[:, c2, :], Wq2[:, c2, 512:1024], c2 == 0, c2 == 1, r=["cqnT", "Wq2"], w=["pb7"])
                        QHp = QH.rearrange("p (a two) d -> p a two d", two=2)
                        cp("dve", QHp[:, 0:8, 0, :], pb[6].rearrange("p (h d) -> p h d", h=8), r=["pb6"], w=["QH"])
                        cp("act", QHp[:, 0:8, 1, :], pb[7].rearrange("p (h d) -> p h d", h=8), r=["pb7"], w=["QH"])
                        for g in range(2):
                            cp("dve", QHp[:, 8:12, g, :], pq[:, 288 + g * 256:288 + (g + 1) * 256].rearrange("p (r d) -> p r d", r=4), r=["pqn"], w=["QH"])
                        yield
                        act(wabs, pq[:, 256:264], AF.Abs, scale=IDX_SCALE, r=["pqa"], w=["wabs"])
                        act(wsgn, pq[:, 256:264], AF.Sign, r=["pqa"], w=["wsgn"])
                        act(gat, pq[:, 264:288], AF.Exp, scale=-1.0, r=["pqa"], w=["gat"])
                        ts("dve", gat, gat, 1.0, None, ALU.add, r=["gat"], w=["gat"])
                        P.op("dve", lambda e: e.reciprocal(out=gat, in_=gat), r=["gat"], w=["gat"])
                        yield
                        tt("dve", QHsq, QH, QH, ALU.mult, r=["QH"], w=["QHsqa", "QHsqb"])
                        red(qss, QHsq, ALU.add, r=["QHsqa", "QHsqb"], w=["qss"])
                        rstd_from_ss(qss, 24, 1.0 / 64, "qss")
                        memset("dve", qss.rearrange("p (a two) -> p a two", two=2)[:, 0:8, 1], 1.0, w=["qss"])
                        tt("dve", QH, QH, qss.unsqueeze(2).to_broadcast([128, 24, 64]), ALU.mult, r=["QH", "qss"], w=["QH"])
                        tt("dve", QH, QH, GQ, ALU.mult, r=["QH", "GQ"], w=["QH"])
                        rope(QH, QHb, 24, ropeQ, ["QH", "ropeQ"], "QHb")
                        qhb2 = QHb.rearrange("p (a two) d -> p a (two d)", two=2)
                        for a in range(8):
                            tp(pbh[6][:, a * 128:(a + 1) * 128], qhb2[:, a, :], r=["QHb"], w=["pb6"])
                        for a in range(4):
                            tp(pbh[7][:, a * 128:(a + 1) * 128], qhb2[:, 8 + a, :], r=["QHb"], w=["pb7"])
                        cp("act", QTd[0:64], pbh[6][0:64].rearrange("p (a t) -> p a t", a=8), r=["pb6"], w=[kQd])
                        cp("dve", QTi[64:128], pbh[6][64:128].rearrange("p (a t) -> p a t", a=8), r=["pb6"], w=["QTi"])
                        cp("act", QTn[0][0:64], pbh[7][0:64, 0:512].rearrange("p (a t) -> p a t", a=4), r=["pb7"], w=[kQn])
                        cp("dve", QTn[1][64:128], pbh[7][64:128, 0:512].rearrange("p (a t) -> p a t", a=4), r=["pb7"], w=[kQn])
                        if "QT" in dbg_out and i == dbg_i:
                            P.barrier()
                            pass
                            P.barrier()

                        yield
                        for ck in range(nck):
                            k0 = ck * 512
                            wd = min(512, L - k0)
                            sc_ap = score[:, k0:k0 + wd]
                            for h in range(8):
                                sbk = 6 + h % 2
                                mm(pb[sbk][:, 0:wd], QTi[:, h, :], KT[:, 0, k0:k0 + wd], True, True, r=["QTi"], w=[f"pb{sbk}"])
                                act(rbuf[h % 2][:, 0:wd], pb[sbk][:, 0:wd], AF.Relu, scale=wabs[:, h:h + 1], r=[f"pb{sbk}", "wabs"], w=[f"rbuf{h % 2}"])
                                if h == 0:
                                    ts("dve", sc_ap, rbuf[h % 2][:, 0:wd], wsgn[:, 0:1], None, ALU.mult, r=[f"rbuf{h % 2}", "wsgn"], w=["score"])
                                else:
                                    P.op("dve", lambda e, sc_ap=sc_ap, h=h, wd=wd: e.scalar_tensor_tensor(out=sc_ap, in0=rbuf[h % 2][:, 0:wd], scalar=wsgn[:, h:h + 1], in1=sc_ap, op0=ALU.mult, op1=ALU.add),
                                         r=[f"rbuf{h % 2}", "wsgn", "score"], w=["score"])
                                yield
                        yield

                    def stageB():
                        def thr_steps(L=L):
                            P.op("dve", lambda e: e.tensor_reduce(out=lo, in_=score[:, 0:L], axis=AX.X, op=ALU.min), r=["score"], w=["lo"])
                            tt("dve", score[:, L - 256:L], score[:, L - 256:L], cmask, ALU.add, r=["score", "cmask"], w=["score"])
                            P.op("dve", lambda e: e.tensor_reduce(out=hi, in_=score[:, 0:L], axis=AX.X, op=ALU.max), r=["score"], w=["hi"])
                            tt("dve", d0, hi, lo, ALU.subtract, r=["hi", "lo"], w=["d0"])
                            ts("dve", Hh, pow2, d0, None, ALU.mult, r=["pow2", "d0"], w=["Hh"])
                            ts("dve", Hh2, Hh, 2.0, None, ALU.mult, r=["Hh"], w=["Hh2"])
                            tt("dve", mid, lo, Hh[:, 0:1], ALU.add, r=["lo", "Hh"], w=["mid"])
                            yield
                            wA = ((L // 3) // 512) * 512
                            LD = L - wA
                            nch = (LD + 2047) // 2048
                            thrA = TOPK - 0.5 - 0.5 * wA
                            for k in range(NIT):
                                if wA:
                                    act(junkA[:, 0:wA], score[:, LD:L], AF.Sign, scale=-1.0, bias=mid, accum_out=cntA, r=["score", "mid"], w=["QH", "cntA"])
                                for c4 in range(nch):
                                    c0 = c4 * 2048
                                    w4 = min(2048, LD - c0)
                                    seed = None if c4 == 0 else cnt[:, c4 - 1:c4]
                                    rk = ["score", "mid"] + ([f"cnt{c4 - 1}"] if c4 else [])
                                    ts("dve", mkj[:, 0:w4], score[:, c0:c0 + w4], mid, seed, ALU.is_ge, ALU.add, r=rk, w=[f"cnt{c4}", "xq_s"], accum=cnt[:, c4:c4 + 1])
                                ckey, cap, cthr = f"cnt{nch - 1}", cnt[:, nch - 1:nch], TOPK - 0.5
                                if wA:
                                    P.op("dve", lambda e, nch=nch: e.scalar_tensor_tensor(out=cnttot, in0=cntA, scalar=-0.5, in1=cnt[:, nch - 1:nch], op0=ALU.mult, op1=ALU.add),
                                         r=["cntA", f"cnt{nch - 1}"], w=["cnttot"])
                                    ckey, cap, cthr = "cnttot", cnttot, thrA
                                ts("dve", t1, cap, cthr, Hh2[:, k + 1:k + 2], ALU.is_ge, ALU.mult, r=[ckey, "Hh2"], w=["t1"])
                                P.op("dve", lambda e, k=k: e.scalar_tensor_tensor(out=mid, in0=t1, scalar=Hh[:, k + 1:k + 2], in1=mid, op0=ALU.subtract, op1=ALU.add),
                                     r=["t1", "Hh", "mid"], w=["mid"])
                                yield
                            tt("dve", lo, mid, Hh[:, NIT:NIT + 1], ALU.subtract, r=["mid", "Hh"], w=["lo"])
                            yield

                        def nsa_steps(i=i, L=L, nkb=nkb, nck=nck):
                            NB = (16 * i + 14) // 128 + 1
                            for nb in range(NB):
                                ts("dve", mcT[:, nb, :], cn[:, nb * 128:(nb + 1) * 128], tq, 256.0 * i, ALU.subtract, ALU.is_gt, r=["tq", "cn"], w=["mcT"])
                                ts("dve", mcT[:, nb, :], mcT[:, nb, :], MNEG, None, ALU.mult, r=["mcT"], w=["mcT"])
                            for g in range(2):
                                zero_bank(4)
                                zero_bank(5)
                                yield from attend(list(range(NB)), [g],
                                                  lambda kb, g_: KCT[:, kb * 128:(kb + 1) * 128],
                                                  lambda kb, g_: VC[:, kb, g_, 0:194],
                                                  lambda kb, g_: (mcT[:, kb, :], ["mcT"]),
                                                  lambda g_: QTn[g_],
                                                  lambda g_, r_: (pb[4 + r_ // 2][:, (r_ % 2) * 194:(r_ % 2) * 194 + 194], f"pb{4 + r_ // 2}"), kkeys=[kQn])
                                for b2 in range(2):
                                    accv = pb[4 + b2][:, 0:388].rearrange("p (h c) -> p h c", c=194)
                                    ts("dve", rsum[:, 0:2], accv[:, :, 64], 1e-30, None, ALU.max, r=[f"pb{4 + b2}"], w=["rsum"])
                                    P.op("dve", lambda e: e.reciprocal(out=rsum[:, 0:2], in_=rsum[:, 0:2]), r=["rsum"], w=["rsum"])
                                    h0 = g * 4 + 2 * b2
                                    gv = gat.rearrange("p (h b) -> p h b", b=3)[:, h0:h0 + 2, 0]
                                    tt("dve", cco[:, 0:2], rsum[:, 0:2], gv, ALU.mult, r=["rsum", "gat"], w=["cco"])
                                    tt("dve", Onsa[:, h0:h0 + 2, :], accv[:, :, 0:64], cco[:, 0:2].unsqueeze(2).to_broadcast([128, 2, 64]), ALU.mult,
                                       r=[f"pb{4 + b2}", "cco"], w=["Onsa"])
                                    for r2 in range(2):
                                        if b2 == 0 and r2 == 0:
                                            ts("dve", blk[g], accv[:, r2, 66:194], rsum[:, r2:r2 + 1], None, ALU.mult, r=[f"pb{4 + b2}", "rsum"], w=[f"blk{g}"])
                                        else:
                                            P.op("dve", lambda e, accv=accv, r2=r2, g=g: e.scalar_tensor_tensor(out=blk[g], in0=accv[:, r2, 66:194], scalar=rsum[:, r2:r2 + 1], in1=blk[g], op0=ALU.mult, op1=ALU.add),
                                                 r=[f"pb{4 + b2}", "rsum", f"blk{g}"], w=[f"blk{g}"])
                                yield
                            for g in range(2):
                                tt("dve", bs[g], blk[g], fb, ALU.add, r=[f"blk{g}", "fb"], w=[f"bs{g}"])
                                P.op("dve", lambda e, g=g: e.max(out=m8[:, 0:8], in_=bs[g]), r=[f"bs{g}"], w=["m8a"])
                                P.op("dve", lambda e, g=g: e.match_replace(out=bs2, in_to_replace=m8[:, 0:8], in_values=bs[g], imm_value=NEG), r=[f"bs{g}", "m8a"], w=["bs2"])
                                P.op("dve", lambda e: e.max(out=m8[:, 8:16], in_=bs2), r=["bs2"], w=["m8b"])
                                ts("dve", thr16, m8[:, 15:16], -1.0e29, None, ALU.max, r=["m8b"], w=["thr16"])
                                ts("dve", bm[g], bs[g], thr16, None, ALU.is_ge, r=[f"bs{g}", "thr16"], w=[f"bm{g}"])
                                ts("dve", bm[g], bm[g], -1.0, -MNEG, ALU.add, ALU.mult, r=[f"bm{g}"], w=[f"bm{g}"])
                            yield
                            zero_bank(4)
                            zero_bank(5)
                            def slc_mask(ck):
                                k0 = ck * 512
                                wd = min(512, L - k0)
                                nb4 = wd // 128
                                for g in range(2):
                                    me = mexp[g][ck % 2]
                                    cp("pool", me[:, 0:wd].rearrange("p (j s) -> p j s", s=64),
                                       bm[g][:, ck * 8:ck * 8 + 2 * nb4].unsqueeze(2).to_broadcast([128, 2 * nb4, 64]), r=[f"bm{g}"], w=[f"mexp{g}{ck % 2}"])
                                    for b4 in range(nb4):
                                        kb = ck * 4 + b4
                                        if kb >= nkb - 2:
                                            kbl = kb - (nkb - 2)
                                            tt("pool", me[:, b4 * 128:(b4 + 1) * 128], me[:, b4 * 128:(b4 + 1) * 128], causT[:, kbl, :], ALU.add,
                                               r=[f"mexp{g}{ck % 2}", "causT"], w=[f"mexp{g}{ck % 2}"])

                            slc_mask(0)
                            for ck in range(nck):
                                nb4 = min(512, L - ck * 512) // 128
                                if ck + 1 < nck:
                                    slc_mask(ck + 1)
                                yield from attend([ck * 4 + b4 for b4 in range(nb4)], [0, 1],
                                                  lambda kb, g_: KT[:, 1, kb * 128:(kb + 1) * 128],
                                                  lambda kb, g_: VS[:, kb, g_, 0:66],
                                                  lambda kb, g_, ck=ck: (mexp[g_][ck % 2][:, (kb - ck * 4) * 128:(kb - ck * 4 + 1) * 128], [f"mexp{g_}{ck % 2}"]),
                                                  lambda g_: QTn[g_],
                                                  lambda g_, r_: (pb[4 + g_][:, r_ * 66:r_ * 66 + 66], f"pb{4 + g_}"), kkeys=[kQn])
                            fold_branch([(4, 0, 4, 66), (5, 4, 4, 66)], 1, False)
                            yield
                            wlist = [w_ for w_ in range(6) if 2 * i - 4 + w_ >= 0]
                            kb0 = 2 * i - 4 + wlist[0]
                            nw = len(wlist)
                            dma(kw_t[:, 0:nw * 128], scrKw[:, kb0 * 128:(kb0 + nw) * 128], w=["kw_t"])
                            dma(vw_t[:, 0:nw].rearrange("p t g c -> p t (g c)"), scrVW[kb0:kb0 + nw].rearrange("t p c -> p t c"), w=["vw_t"])
                            zero_bank(4)
                            zero_bank(5)
                            yield from attend([2 * i - 4 + w_ for w_ in wlist], [0, 1],
                                              lambda kb, g_: kw_t[:, (kb - kb0) * 128:(kb - kb0 + 1) * 128],
                                              lambda kb, g_: vw_t[:, kb - kb0, g_, 0:66],
                                              lambda kb, g_: (winT[:, kb - (2 * i - 4), :], ["winT"]),
                                              lambda g_: QTn[g_],
                                              lambda g_, r_: (pb[4 + g_][:, r_ * 66:r_ * 66 + 66], f"pb{4 + g_}"),
                                              kkeys=["kw_t", kQn], vkeys=["vw_t"])
                            fold_branch([(4, 0, 4, 66), (5, 4, 4, 66)], 2, False)
                            yield

                        gen_t = thr_steps()
                        gen_n = nsa_steps()
                        ratio = max(1, (2 * nkb + 24 + NIT) // (NIT + 1))
                        alive_t = alive_n = True
                        while alive_t or alive_n:
                            if alive_t:
                                try:
                                    next(gen_t)
                                except StopIteration:
                                    alive_t = False
                            for _ in range(ratio if alive_t else 1000000):
                                if not alive_n:
                                    break
                                try:
                                    next(gen_n)
                                except StopIteration:
                                    alive_n = False
                        if "thr" in dbg_out and i == dbg_i:
                            P.barrier()
                            dump("thr", lo)
                            dump("blk", blk[0])
                            P.barrier()


                    def stageC():
                        zero_bank(4)
                        zero_bank(5)
                        for ck in range(nck):
                            k0 = ck * 512
                            wd = min(512, L - k0)
                            ts("dve", MR[:, k0:k0 + wd], score[:, k0:k0 + wd], lo, MNEG, ALU.is_lt, ALU.mult, r=["score", "lo"], w=["MR"])
                        yield
                        for ck in range(nck):
                            nb4 = min(512, L - ck * 512) // 128
                            yield from attend([ck * 4 + b4 for b4 in range(nb4)], [0, 1],
                                              lambda kb, hf: KT[:, 0, kb * 128:(kb + 1) * 128],
                                              lambda kb, hf: VD[:, kb, 0:66],
                                              lambda kb, hf: (MR[:, kb * 128:(kb + 1) * 128], ["MR"]),
                                              lambda hf: QTd[:, 4 * hf:4 * hf + 4, :],
                                              lambda hf, r_: (pb[4 + hf][:, r_ * 66:r_ * 66 + 66], f"pb{4 + hf}"),
                                              kkeys=[kQd])
                        Ob3 = Ob.rearrange("p (h d) -> p h d", d=64)
                        for half in range(2):
                            accv = pb[4 + half][:, 0:264].rearrange("p (h c) -> p h c", c=66)
                            P.op("dve", lambda e, accv=accv: e.reciprocal(out=rsum[:, 0:4], in_=accv[:, :, 64]), r=[f"pb{4 + half}"], w=["rsum"])
                            tt("dve", Ob3[:, 4 * half:4 * half + 4, :], accv[:, :, 0:64], rsum[:, 0:4].unsqueeze(2).to_broadcast([128, 4, 64]), ALU.mult,
                               r=[f"pb{4 + half}", "rsum"], w=["Ob"])
                        cp("act", Ob[:, 512:1024], Onsa.rearrange("p h d -> p (h d)"), r=["Onsa"], w=["Ob"])
                        if "O" in dbg_out and i == dbg_i:
                            P.barrier()
                            dump("O", Ob)
                            P.barrier()
                        for a in range(8):
                            tp(pbh[4][:, a * 128:(a + 1) * 128], Ob[:, a * 128:(a + 1) * 128], r=["Ob"], w=["pb4"])
                        cp("act", OT, pbh[4], r=["pb4"], w=["OT"])
                        dma(scrO[i], OT, r=["OT"], w=[f"scrO{i}"])

                        yield

                    return stageA, stageB, stageC

            def run_gen(g_):
                for _ in g_:
                    pass

            def interleave(ga, gb, rb=1):
                la = lb = True
                while la or lb:
                    if la:
                        try:
                            next(ga)
                        except StopIteration:
                            la = False
                    for _ in range(rb):
                        if lb:
                            try:
                                next(gb)
                            except StopIteration:
                                lb = False

            ilist = list(qlist) if qlist is not None else list(range(nqb))
            sA, sB, sC = make_stages(ilist[0])
            run_gen(sA())
            sB()
            prevC = sC
            for i_ in ilist[1:]:
                sA, sB, sC = make_stages(i_)
                interleave(prevC(), sA(), rb=2)
                sB()
                prevC = sC
            run_gen(prevC())
            P.barrier()
        A.release(mark_resident)

        if 3 in phases:
            A.release(mark_resident0)
            Wout = A.alloc(8 * 1024, BF16).rearrange("p (c n) -> p c n", c=8)
            Wxq = A.alloc(8 * 512, BF16).rearrange("p (c n) -> p c n", c=8)
            Wxo = A.alloc(4 * 1024, BF16).rearrange("p (c n) -> p c n", c=4)
            KxT = A.alloc(4 * 256, BF16).rearrange("p (h m) -> p h m", h=4)
            Vx = A.alloc(2 * 4 * 130, BF16).rearrange("p (t h c) -> p t h c", t=2, h=4)
            gx = A.alloc(24)
            gxq = A.alloc(128)
            gxk = A.alloc(128)
            m3 = A.mark()
            stg = A.alloc(4096 + 2048)
            stgA = stg[:, 0:4096]
            dma(gx[:, 0:8], W["norm2_g"].rearrange("(c p) -> p c", p=128), w=["gx"], allow_slow_non_contiguous=True)
            dma(gx[:, 8:16], W["mem_norm_g"].rearrange("(c p) -> p c", p=128), w=["gx"], allow_slow_non_contiguous=True)
            dma(gx[:, 16:24], W["norm3_g"].rearrange("(c p) -> p c", p=128), w=["gx"], allow_slow_non_contiguous=True)
            bvec(gxq, W["xq_norm_g"], 128, "gxq")
            bvec(gxk, W["xk_norm_g"], 128, "gxk")
            ts("dve", gxq, gxq, 128 ** -0.5, None, ALU.mult, r=["gxq"], w=["gxq"])
            castn = [0]

            def cast_rows(dst3, src3, nchunk, gcol0=None, skey="stg"):
                for c in range(nchunk):
                    castn[0] += 1
                    if gcol0 is None:
                        eng = ["act", "dve", "pool"][castn[0] % 3]
                        cp(eng, dst3[:, c, :], src3[:, c, :], r=[skey], w=[uk("W")])
                    elif castn[0] % 2 == 0:
                        act(dst3[:, c, :], src3[:, c, :], AF.Copy, scale=gx[:, gcol0 + c:gcol0 + c + 1], r=[skey, "gx"], w=[uk("W")])
                    else:
                        ts("dve", dst3[:, c, :], src3[:, c, :], gx[:, gcol0 + c:gcol0 + c + 1], None, ALU.mult, r=[skey, "gx"], w=[uk("W")])

            for hf in range(2):
                s3 = stgA.rearrange("p (c n) -> p c n", c=4)
                dma(s3, W["w_out"].rearrange("(c p) n -> p c n", p=128)[:, hf * 4:(hf + 1) * 4, :], w=["stg"])
                cast_rows(Wout[:, hf * 4:(hf + 1) * 4, :], s3, 4)
            s3 = stgA.rearrange("p (c n) -> p c n", c=8)
            dma(s3, W["w_xq"].rearrange("(c p) n -> p c n", p=128), w=["stg"])
            cast_rows(Wxq, s3, 8, 0)
            s3 = stgA.rearrange("p (c n) -> p c n", c=4)
            dma(s3, W["w_xo"].rearrange("(c p) n -> p c n", p=128), w=["stg"])
            cast_rows(Wxo, s3, 4)
            m3b = A.mark()
            Wxk = A.alloc(8 * 512, BF16).rearrange("p (c n) -> p c n", c=8)
            Wxv = A.alloc(8 * 512, BF16).rearrange("p (c n) -> p c n", c=8)
            ms = A.alloc(8 * 256).rearrange("p (c t) -> p c t", c=8)
            mb = A.alloc(8 * 256, BF16).rearrange("p (c t) -> p c t", c=8)
            msq = A.alloc(8 * 256, BF16).rearrange("p (c t) -> p c t", c=8)
            kx = A.alloc(512)
            kxsq = A.alloc(512)
            kxb = A.alloc(512, BF16)
            rm = A.alloc(2)
            kxss = A.alloc(4)
            s3 = stgA.rearrange("p (c n) -> p c n", c=8)
            dma(s3, W["w_xk"].rearrange("(c p) n -> p c n", p=128), w=["stg"])
            cast_rows(Wxk, s3, 8, 8)
            dma(s3, W["w_xv"].rearrange("(c p) n -> p c n", p=128), w=["stg"])
            cast_rows(Wxv, s3, 8, 8)
            dma(ms, memT.rearrange("(c p) t -> p c t", p=128), w=["ms"])
            cp("act", mb, ms, r=["ms"], w=["mb"])
            tt("dve", msq, ms, ms, ALU.mult, r=["ms"], w=["msq"])
            P.barrier()
            memset("pool", Vx[:, :, :, 128:130], 1.0, w=["Vxones"])
            for mt in range(2):
                for c in range(8):
                    mm(pb[7][:, 2 * mt:2 * mt + 2], msq[:, c, mt * 128:(mt + 1) * 128], ones_bf[:, 0:2], c == 0, c == 7, r=["msq", "ones"], w=["pb7"])
            cp("dve", rm, pb[7][:, 0:4:2], r=["pb7"], w=["rm"])
            rstd_from_ss(rm, 2, 1.0 / 1024, "rm")
            for mt in range(2):
                for c in range(8):
                    mm(pb[0], mb[:, c, mt * 128:(mt + 1) * 128], Wxk[:, c, :], c == 0, c == 7, r=["mb"], w=["pb0"])
                for c in range(8):
                    mm(pb[1], mb[:, c, mt * 128:(mt + 1) * 128], Wxv[:, c, :], c == 0, c == 7, r=["mb"], w=["pb1"])
                act(kx, pb[0], AF.Copy, scale=rm[:, mt:mt + 1], r=["pb0", "rm"], w=["kx"])
                act(Vx[:, mt, :, 0:128], pb[1].rearrange("p (h d) -> p h d", h=4), AF.Copy, scale=rm[:, mt:mt + 1], r=["pb1", "rm"], w=["Vx"])
                tt("dve", kxsq, kx, kx, ALU.mult, r=["kx"], w=["kxsq"])
                red(kxss, kxsq.rearrange("p (h d) -> p h d", h=4), ALU.add, r=["kxsq"], w=["kxss"])
                rstd_from_ss(kxss, 4, 1.0 / 128, "kxss")
                kx3 = kx.rearrange("p (h d) -> p h d", h=4)
                tt("dve", kx3, kx3, kxss.unsqueeze(2).to_broadcast([128, 4, 128]), ALU.mult, r=["kx", "kxss"], w=["kx"])
                tt("dve", kxb.rearrange("p (h d) -> p h d", h=4), kx3, gxk.unsqueeze(1).to_broadcast([128, 4, 128]), ALU.mult, r=["kx", "gxk"], w=["kxb"])
                for h in range(4):
                    tp(pbh[6][:, h * 128:(h + 1) * 128], kxb[:, h * 128:(h + 1) * 128], r=["kxb"], w=["pb6"])
                cp("act", KxT[:, :, mt * 128:(mt + 1) * 128], pbh[6][:, 0:512].rearrange("p (h m) -> p h m", h=4), r=["pb6"], w=["KxT"])
            P.barrier()
            A.release(m3)
            Wff1 = A.alloc(8 * 4096, BF16).rearrange("p (c n) -> p c n", c=8)
            Wff2 = A.alloc(32 * 1024, BF16).rearrange("p (c n) -> p c n", c=32)
            m3 = A.mark()
            stgs = [A.alloc(4096), A.alloc(4096)]
            for c in range(8):
                sk = c % 2
                s3 = stgs[sk].rearrange("p (c n) -> p c n", c=1)
                dma(s3, W["w_ff_in"].rearrange("(c p) n -> p c n", p=128)[:, c:c + 1, :], w=[f"stg{sk}"])
                for hq_ in range(2):
                    castn[0] += 1
                    cs_ = slice(hq_ * 2048, (hq_ + 1) * 2048)
                    if castn[0] % 2 == 0:
                        act(Wff1[:, c, cs_], s3[:, 0, cs_], AF.Copy, scale=gx[:, 16 + c:17 + c], r=[f"stg{sk}", "gx"], w=[uk("W")])
                    else:
                        ts("dve", Wff1[:, c, cs_], s3[:, 0, cs_], gx[:, 16 + c:17 + c], None, ALU.mult, r=[f"stg{sk}", "gx"], w=[uk("W")])
            for c8 in range(8):
                sk = c8 % 2
                s3 = stgs[sk].rearrange("p (c n) -> p c n", c=4)
                dma(s3, W["w_ff_out"].rearrange("(c p) n -> p c n", p=128)[:, c8 * 4:(c8 + 1) * 4, :], w=[f"stg{sk}"])
                cast_rows(Wff2[:, c8 * 4:(c8 + 1) * 4, :], s3, 4, skey=f"stg{sk}")
            P.barrier()
            A.release(m3)
            OTs = A.alloc(1024, BF16).rearrange("p (c t) -> p c t", c=8)
            x1s = [A.alloc(1024) for _ in range(2)]
            xbX = A.alloc(1024, BF16)
            hTX = A.alloc(1024, BF16).rearrange("p (c t) -> p c t", c=8)
            xbM = A.alloc(1024, BF16)
            hTM = A.alloc(1024, BF16).rearrange("p (c t) -> p c t", c=8)
            xblk = A.alloc(1024)
            qx = xblk[:, 0:512]
            qxb = xblk[:, 512:768].bitcast(BF16)
            qxT = xblk[:, 768:1024].bitcast(BF16).rearrange("p (h t) -> p h t", h=4)
            ex = A.alloc(512, BF16)
            oxb = A.alloc(512, BF16)
            oxb_f = xbX.bitcast(F32)
            oxT = A.alloc(512, BF16).rearrange("p (h t) -> p h t", h=4)
            uT = A.alloc(32 * 128, BF16).rearrange("p (f t) -> p f t", f=32)
            rlb = A.alloc(1024)
            rl = [rlb[:, 0:512], rlb[:, 512:1024]]
            r2 = [A.alloc(1) for _ in range(2)]
            qxss = A.alloc(4)
            rs4 = A.alloc(4)

            def norm_T(src, key_src, junk, junk_keys, xb_, kxb, hT_, khT, r2_, kr2):
                act(junk, src, AF.Square, accum_out=r2_, r=[key_src], w=list(junk_keys) + [kr2])
                rstd_from_ss(r2_, 1, 1.0 / 1024, kr2)
                ts("dve", xb_, src, r2_, None, ALU.mult, r=[key_src, kr2], w=[kxb])
                for a in range(8):
                    tp(pbh[6][:, a * 128:(a + 1) * 128], xb_[:, a * 128:(a + 1) * 128], r=[kxb], w=["pb6"])
                cp("act", hT_, pbh[6].rearrange("p (c t) -> p c t", c=8), r=["pb6"], w=[khT])

            def genX(i, slot):
                x1 = x1s[slot]
                kx = f"x1_{slot}"
                tsl = slice(i * 128, (i + 1) * 128)
                dma(OTs, scrO[i].rearrange("p (c t) -> p c t", c=8), w=["OTs"])
                dma(x1, xq[tsl, :], w=[kx])
                for hfo in range(2):
                    for c in range(8):
                        mm(pb[4 + hfo], OTs[:, c, :], Wout[:, c, hfo * 512:(hfo + 1) * 512], c == 0, c == 7, r=["OTs"], w=[f"pb{4 + hfo}"])
                    tt("dve", x1[:, hfo * 512:(hfo + 1) * 512], pb[4 + hfo], x1[:, hfo * 512:(hfo + 1) * 512], ALU.add, r=[f"pb{4 + hfo}", kx], w=[kx])
                yield
                if "x1" in dbg_out and i == dbg_i:
                    P.barrier()
                    dump("x1", x1)
                    P.barrier()
                norm_T(x1, kx, xblk, ["qx", "qxb", "qxT"], xbX, "xbX", hTX, "hTX", r2[0], "r2X")
                yield
                for c in range(8):
                    mm(pb[6], hTX[:, c, :], Wxq[:, c, :], c == 0, c == 7, r=["hTX"], w=["pb6"])
                cp("act", qx, pb[6], r=["pb6"], w=["qx"])
                yield
                tt("dve", oxb_f, qx, qx, ALU.mult, r=["qx"], w=["xbX"])
                red(qxss, oxb_f.rearrange("p (h d) -> p h d", h=4), ALU.add, r=["xbX"], w=["qxss"])
                rstd_from_ss(qxss, 4, 1.0 / 128, "qxss")
                qx3 = qx.rearrange("p (h d) -> p h d", h=4)
                tt("dve", qx3, qx3, qxss.unsqueeze(2).to_broadcast([128, 4, 128]), ALU.mult, r=["qx", "qxss"], w=["qx"])
                tt("dve", qxb.rearrange("p (h d) -> p h d", h=4), qx3, gxq.unsqueeze(1).to_broadcast([128, 4, 128]), ALU.mult, r=["qx", "gxq"], w=["qxb"])
                yield
                for h in range(4):
                    tp(pbh[6][:, h * 128:(h + 1) * 128], qxb[:, h * 128:(h + 1) * 128], r=["qxb"], w=["pb6"])
                cp("dve", qxT, pbh[6][:, 0:512].rearrange("p (h t) -> p h t", h=4), r=["pb6"], w=["qxT"])
                yield
                zero_bank(4)
                zero_bank(5)
                for mt in range(2):
                    for h in range(4):
                        mm(pb[7][:, h * 128:(h + 1) * 128], KxT[:, h, mt * 128:(mt + 1) * 128], qxT[:, h, :], True, True, r=["qxT", "KxT"], w=["pb7"])
                    act(ex, pb[7], AF.Exp, r=["pb7"], w=["ex"])
                    for h in range(4):
                        mm(pb[4 + h // 2][:, (h % 2) * 130:(h % 2) * 130 + 130], ex[:, h * 128:(h + 1) * 128], Vx[:, mt, h, 0:130], False, mt == 1,
                           r=["ex", "Vx"], w=[f"pb{4 + h // 2}"], skip_group_check=True)
                for b2 in range(2):
                    accv = pb[4 + b2][:, 0:260].rearrange("p (h c) -> p h c", c=130)
                    P.op("dve", lambda e, accv=accv: e.reciprocal(out=rs4[:, 0:2], in_=accv[:, :, 128]), r=[f"pb{4 + b2}"], w=["rs4"])
                    tt("dve", oxb[:, b2 * 256:(b2 + 1) * 256].rearrange("p (h d) -> p h d", h=2), accv[:, :, 0:128],
                       rs4[:, 0:2].unsqueeze(2).to_broadcast([128, 2, 128]), ALU.mult, r=[f"pb{4 + b2}", "rs4"], w=["oxb"])
                yield
                for h in range(4):
                    tp(pbh[6][:, h * 128:(h + 1) * 128], oxb[:, h * 128:(h + 1) * 128], r=["oxb"], w=["pb6"])
                cp("act", oxT, pbh[6][:, 0:512].rearrange("p (h t) -> p h t", h=4), r=["pb6"], w=["oxT"])
                yield
                for hfo in range(2):
                    for h in range(4):
                        mm(pb[4 + hfo], oxT[:, h, :], Wxo[:, h, hfo * 512:(hfo + 1) * 512], h == 0, h == 3, r=["oxT"], w=[f"pb{4 + hfo}"])
                    tt("dve", x1[:, hfo * 512:(hfo + 1) * 512], pb[4 + hfo], x1[:, hfo * 512:(hfo + 1) * 512], ALU.add, r=[f"pb{4 + hfo}", kx], w=[kx])
                if "x2" in dbg_out and i == dbg_i:
                    P.barrier()
                    dump("x2", x1)
                    P.barrier()
                yield

            def genM(i, slot):
                x1 = x1s[slot]
                kx = f"x1_{slot}"
                tsl = slice(i * 128, (i + 1) * 128)
                norm_T(x1, kx, rlb, ["rl0", "rl1"], xbM, "xbM", hTM, "hTM", r2[1], "r2M")
                yield
                for f4 in range(8):
                    bk = 2 + f4 % 2
                    for fi in range(4):
                        fc = f4 * 4 + fi
                        for c in range(8):
                            mm(pb[bk][:, fi * 128:(fi + 1) * 128], Wff1[:, c, fc * 128:(fc + 1) * 128], hTM[:, c, :], c == 0, c == 7, r=["hTM"], w=[f"pb{bk}"])
                    act(rl[f4 % 2], pb[bk], AF.Relu, r=[f"pb{bk}"], w=[f"rl{f4 % 2}"])
                    tt("pool" if f4 % 2 else "dve", uT[:, f4 * 4:(f4 + 1) * 4, :].rearrange("p f t -> p (f t)"), rl[f4 % 2], rl[f4 % 2], ALU.mult, r=[f"rl{f4 % 2}"], w=[f"uT{f4}"])
                    yield
                for fc in range(32):
                    for hfo in range(2):
                        mm(pb[hfo], uT[:, fc, :], Wff2[:, fc, hfo * 512:(hfo + 1) * 512], fc == 0, fc == 31, r=[f"uT{fc // 4}"], w=[f"pb{hfo}"])
                    if fc % 4 == 3:
                        yield
                for hfo in range(2):
                    tt("dve", x1[:, hfo * 512:(hfo + 1) * 512], pb[hfo], x1[:, hfo * 512:(hfo + 1) * 512], ALU.add, r=[f"pb{hfo}", kx], w=[kx])
                dma(out[tsl, :], x1, r=[kx], w=[f"out{i}"])
                yield

            def run_gen3(g_):
                for _ in g_:
                    pass

            def interleave3(ga, gb):
                la = lb = True
                while la or lb:
                    if la:
                        try:
                            next(ga)
                        except StopIteration:
                            la = False
                    if lb:
                        try:
                            next(gb)
                        except StopIteration:
                            lb = False

            ilist3 = list(qlist) if qlist is not None else list(range(nqb))
            run_gen3(genX(ilist3[0], 0))
            for k3 in range(len(ilist3)):
                if k3 + 1 < len(ilist3):
                    interleave3(genM(ilist3[k3], k3 % 2), genX(ilist3[k3 + 1], (k3 + 1) % 2))
                else:
                    run_gen3(genM(ilist3[k3], k3 % 2))
        P.barrier()
        print("arena peak (KiB/partition):", A.peak * 4 / 1024, "ops:", len(P.ops))
        P.emit()
    return nc


def host_prepare(inputs):
    f32 = np.float32
    x = np.asarray(inputs["x"], dtype=f32)
    mem = np.asarray(inputs["mem"], dtype=f32)
    wts = {}
    for k, v in inputs.items():
        if k in ("x", "mem"):
            continue
        v = np.asarray(v, dtype=f32)
        wts[k] = np.ascontiguousarray(v.reshape(v.shape[1:]))
    half = 32
    inv = (f32(10000.0) ** (-(np.arange(half, dtype=f32) / f32(half)))).astype(f32)

    def rope_tab(pos):
        ang = pos.astype(f32)[:, None] * inv[None, :]
        return np.concatenate([np.cos(ang), np.sin(ang)], axis=1).astype(f32)
    ropeK = rope_tab(np.arange(SEQ))
    ropeC = rope_tab(np.minimum(np.arange(512) * 16 + 31, 16 * 511 + 31))
    n = np.arange(512)
    j = np.arange(128)
    ovl = ((n[:, None] * 16 < j[None, :] * 64 + 64) & (n[:, None] * 16 + 32 > j[None, :] * 64)).astype(f32)
    ovl[511] = 0.0
    cn = np.tile((16 * np.arange(512) + 31).astype(f32)[None, :], (128, 1))
    pow2 = np.tile((2.0 ** -(np.arange(NIT + 1) + 1)).astype(f32)[None, :], (128, 1))
    ident = np.eye(128, dtype=f32)
    xT_b = [np.ascontiguousarray(x[b].T) for b in range(4)]
    memT_b = [np.ascontiguousarray(mem[b].T) for b in range(4)]
    maps = []
    m = np.arange(128)
    for c in range(8):
        b, p = c // 2, c % 2
        tok = (np.arange(NQB)[:, None] * 2 + p) * 128 + m[None, :]
        tok = tok.reshape(-1)
        xq = np.ascontiguousarray(x[b][tok])
        xqT = np.ascontiguousarray(xq.T)
        sp = np.arange(256)
        cmask = np.where(sp[None, :] <= (128 * p + m)[:, None], 0.0, NEG).astype(f32)
        s = np.arange(128)
        causT = np.zeros((128, 2, 128), f32)
        for kbl in range(2):
            causT[:, kbl, :] = (128 * kbl + s[None, :] <= 128 * p + m[:, None])
        winT = np.zeros((128, 6, 128), f32)
        for w in range(6):
            rel = 128 * (w - 4 - p) + s[None, :] - m[:, None]
            winT[:, w, :] = (rel <= 0) & (rel > -512)
        tq = (128 * p + m).astype(f32)[:, None].copy()
        fb = np.zeros((NQB, 128, 128), f32)
        for i in range(NQB):
            t = (2 * i + p) * 128 + m
            tb = t // 64
            forced = (j[None, :] == 0) | (j[None, :] == tb[:, None]) | (j[None, :] == tb[:, None] - 1)
            adm = j[None, :] * 64 <= t[:, None]
            fb[i] = np.where(adm, np.where(forced, 1e4, 0.0), NEG)
        d = dict(xT=xT_b[b], xqT=xqT, xq=xq, memT=memT_b[b], c_ident=ident, c_ropeK=ropeK, c_ropeQ=ropeK[tok],
                 c_ropeC=ropeC, c_cmask=cmask, c_causT=causT.reshape(128, 256), c_winT=winT.reshape(128, 768),
                 c_tq=tq, c_cn=cn, c_fbias=fb.reshape(NQB * 128, 128), c_ovl=ovl, c_pow2=pow2)
        d.update(wts)
        maps.append(d)
    return maps


def kernel(**inputs):
    maps = host_prepare(inputs)
    nc = build_program()
    res = run_bass_kernel_spmd(nc, maps, core_ids=list(range(8)))
    outp = np.zeros((4, SEQ, D_MODEL), np.float32)
    for c in range(8):
        b, p = c // 2, c % 2
        o = np.asarray(res.results[c]["out"]).reshape(NQB, 128, D_MODEL)
        for i in range(NQB):
            qb = 2 * i + p
            outp[b, qb * 128:(qb + 1) * 128] = o[i]
    return outp
```

```python
import numpy as np
from contextlib import ExitStack
import concourse.bass as bass
import concourse.mybir as mybir
from concourse.bass_utils import run_bass_kernel_spmd

F32 = mybir.dt.float32
BF16 = mybir.dt.bfloat16
ALU = mybir.AluOpType
AF = mybir.ActivationFunctionType
AX = mybir.AxisListType

SAME_ENGINE_SYNC = True
SEM_CHUNK = 16000
NSLOT = 8

D_MODEL = 1024
SEQ = 8192
NQB = 32
NKT = 64
NST = 16
NIT = 16
NEG = -1.0e30
EPS = 1e-6
IDX_SCALE = 8 ** -0.5 * 64 ** -0.5
TOPK = 256


class Prog:
    ENGS = ["pe", "act", "dve", "pool", "sp"]
    BLK = {"pe": "tensor", "act": "scalar", "dve": "vector", "pool": "gpsimd", "sp": "sync"}

    def __init__(self, nc):
        self.nc = nc
        self.ops = []
        self.lastw = {}
        self.readers = {}

    def op(self, eng, fn, r=(), w=(), dma=False):
        i = len(self.ops)
        w = list(w) + [k for k in r if k.startswith("pb") and k not in w]
        deps = set()
        for k in r:
            if k in self.lastw:
                deps.add(self.lastw[k])
        for k in w:
            if k in self.lastw:
                deps.add(self.lastw[k])
            for x in self.readers.get(k, {}).values():
                deps.add(x)
        for k in w:
            self.lastw[k] = i
            self.readers[k] = {}
        for k in r:
            if k not in w:
                d = self.readers.setdefault(k, {})
                d[("dma", i) if dma else eng] = i
        fd = set()
        for d in deps:
            o = self.ops[d]
            if (not dma) and (not o["dma"]) and o["eng"] == eng:
                if eng == "pe" or not SAME_ENGINE_SYNC:
                    continue
            fd.add(d)
        self.ops.append(dict(eng=eng, fn=fn, deps=fd, dma=dma, sig=None))
        return i

    def barrier(self):
        last = {}
        dmas = []
        for i, o in enumerate(self.ops):
            if o["fn"] is None:
                continue
            if o["dma"]:
                dmas.append(i)
            else:
                last[o["eng"]] = i
        deps = set(last.values()) | set(dmas)
        for e in self.ENGS:
            self.ops.append(dict(eng=e, fn=None, deps=set(deps), dma=False, sig=None))
        self.lastw = {}
        self.readers = {}

    def emit(self):
        nc = self.nc
        ops = self.ops
        needs = [False] * len(ops)
        for o in ops:
            for d in o["deps"]:
                needs[d] = True
        cnt = {e: 0 for e in self.ENGS}
        dcnt = {e: 0 for e in self.ENGS}
        slot_uses = {}
        for i, o in enumerate(ops):
            e = o["eng"]
            if o["fn"] is None:
                continue
            if o["dma"]:
                k = dcnt[e]
                dcnt[e] += 1
                slot = (e, k % NSLOT)
                slot_uses[slot] = slot_uses.get(slot, 0) + 1
                o["sig"] = ("d", slot, 16 * slot_uses[slot])
            elif needs[i]:
                cnt[e] += 1
                c = cnt[e] - 1
                o["sig"] = ("c", (e, c // SEM_CHUNK), c % SEM_CHUNK + 1)
        with ExitStack() as st:
            sems = {}
            for o in ops:
                if o["sig"] is not None:
                    kk = (o["sig"][0], o["sig"][1])
                    if kk not in sems:
                        sems[kk] = st.enter_context(nc.semaphore(f"s{kk[0]}_{kk[1][0]}_{kk[1][1]}"))
            block = st.enter_context(nc.Block())
            for e in self.ENGS:
                myops = [o for o in ops if o["eng"] == e]

                def body(eng, e=e, myops=myops):
                    waited = {}

                    def wait(sig):
                        kind, key, val = sig
                        kk = (kind, key)
                        if waited.get(kk, 0) >= val:
                            return
                        eng.wait_ge(sems[kk], val)
                        waited[kk] = val
                    for o in myops:
                        for d in sorted(o["deps"]):
                            wait(ops[d]["sig"])
                        if o["fn"] is None:
                            continue
                        if o["dma"] and o["sig"][2] > 16:
                            wait(("d", o["sig"][1], o["sig"][2] - 16))
                        ins = o["fn"](eng)
                        if o["sig"] is not None:
                            ins.then_inc(sems[(o["sig"][0], o["sig"][1])], 16 if o["dma"] else 1)
                    if e == "sp":
                        for slot, uses in slot_uses.items():
                            wait(("d", slot, 16 * uses))
                        for ee in self.ENGS:
                            if cnt[ee] > 0:
                                c = cnt[ee] - 1
                                wait(("c", (ee, c // SEM_CHUNK), c % SEM_CHUNK + 1))
                getattr(block, self.BLK[e])(body)


class Arena:
    def __init__(self, ap_f32, size_f32):
        self.ap = ap_f32
        self.size = size_f32
        self.off = 0
        self.peak = 0

    def alloc(self, n, dt=F32):
        nb = n * (4 if dt == F32 else 2)
        nf = (nb + 3) // 4
        assert self.off + nf <= self.size, f"SBUF arena overflow: {self.off}+{nf}>{self.size}"
        v = self.ap[:, self.off:self.off + nf]
        self.off += nf
        self.peak = max(self.peak, self.off)
        if dt != F32:
            v = v.bitcast(dt)
        return v

    def mark(self):
        return self.off

    def release(self, m):
        self.off = m


C_CQ = (0, 256)
C_CKV = (256, 384)
C_KIDX = (384, 448)
C_WIDX = (448, 456)
C_QNSA = (456, 968)
C_CMPK = (968, 1096)
C_CMPV = (1096, 1224)
C_SLCK = (1224, 1352)
C_SLCV = (1352, 1480)
C_WINK = (1480, 1608)
C_WINV = (1608, 1736)
C_GNSA = (1736, 1760)
W1Q_GROUPS = [C_CQ, C_WIDX, C_GNSA, C_QNSA]
W1K_GROUPS = [C_SLCK, C_SLCV, C_WINK, C_WINV, C_CKV, C_KIDX]
W1C_GROUPS = [C_CMPK, C_CMPV]

import os
LEVEL = float(os.environ.get('K1_LEVEL', '99'))
K1_NST = int(os.environ.get('K1_NST', '16'))


def build_program(dbg=None, nqb=NQB, phases=(0, 1, 2, 3), qlist=None, dbg_i=-1):
    dbg = dbg or {}
    nc = bass.Bass("TRN2", target_bir_lowering=False)

    def din(name, shape, dt=F32):
        return nc.dram_tensor(name, list(shape), dt, kind="ExternalInput").ap()

    xT = din("xT", [1024, SEQ])
    xqT = din("xqT", [1024, NQB * 128])
    xq = din("xq", [NQB * 128, 1024])
    memT = din("memT", [1024, 256])
    W = {}
    for nm, shp in [("norm1_g", [1024]), ("w_in", [1024, 1760]), ("dsa_cq_g", [256]), ("dsa_ckv_g", [128]),
                    ("w_dsa_uq", [256, 512]), ("w_dsa_ukv", [128, 128]), ("w_idx_q", [256, 512]),
                    ("idx_k_ln_g", [64]), ("idx_k_ln_b", [64]), ("dsa_qn_g", [64]), ("dsa_kn_g", [64]),
                    ("nsa_cmp_pe", [2, 32, 64]), ("w_nsa_cmp", [2, 32, 64, 64]), ("nsa_qn_g", [64]),
                    ("nsa_kn_g", [3, 64]), ("w_out", [1024, 1024]), ("norm2_g", [1024]), ("mem_norm_g", [1024]),
                    ("w_xq", [1024, 512]), ("w_xk", [1024, 512]), ("w_xv", [1024, 512]), ("xq_norm_g", [128]),
                    ("xk_norm_g", [128]), ("w_xo", [512, 1024]), ("norm3_g", [1024]), ("w_ff_in", [1024, 4096]),
                    ("w_ff_out", [4096, 1024])]:
        W[nm] = din(nm, shp)
    c_ident = din("c_ident", [128, 128])
    c_ropeK = din("c_ropeK", [SEQ, 64])
    c_ropeQ = din("c_ropeQ", [NQB * 128, 64])
    c_ropeC = din("c_ropeC", [512, 64])
    c_cmask = din("c_cmask", [128, 256])
    c_causT = din("c_causT", [128, 256])
    c_winT = din("c_winT", [128, 768])
    c_tq = din("c_tq", [128, 1])
    c_cn = din("c_cn", [128, 512])
    c_fbias = din("c_fbias", [NQB * 128, 128])
    c_ovl = din("c_ovl", [512, 128])
    c_pow2 = din("c_pow2", [128, NIT + 1])
    out = nc.dram_tensor("out", [NQB * 128, 1024], F32, kind="ExternalOutput").ap()
    scrO = nc.dram_tensor("scrO", [NQB, 128, 1024], BF16, kind="Internal").ap()
    scrKw = nc.dram_tensor("scrKw", [128, SEQ], BF16, kind="Internal").ap()
    scrVW = nc.dram_tensor("scrVW", [NKT, 128, 132], BF16, kind="Internal").ap()
    dbg_out = {}
    for nm, shp in dbg.items():
        dbg_out[nm] = nc.dram_tensor("dbg_" + nm, list(shp), F32, kind="ExternalOutput").ap()

    P = Prog(nc)
    with ExitStack() as st:
        ARENA_F32 = 53000
        arena_t = st.enter_context(nc.sbuf_tensor("arena", [128, ARENA_F32], F32))
        A = Arena(arena_t[:], ARENA_F32)
        pb = [st.enter_context(nc.psum_tensor(f"pb{k}", [128, 512], F32))[:] for k in range(8)]
        pbh = [p.bitcast(BF16) for p in pb]

        def dma(out_, in_, r=(), w=(), eng="sp", **kw):
            P.op(eng, lambda e: e.dma_start(out=out_, in_=in_, **kw), r=r, w=w, dma=True)

        def tt(eng, out_, in0, in1, op, r=(), w=()):
            P.op(eng, lambda e: e.tensor_tensor(out=out_, in0=in0, in1=in1, op=op), r=r, w=w)

        def ts(eng, out_, in0, s1, s2, op0, op1=None, r=(), w=(), accum=None):
            kw = {}
            if op1 is not None:
                kw["op1"] = op1
            if accum is not None:
                kw["accum_out"] = accum
            P.op(eng, lambda e: e.tensor_scalar(out=out_, in0=in0, scalar1=s1, scalar2=s2, op0=op0, **kw), r=r, w=w)

        def cp(eng, out_, in_, r=(), w=()):
            if eng == "act":
                P.op(eng, lambda e: e.copy(out=out_, in_=in_), r=r, w=w)
            else:
                P.op(eng, lambda e: e.tensor_copy(out=out_, in_=in_), r=r, w=w)

        def act(out_, in_, func, r=(), w=(), **kw):
            P.op("act", lambda e: e.activation(out=out_, in_=in_, func=func, **kw), r=r, w=w)

        def mm(out_, lhsT, rhs, start, stop, r=(), w=(), **kw):
            P.op("pe", lambda e: e.matmul(out_, lhsT=lhsT, rhs=rhs, start=start, stop=stop, **kw), r=r, w=w)

        def tp(out_, in_, r=(), w=()):
            P.op("pe", lambda e: e.transpose(out=out_, in_=in_, identity=ident_bf), r=list(r) + ["ident"], w=w)

        def memset(eng, ap, val, w=()):
            P.op(eng, lambda e: e.memset(ap, val), w=w)

        def red(out_, in_, op, r=(), w=()):
            P.op("dve", lambda e: e.tensor_reduce(out=out_, in_=in_, axis=AX.X, op=op), r=r, w=w)

        def rstd_from_ss(ss_ap, n, inv_n, key, eng_r=()):
            ts("dve", ss_ap, ss_ap, inv_n, EPS, ALU.mult, ALU.add, r=[key], w=[key])
            act(ss_ap, ss_ap, AF.Sqrt, r=[key], w=[key])
            P.op("dve", lambda e: e.reciprocal(out=ss_ap, in_=ss_ap), r=[key], w=[key])

        def dump(name, ap_sb):
            if name not in dbg_out:
                return
            dst = dbg_out[name]
            n = ap_sb.shape[1]
            CH = 256
            for ci, off in enumerate(range(0, n, CH)):
                w_ = min(CH, n - off)
                tmpb = dbg_tmp[ci % 2]
                cp("dve", tmpb[:, 0:w_], ap_sb[:, off:off + w_], r=["*"], w=[f"dbgtmp{ci % 2}"])
                dma(dst[:, off:off + w_], tmpb[:, 0:w_], r=[f"dbgtmp{ci % 2}"])

        uid = [0]

        def uk(prefix="j"):
            uid[0] += 1
            return f"{prefix}{uid[0]}"

        def zero_bank(k):
            mm(pb[k], zeros_bf[0:1, 0:128], zeros_bf[0:1, 0:512], True, False, r=["zeros"], w=[f"pb{k}"], skip_group_check=True)

        ident_f = A.alloc(128)
        ident_bf = A.alloc(128, BF16)
        ones_bf = A.alloc(128, BF16)
        zeros_bf = A.alloc(512, BF16)
        neghalf = A.alloc(32)
        dbg_tmp = [A.alloc(256) for _ in range(2)] if dbg else None
        dma(ident_f, c_ident, w=["identf"])
        cp("dve", ident_bf, ident_f, r=["identf"], w=["ident"])
        memset("pool", ones_bf, 1.0, w=["ones"])
        memset("pool", zeros_bf, 0.0, w=["zeros"])
        memset("pool", neghalf, -0.5, w=["neghalf"])

        mark_resident0 = A.mark()
        KT = A.alloc(2 * SEQ, BF16).rearrange("p (a t) -> p a t", a=2)
        VD = A.alloc(NKT * 66, BF16).rearrange("p (t c) -> p t c", c=66)
        VS = A.alloc(NKT * 132, BF16).rearrange("p (t g c) -> p t g c", g=2, c=66)
        KCT = A.alloc(512, BF16)
        VC = A.alloc(4 * 2 * 194, BF16).rearrange("p (t g c) -> p t g c", g=2, c=194)
        mark_resident = A.mark()

        m0 = A.mark()
        W1k = A.alloc(8 * 704, BF16).rearrange("p (c n) -> p c n", c=8)
        W1c = A.alloc(8 * 256, BF16).rearrange("p (c n) -> p c n", c=8)
        Wukv = A.alloc(128, BF16)
        Wc = A.alloc(2 * 32 * 64, BF16).rearrange("p (k l e) -> p k l e", k=2, l=32)
        biasKV = A.alloc(128)
        G6 = A.alloc(6 * 64).rearrange("p (h d) -> p h d", h=6)
        B1 = A.alloc(64)
        GC = A.alloc(64)
        g1 = A.alloc(8)
        gckv = A.alloc(1)
        m1 = A.mark()
        stg = A.alloc(4 * 1760).rearrange("p (c n) -> p c n", c=4)
        stg2 = stg.rearrange("p c n -> p (c n)")

        def bvec(dst, src_vec, n, key):
            dma(dst, src_vec.rearrange("(o n) -> o n", o=1).partition_broadcast(128), w=[key])

        w_in_v = W["w_in"].rearrange("(c p) n -> p c n", p=128)

        def load_w_in(groups, dstW, key, gvec, gkey):
            for hf in range(2):
                col = 0
                for (a, b) in groups:
                    dma(stg[:, :, col:col + (b - a)], w_in_v[:, hf * 4:(hf + 1) * 4, a:b], w=["stg"])
                    col += b - a
                for c4 in range(4):
                    c = hf * 4 + c4
                    if c % 2 == 0:
                        act(dstW[:, c, :], stg[:, c4, 0:col], AF.Copy, scale=gvec[:, c:c + 1], r=["stg", gkey], w=[key])
                    else:
                        ts("dve", dstW[:, c, :], stg[:, c4, 0:col], gvec[:, c:c + 1], None, ALU.mult, r=["stg", gkey], w=[key])

        dma(g1, W["norm1_g"].rearrange("(c p) -> p c", p=128), w=["g1"], allow_slow_non_contiguous=True)
        dma(gckv, W["dsa_ckv_g"].rearrange("(c p) -> p c", p=128), w=["gckv"], allow_slow_non_contiguous=True)
        load_w_in(W1K_GROUPS, W1k, "W1k", g1, "g1")
        load_w_in(W1C_GROUPS, W1c, "W1c", g1, "g1")
        P.barrier()
        s_ukv = stg2[:, 2048:2176]
        dma(s_ukv, W["w_dsa_ukv"], w=["sukv"])
        ts("dve", Wukv, s_ukv, gckv[:, 0:1], None, ALU.mult, r=["sukv", "gckv"], w=["Wukv"])
        s_wc = stg2[:, 2176:2176 + 4096].rearrange("p (k l e) -> p k l e", k=2, l=32)
        wc_src = W["w_nsa_cmp"].rearrange("k l d e -> d k l e")
        for hf in range(2):
            for kv in range(2):
                dma(s_wc[hf * 64:(hf + 1) * 64, kv], wc_src[:, kv], w=["swc"])
        cp("act", Wc, s_wc, r=["swc"], w=["Wc"])
        s_pe = stg2[:, 6400:6464]
        dma(s_pe[0:64, :], W["nsa_cmp_pe"].rearrange("k l d -> (k l) d"), w=["spe"])
        pe_bf = A.alloc(64, BF16)
        cp("dve", pe_bf[0:64, :], s_pe[0:64, :], r=["spe"], w=["pebf"])
        P.op("pe", lambda e: e.transpose(out=pbh[6][0:64, 0:64], in_=pe_bf[0:64, 0:64], identity=ident_bf[0:64, 0:64]), r=["pebf", "ident"], w=["pb6"])
        peT = A.alloc(64, BF16)
        cp("act", peT[0:64, :], pbh[6][0:64, 0:64], r=["pb6"], w=["peT"])
        peB = A.alloc(64 * 128, BF16).rearrange("p (a m) -> p a m", a=64)
        cp("dve", peB[0:64], peT[0:64, :].unsqueeze(2).to_broadcast([64, 64, 128]), r=["peT"], w=["peB"])
        for kv in range(2):
            for l in range(32):
                mm(pb[7][:, kv * 64:(kv + 1) * 64], peB[0:64, kv * 32 + l, :], Wc[0:64, kv, l, :], l == 0, l == 31,
                   r=["peB", "Wc"], w=["pb7"])
        cp("dve", biasKV, pb[7][:, 0:128], r=["pb7"], w=["biasKV"])
        bvec(G6[:, 0, :], W["dsa_kn_g"], 64, "G6")
        bvec(G6[:, 1, :], W["idx_k_ln_g"], 64, "G6")
        for hh, row in [(2, 1), (3, 1), (4, 2), (5, 2)]:
            dma(G6[:, hh, :], W["nsa_kn_g"][row:row + 1, :].partition_broadcast(128), w=["G6"])
        dma(GC, W["nsa_kn_g"][0:1, :].partition_broadcast(128), w=["GC"])
        bvec(B1, W["idx_k_ln_b"], 64, "B1")
        memset("pool", VD[:, :, 64:66], 1.0, w=["VDones"])
        memset("pool", VS[:, :, :, 64:66], 1.0, w=["VSones"])
        memset("pool", VC[:, :, :, 64:66], 1.0, w=["VCones"])
        P.barrier()
        A.release(m1)

        cmpT = A.alloc(2 * (SEQ + 32), BF16).rearrange("p (k t) -> p k t", k=2)
        memset("pool", cmpT[:, :, SEQ:SEQ + 32], 0.0, w=["cmpTpad"])
        m1b = A.mark()
        xs = [A.alloc(8 * 512).rearrange("p (c t) -> p c t", c=8)] * 2
        hT = [A.alloc(8 * 512, BF16).rearrange("p (c t) -> p c t", c=8)] * 2
        xsq = [A.alloc(8 * 512, BF16).rearrange("p (c t) -> p c t", c=8)] * 2
        rbc = A.alloc(512)
        rtok = A.alloc(4)
        pk = [A.alloc(704) for _ in range(2)]
        ropeK = [A.alloc(64) for _ in range(2)]
        KH = [A.alloc(6 * 64).rearrange("p (h d) -> p h d", h=6) for _ in range(2)]
        KHsq2 = [A.alloc(6 * 64).rearrange("p (h d) -> p h d", h=6) for _ in range(2)]
        KHsq = KHsq2[0]
        KHb = [A.alloc(6 * 64, BF16).rearrange("p (h d) -> p h d", h=6) for _ in range(2)]
        kss = A.alloc(8)
        kss2 = [A.alloc(8) for _ in range(2)]
        kmu2 = [A.alloc(1) for _ in range(2)]
        ckvss2 = [A.alloc(1) for _ in range(2)]
        ckvn2 = [A.alloc(128, BF16) for _ in range(2)]
        ckvnT2 = [A.alloc(128, BF16) for _ in range(2)]
        rt = [A.alloc(6 * 32).rearrange("p (h d) -> p h d", h=6) for _ in range(4)]
        kwst = [A.alloc(128, BF16) for _ in range(2)]
        vwst = [A.alloc(132, BF16).rearrange("p (g c) -> p g c", g=2) for _ in range(2)]
        for q_ in range(2):
            memset("pool", vwst[q_][:, :, 64:66], 1.0, w=[f"vwones{q_}"])
        xT_v = xT.rearrange("(c p) t -> p c t", p=128)

        for stI in range(min(NST, K1_NST) if 1 in phases else 0):
            b = 0
            t0 = stI * 512
            dma(xs[b], xT_v[:, :, t0:t0 + 512], w=[f"xs{b}"])
            cp("act", hT[b], xs[b], r=[f"xs{b}"], w=[f"hT{b}"])
            tt("pool", xsq[b], xs[b], xs[b], ALU.mult, r=[f"xs{b}"], w=[f"xsq{b}"])
            for c in range(8):
                mm(pb[7], ones_bf, xsq[b][:, c, :], c == 0, c == 7, r=["ones", f"xsq{b}"], w=["pb7"])
            ts("dve", rbc, pb[7], 1.0 / 1024, EPS, ALU.mult, ALU.add, r=["pb7"], w=["rbc"])
            act(rbc, rbc, AF.Sqrt, r=["rbc"], w=["rbc"])
            P.op("dve", lambda e: e.reciprocal(out=rbc, in_=rbc), r=["rbc"], w=["rbc"])
            for j in range(4):
                for c in range(8):
                    mm(pb[6][:, 2 * j:2 * j + 2], xsq[b][:, c, j * 128:(j + 1) * 128], ones_bf[:, 0:2], c == 0, c == 7,
                       r=["ones", f"xsq{b}"], w=["pb6"])
            ts("dve", rtok, pb[6][:, 0:8:2], 1.0 / 1024, EPS, ALU.mult, ALU.add, r=["pb6"], w=["rtok"])
            act(rtok, rtok, AF.Sqrt, r=["rtok"], w=["rtok"])
            P.op("dve", lambda e: e.reciprocal(out=rtok, in_=rtok), r=["rtok"], w=["rtok"])
            for kv in range(2 if LEVEL >= 2 else 0):
                for c in range(8):
                    mm(pb[4 + kv], W1c[:, c, kv * 128:(kv + 1) * 128], hT[b][:, c, :], c == 0, c == 7,
                       r=["W1c", f"hT{b}"], w=[f"pb{4 + kv}"])
                tt("dve", cmpT[:, kv, t0:t0 + 512], pb[4 + kv], rbc, ALU.mult, r=[f"pb{4 + kv}", "rbc"], w=["cmpT"])
            def ktile(stI, j, q, b=b):
                tile = stI * 4 + j
                tsl = slice(j * 128, (j + 1) * 128)
                gsl = slice(tile * 128, (tile + 1) * 128)
                ubank = 2 if q == 0 else 5
                sq_, css_, cn_, cT_, mu_, ss_ = KHsq2[q], ckvss2[q], ckvn2[q], ckvnT2[q], kmu2[q], kss2[q]
                ta, tb = rt[2 * q], rt[2 * q + 1]
                dma(ropeK[q], c_ropeK[gsl, :], w=[f"ropeK{q}"])
                for c in range(8):
                    mm(pb[0], hT[b][:, c, tsl], W1k[:, c, 0:512], c == 0, c == 7, r=["W1k", f"hT{b}"], w=["pb0"])
                for c in range(8):
                    mm(pb[1][:, 0:192], hT[b][:, c, tsl], W1k[:, c, 512:704], c == 0, c == 7, r=["W1k", f"hT{b}"], w=["pb1"])
                act(pk[q][:, 0:512], pb[0], AF.Copy, scale=rtok[:, j:j + 1], r=["pb0", "rtok"], w=[f"pk{q}"])
                act(pk[q][:, 512:704], pb[1][:, 0:192], AF.Copy, scale=rtok[:, j:j + 1], r=["pb1", "rtok"], w=[f"pk{q}"])
                yield
                act(sq_.rearrange("p h d -> p (h d)")[:, 0:128], pk[q][:, 512:640], AF.Square, accum_out=css_, r=[f"pk{q}"], w=[f"KHsq{q}", f"ckvss{q}"])
                rstd_from_ss(css_, 1, 1.0 / 128, f"ckvss{q}")
                ts("dve", cn_, pk[q][:, 512:640], css_[:, 0:1], None, ALU.mult, r=[f"pk{q}", f"ckvss{q}"], w=[f"ckvn{q}"])
                yield
                tp(pbh[6][:, 128:256], cn_, r=[f"ckvn{q}"], w=["pb6"])
                cp("act", cT_, pbh[6][:, 128:256], r=["pb6"], w=[f"ckvnT{q}"])
                yield
                mm(pb[ubank][:, 0:128], cT_, Wukv, True, True, r=[f"ckvnT{q}", "Wukv"], w=[f"pb{ubank}"])
                kh = KH[q]
                cp("dve", kh[:, 0, :], pb[ubank][:, 0:64], r=[f"pb{ubank}"], w=[f"KH{q}"])
                cp("act", VD[:, tile, 0:64], pb[ubank][:, 64:128], r=[f"pb{ubank}"], w=[f"VD{tile}"])
                yield
                cp("act", kh[:, 2:4, :], pk[q][:, 0:128].rearrange("p (h d) -> p h d", h=2), r=[f"pk{q}"], w=[f"KH{q}"])
                cp("act", kh[:, 4:6, :], pk[q][:, 256:384].rearrange("p (h d) -> p h d", h=2), r=[f"pk{q}"], w=[f"KH{q}"])
                cp("act", VS[:, tile, :, 0:64], pk[q][:, 128:256].rearrange("p (g d) -> p g d", g=2), r=[f"pk{q}"], w=[f"VS{tile}"])
                cp("act", vwst[q][:, :, 0:64], pk[q][:, 384:512].rearrange("p (g d) -> p g d", g=2), r=[f"pk{q}", f"vwones{q}"], w=[f"vwst{q}"])
                dma(scrVW[tile], vwst[q].rearrange("p g c -> p (g c)"), r=[f"vwst{q}"], w=[f"scrVW{tile}"])
                red(mu_, pk[q][:, 640:704], ALU.add, r=[f"pk{q}"], w=[f"kmu{q}"])
                ts("dve", mu_, mu_, -1.0 / 64, None, ALU.mult, r=[f"kmu{q}"], w=[f"kmu{q}"])
                ts("dve", kh[:, 1, :], pk[q][:, 640:704], mu_[:, 0:1], None, ALU.add, r=[f"kmu{q}", f"pk{q}"], w=[f"KH{q}"])
                yield
                tt("dve", sq_, kh, kh, ALU.mult, r=[f"KH{q}"], w=[f"KHsq{q}"])
                red(ss_[:, 0:6], sq_, ALU.add, r=[f"KHsq{q}"], w=[f"kss{q}"])
                rstd_from_ss(ss_[:, 0:6], 6, 1.0 / 64, f"kss{q}")
                yield
                tt("dve", kh, kh, ss_[:, 0:6].unsqueeze(2).to_broadcast([128, 6, 64]), ALU.mult, r=[f"KH{q}", f"kss{q}"], w=[f"KH{q}"])
                tt("dve", kh, kh, G6, ALU.mult, r=[f"KH{q}", "G6"], w=[f"KH{q}"])
                tt("dve", kh[:, 1, :], kh[:, 1, :], B1, ALU.add, r=[f"KH{q}", "B1"], w=[f"KH{q}"])
                yield
                cosb = ropeK[q][:, 0:32].unsqueeze(1).to_broadcast([128, 6, 32])
                sinb = ropeK[q][:, 32:64].unsqueeze(1).to_broadcast([128, 6, 32])
                x1 = kh[:, :, 0:32]
                x2 = kh[:, :, 32:64]
                tt("dve", ta, x1, cosb, ALU.mult, r=[f"KH{q}", f"ropeK{q}"], w=[f"rta{q}"])
                tt("dve", tb, x2, sinb, ALU.mult, r=[f"KH{q}", f"ropeK{q}"], w=[f"rtb{q}"])
                tt("dve", KHb[q][:, :, 0:32], ta, tb, ALU.subtract, r=[f"rta{q}", f"rtb{q}"], w=[f"KHb{q}"])
                yield
                tt("dve", ta, x1, sinb, ALU.mult, r=[f"KH{q}", f"ropeK{q}"], w=[f"rta{q}"])
                tt("dve", tb, x2, cosb, ALU.mult, r=[f"KH{q}", f"ropeK{q}"], w=[f"rtb{q}"])
                tt("dve", KHb[q][:, :, 32:64], ta, tb, ALU.add, r=[f"rta{q}", f"rtb{q}"], w=[f"KHb{q}"])
                yield
                khb2 = KHb[q].rearrange("p (a two) d -> p a (two d)", two=2)
                for a3 in range(3):
                    tp(pbh[3][:, a3 * 128:(a3 + 1) * 128], khb2[:, a3, :], r=[f"KHb{q}"], w=["pb3"])
                cp("act", KT[:, :, gsl], pbh[3][:, 0:256].rearrange("p (a t) -> p a t", a=2), r=["pb3"], w=[f"KT{tile}"])
                cp("dve", kwst[q], pbh[3][:, 256:384], r=["pb3"], w=[f"kwst{q}"])
                dma(scrKw[:, gsl], kwst[q], r=[f"kwst{q}"], w=[f"scrKw{tile}"])
                yield

            if LEVEL >= 3:
                for jp in range(2):
                    ga_, gb_ = ktile(stI, 2 * jp, 0), ktile(stI, 2 * jp + 1, 1)
                    la_ = lb_ = True
                    while la_ or lb_:
                        if la_:
                            try:
                                next(ga_)
                            except StopIteration:
                                la_ = False
                        if lb_:
                            try:
                                next(gb_)
                            except StopIteration:
                                lb_ = False

        if 1 in phases and LEVEL >= 7:
            P.barrier()
            A.release(m1b)
            rt = [A.alloc(6 * 32).rearrange("p (h d) -> p h d", h=6) for _ in range(4)]
            kss = A.alloc(8)
            ropeC = A.alloc(4 * 64).rearrange("p (t d) -> p t d", t=4)
            dma(ropeC, c_ropeC.rearrange("(t p) d -> p t d", p=128), w=["ropeC"])
            ovl_f = A.alloc(4 * 128).rearrange("p (t j) -> p t j", t=4)
            dma(ovl_f, c_ovl.rearrange("(t p) j -> p t j", p=128), w=["ovlf"])
            for g in range(2):
                cp("dve", VC[:, :, g, 66:194], ovl_f, r=["ovlf"], w=["VC"])
            kc = A.alloc(2 * 64).rearrange("p (g d) -> p g d", g=2)
            kcsq = A.alloc(2 * 64).rearrange("p (g d) -> p g d", g=2)
            kcb = A.alloc(2 * 64, BF16)
            for nb in range(4):
                for kv in range(2):
                    for g in range(2):
                        bank = pb[0] if g == 0 else pb[2]
                        for l in range(32):
                            s0 = nb * 2048 + l
                            mm(bank[:, kv * 64:(kv + 1) * 64], cmpT[g * 64:(g + 1) * 64, kv, s0:s0 + 16 * 127 + 1:16],
                               Wc[g * 64:(g + 1) * 64, kv, l, :], l == 0, l == 31, r=["cmpT", "cmpTpad", "Wc"], w=["pb0" if g == 0 else "pb2"])
                bk = biasKV.rearrange("p (k e) -> p k e", k=2)
                for g in range(2):
                    bank = pb[0] if g == 0 else pb[2]
                    bkey = "pb0" if g == 0 else "pb2"
                    tt("dve", kc[:, g, :], bank[:, 0:64], bk[:, 0, :], ALU.add, r=[bkey, "biasKV"], w=["kc"])
                    tt("dve", VC[:, nb, g, 0:64], bank[:, 64:128], bk[:, 1, :], ALU.add, r=[bkey, "biasKV"], w=["VC"])
                tt("dve", kcsq, kc, kc, ALU.mult, r=["kc"], w=["kcsq"])
                red(kss[:, 6:8], kcsq, ALU.add, r=["kcsq"], w=["kss2"])
                ts("dve", kss[:, 6:8], kss[:, 6:8], 1.0 / 64, EPS, ALU.mult, ALU.add, r=["kss2"], w=["kss2"])
                act(kss[:, 6:8], kss[:, 6:8], AF.Sqrt, r=["kss2"], w=["kss2"])
                P.op("dve", lambda e: e.reciprocal(out=kss[:, 6:8], in_=kss[:, 6:8]), r=["kss2"], w=["kss2"])
                tt("dve", kc, kc, kss[:, 6:8].unsqueeze(2).to_broadcast([128, 2, 64]), ALU.mult, r=["kc", "kss2"], w=["kc"])
                tt("dve", kc, kc, GC.unsqueeze(1).to_broadcast([128, 2, 64]), ALU.mult, r=["kc", "GC"], w=["kc"])
                cosb = ropeC[:, nb, 0:32].unsqueeze(1).to_broadcast([128, 2, 32])
                sinb = ropeC[:, nb, 32:64].unsqueeze(1).to_broadcast([128, 2, 32])
                kcb3 = kcb.rearrange("p (g d) -> p g d", g=2)
                tt("dve", rt[0][:, 0:2, :], kc[:, :, 0:32], cosb, ALU.mult, r=["kc", "ropeC"], w=["rt0"])
                tt("dve", rt[1][:, 0:2, :], kc[:, :, 32:64], sinb, ALU.mult, r=["kc", "ropeC"], w=["rt1"])
                tt("dve", rt[2][:, 0:2, :], kc[:, :, 0:32], sinb, ALU.mult, r=["kc", "ropeC"], w=["rt2"])
                tt("dve", rt[3][:, 0:2, :], kc[:, :, 32:64], cosb, ALU.mult, r=["kc", "ropeC"], w=["rt3"])
                tt("dve", kcb3[:, :, 0:32], rt[0][:, 0:2, :], rt[1][:, 0:2, :], ALU.subtract, r=["rt0", "rt1"], w=["kcb"])
                tt("dve", kcb3[:, :, 32:64], rt[2][:, 0:2, :], rt[3][:, 0:2, :], ALU.add, r=["rt2", "rt3"], w=["kcb"])
                tp(pbh[6][:, 0:128], kcb, r=["kcb"], w=["pb6"])
                cp("act", KCT[:, nb * 128:(nb + 1) * 128], pbh[6][:, 0:128], r=["pb6"], w=["KCT"])
            P.barrier()
            dump("KT", KT.rearrange("p a t -> p (a t)"))
            dump("VD", VD.rearrange("p t c -> p (t c)"))
            dump("VS", VS.rearrange("p t g c -> p (t g c)"))
            dump("KCT", KCT)
            dump("VC", VC.rearrange("p t g c -> p (t g c)"))
        P.barrier()
        A.release(m0)

        if 2 in phases:
            W1q = A.alloc(8 * 800, BF16).rearrange("p (c n) -> p c n", c=8)
            Wq2 = A.alloc(2 * 1024, BF16).rearrange("p (c n) -> p c n", c=2)
            GQ = A.alloc(24 * 64).rearrange("p (h d) -> p h d", h=24)
            g1 = A.alloc(8)
            gcq = A.alloc(2)
            cmask = A.alloc(256)
            causT = A.alloc(256, BF16).rearrange("p (k q) -> p k q", k=2)
            winT = A.alloc(768, BF16).rearrange("p (k q) -> p k q", k=6)
            tq = A.alloc(1)
            cn = A.alloc(512)
            pow2 = A.alloc(NIT + 1)
            m2 = A.mark()
            stg = A.alloc(4 * 1760).rearrange("p (c n) -> p c n", c=4)
            stg2 = stg.rearrange("p c n -> p (c n)")
            dma(g1, W["norm1_g"].rearrange("(c p) -> p c", p=128), w=["g1"], allow_slow_non_contiguous=True)
            dma(gcq, W["dsa_cq_g"].rearrange("(c p) -> p c", p=128), w=["gcq"], allow_slow_non_contiguous=True)
            load_w_in(W1Q_GROUPS, W1q, "W1q", g1, "g1")
            P.barrier()
            s_q2 = stg2[:, 0:2048].rearrange("p (c n) -> p c n", c=2)
            dma(s_q2[:, :, 0:512], W["w_dsa_uq"].rearrange("(c p) n -> p c n", p=128), w=["sq2"])
            dma(s_q2[:, :, 512:1024], W["w_idx_q"].rearrange("(c p) n -> p c n", p=128), w=["sq2"])
            for c in range(2):
                ts("dve", Wq2[:, c, :], s_q2[:, c, :], gcq[:, c:c + 1], None, ALU.mult, r=["sq2", "gcq"], w=["Wq2"])
            s_c = stg2[:, 2048:2048 + 1024]
            dma(s_c[:, 0:256], c_causT, w=["sc"])
            dma(s_c[:, 256:1024], c_winT, w=["sc"])
            ts("dve", causT.rearrange("p k q -> p (k q)"), s_c[:, 0:256], -1.0, 30000.0, ALU.add, ALU.mult, r=["sc"], w=["causT"])
            ts("dve", winT.rearrange("p k q -> p (k q)"), s_c[:, 256:1024], -1.0, 30000.0, ALU.add, ALU.mult, r=["sc"], w=["winT"])
            dma(cmask, c_cmask, w=["cmask"])
            dma(tq, c_tq, w=["tq"])
            dma(cn, c_cn, w=["cn"])
            dma(pow2, c_pow2, w=["pow2"])
            for pr in range(8):
                bvec(GQ[:, 2 * pr, :], W["dsa_qn_g"], 64, "GQ")
                memset("pool", GQ[:, 2 * pr + 1, :], 1.0, w=["GQ"])
            for hh in range(16, 24):
                bvec(GQ[:, hh, :], W["nsa_qn_g"], 64, "GQ")
            GQp = GQ.rearrange("p (a two) d -> p a two d", two=2)
            ts("dve", GQp[:, 0:8, 0, :], GQp[:, 0:8, 0, :], 0.125, None, ALU.mult, r=["GQ"], w=["GQ"])
            ts("dve", GQ[:, 16:24, :], GQ[:, 16:24, :], 0.125, None, ALU.mult, r=["GQ"], w=["GQ"])
            P.barrier()
            A.release(m2)

            score = A.alloc(SEQ)
            MR = A.alloc(SEQ, BF16)
            xq_flat = A.alloc(8 * 128)
            xq_s = xq_flat.rearrange("p (c t) -> p c t", c=8)
            mkj = xq_flat.bitcast(BF16)
            hq = A.alloc(8 * 128, BF16).rearrange("p (c t) -> p c t", c=8)
            xsqq = A.alloc(8 * 128, BF16).rearrange("p (c t) -> p c t", c=8)
            pq = A.alloc(800)
            cqn = A.alloc(256, BF16)
            cqnT = A.alloc(256, BF16).rearrange("p (c t) -> p c t", c=2)
            QH = A.alloc(24 * 64).rearrange("p (h d) -> p h d", h=24)
            QHsq = A.alloc(24 * 64).rearrange("p (h d) -> p h d", h=24)
            QHb = A.alloc(24 * 64, BF16).rearrange("p (h d) -> p h d", h=24)
            QTd2 = [A.alloc(8 * 128, BF16).rearrange("p (a t) -> p a t", a=8) for _ in range(2)]
            QTi = A.alloc(8 * 128, BF16).rearrange("p (a t) -> p a t", a=8)
            QTn2 = [[A.alloc(4 * 128, BF16).rearrange("p (a t) -> p a t", a=4) for _ in range(2)] for _ in range(2)]
            for par_ in range(2):
                memset("pool", QTd2[par_][64:128], 0.0, w=[f"QTd{par_}"])
                memset("pool", QTn2[par_][0][64:128], 0.0, w=[f"QTn{par_}"])
                memset("pool", QTn2[par_][1][0:64], 0.0, w=[f"QTn{par_}"])
            memset("pool", QTi[0:64], 0.0, w=["QTi"])
            ropeQ = A.alloc(64)
            rbuf = [A.alloc(512, BF16) for _ in range(2)]
            pbuf = [A.alloc(512, BF16) for _ in range(4)]
            mexp = [[A.alloc(512, BF16) for _ in range(2)] for _ in range(2)]
            mcT = A.alloc(4 * 128, BF16).rearrange("p (k q) -> p k q", k=4)
            bm = [A.alloc(128, BF16) for _ in range(2)]
            bs = [A.alloc(128) for _ in range(2)]
            bs2 = A.alloc(128)
            blk = [A.alloc(128) for _ in range(2)]
            fb = A.alloc(128)
            Onsa = A.alloc(8 * 64).rearrange("p (h d) -> p h d", h=8)
            tmpO = A.alloc(4 * 64).rearrange("p (h d) -> p h d", h=4)
            Ob = A.alloc(1024, BF16)
            OT = A.alloc(1024, BF16)
            kw_t = A.alloc(768, BF16)
            vw_t = A.alloc(6 * 132, BF16).rearrange("p (t g c) -> p t g c", t=6, g=2)
            sm = A.alloc(64)
            rq, cqss, wabs, wsgn, gat, qss = sm[:, 0:1], sm[:, 1:2], sm[:, 2:10], sm[:, 10:18], sm[:, 18:42], A.alloc(24)
            lo, hi, d0, mid, t1, cnttot, thr16 = (A.alloc(1) for _ in range(7))
            cnt = A.alloc(4)
            cntA = A.alloc(1)
            junkA = QH.rearrange("p h d -> p (h d)").bitcast(BF16)
            Hh = A.alloc(NIT + 1)
            Hh2 = A.alloc(NIT + 1)
            m8 = A.alloc(16)
            rsum = A.alloc(8)
            cco = A.alloc(8)
            xqT_v = xqT.rearrange("(c p) t -> p c t", p=128)
            print("phase2 arena used (KiB):", A.off * 4 / 1024)
            def rope(src, dstb, nh, ropet, keys_r, key_w):
                cosb = ropet[:, 0:32].unsqueeze(1).to_broadcast([128, nh, 32])
                sinb = ropet[:, 32:64].unsqueeze(1).to_broadcast([128, nh, 32])
                ta = QHsq[:, 0:nh, 0:32]
                tb = QHsq[:, 0:nh, 32:64]
                x1 = src[:, :, 0:32]
                x2 = src[:, :, 32:64]
                tt("dve", ta, x1, cosb, ALU.mult, r=keys_r, w=["QHsqa"])
                tt("pool", tb, x2, sinb, ALU.mult, r=keys_r, w=["QHsqb"])
                tt("dve", dstb[:, :, 0:32], ta, tb, ALU.subtract, r=["QHsqa", "QHsqb"], w=[key_w])
                tt("dve", ta, x1, sinb, ALU.mult, r=keys_r, w=["QHsqa"])
                tt("pool", tb, x2, cosb, ALU.mult, r=keys_r, w=["QHsqb"])
                tt("dve", dstb[:, :, 32:64], ta, tb, ALU.add, r=["QHsqa", "QHsqb"], w=[key_w])

            def fold_branch(banks, br, first):
                for (bk, h0, nh, stride) in banks:
                    accv = pb[bk][:, 0:nh * stride].rearrange("p (h c) -> p h c", c=stride)
                    ts("dve", rsum[:, 0:nh], accv[:, :, 64], 1e-30, None, ALU.max, r=[f"pb{bk}"], w=["rsum"])
                    P.op("dve", lambda e, nh=nh: e.reciprocal(out=rsum[:, 0:nh], in_=rsum[:, 0:nh]), r=["rsum"], w=["rsum"])
                    gv = gat.rearrange("p (h b) -> p h b", b=3)[:, h0:h0 + nh, br]
                    tt("dve", cco[:, 0:nh], rsum[:, 0:nh], gv, ALU.mult, r=["rsum", "gat"], w=["cco"])
                    cb = cco[:, 0:nh].unsqueeze(2).to_broadcast([128, nh, 64])
                    if first:
                        tt("dve", Onsa[:, h0:h0 + nh, :], accv[:, :, 0:64], cb, ALU.mult, r=[f"pb{bk}", "cco"], w=["Onsa"])
                    else:
                        tt("dve", tmpO[:, 0:nh, :], accv[:, :, 0:64], cb, ALU.mult, r=[f"pb{bk}", "cco"], w=["tmpO"])
                        tt("dve", Onsa[:, h0:h0 + nh, :], Onsa[:, h0:h0 + nh, :], tmpO[:, 0:nh, :], ALU.add, r=["tmpO", "Onsa"], w=["Onsa"])

            MNEG = -30000.0
            NPB = 4
            PVLAG = 2
            pctr = [0]

            def attend(kb_list, g_list, kplane_fn, v_fn, mask_fn, qsel, acc_fn, kkeys=(), vkeys=(), bank_fn=None):
                units = [(kb, g) for kb in kb_list for g in g_list]
                pend = []
                for ui, (kb, g) in enumerate(units):
                    bank = bank_fn(ui, g) if bank_fn else 2 * g + (ui // len(g_list)) % 2
                    sb = pb[bank]
                    kT = kplane_fn(kb, g)
                    mk_ap, mk_keys = mask_fn(kb, g)
                    mm(sb, kT, qsel(g), True, False, r=list(kkeys), w=[f"pb{bank}"], skip_group_check=True)
                    mm(sb, mk_ap, ident_bf.unsqueeze(1).to_broadcast([128, 4, 128]), False, True, r=["ident"] + mk_keys, w=[f"pb{bank}"], skip_group_check=True)
                    pi = pctr[0] % NPB
                    pctr[0] += 1
                    pbf = pbuf[pi]
                    act(pbf, sb, AF.Exp, r=[f"pb{bank}"], w=[f"pbuf{pi}"])
                    if len(pend) >= PVLAG:
                        pend.pop(0)()
                    def do_pv(kb=kb, g=g, pbf=pbf, pi=pi, ui=ui):
                        vv = v_fn(kb, g)
                        for r_ in range(4):
                            o_ap, okey = acc_fn(g, r_)
                            mm(o_ap, pbf[:, r_ * 128:(r_ + 1) * 128], vv, False, ui == len(units) - 1, r=[f"pbuf{pi}"] + list(vkeys), w=[okey], skip_group_check=True)
                    pend.append(do_pv)
                    yield
                while pend:
                    pend.pop(0)()
                yield

            def make_stages(i):
                    L = 256 * (i + 1)
                    nkb = L // 128
                    nck = (L + 511) // 512
                    tsl = slice(i * 128, (i + 1) * 128)
                    par = i % 2
                    QTd = QTd2[par]
                    QTn = QTn2[par]
                    kQd = f"QTd{par}"
                    kQn = f"QTn{par}"

                    def stageA():
                        dma(xq_s, xqT_v[:, :, tsl], w=["xq_s"])
                        dma(ropeQ, c_ropeQ[tsl, :], w=["ropeQ"])
                        dma(fb, c_fbias[tsl, :], w=["fb"])
                        cp("act", hq, xq_s, r=["xq_s"], w=["hq"])
                        tt("pool", xsqq, xq_s, xq_s, ALU.mult, r=["xq_s"], w=["xsqq"])
                        for c in range(8):
                            mm(pb[7][:, 0:2], xsqq[:, c, :], ones_bf[:, 0:2], c == 0, c == 7, r=["ones", "xsqq"], w=["pb7"])
                        cp("dve", rq, pb[7][:, 0:1], r=["pb7"], w=["rq"])
                        rstd_from_ss(rq, 1, 1.0 / 1024, "rq")
                        for c in range(8):
                            mm(pb[6], hq[:, c, :], W1q[:, c, 288:800], c == 0, c == 7, r=["hq", "W1q"], w=["pb6"])
                        for c in range(8):
                            mm(pb[7][:, 0:288], hq[:, c, :], W1q[:, c, 0:288], c == 0, c == 7, r=["hq", "W1q"], w=["pb7"])
                        act(pq[:, 288:800], pb[6], AF.Copy, scale=rq, r=["pb6", "rq"], w=["pqn"])
                        act(pq[:, 0:288], pb[7][:, 0:288], AF.Copy, scale=rq, r=["pb7", "rq"], w=["pqa"])
                        yield
                        act(QHsq.rearrange("p h d -> p (h d)")[:, 0:256], pq[:, 0:256], AF.Square, accum_out=cqss, r=["pqa"], w=["QHsqa", "QHsqb", "cqss"])
                        rstd_from_ss(cqss, 1, 1.0 / 256, "cqss")
                        ts("dve", cqn, pq[:, 0:256], cqss, None, ALU.mult, r=["pqa", "cqss"], w=["cqn"])
                        for c2 in range(2):
                            tp(pbh[6][:, c2 * 128:(c2 + 1) * 128], cqn[:, c2 * 128:(c2 + 1) * 128], r=["cqn"], w=["pb6"])
                        cp("act", cqnT, pbh[6][:, 0:256].rearrange("p (c t) -> p c t", c=2), r=["pb6"], w=["cqnT"])
                        for c2 in range(2):
                            mm(pb[6], cqnT[:, c2, :], Wq2[:, c2, 0:512], c2 == 0, c2 == 1, r=["cqnT", "Wq2"], w=["pb6"])
                        for c2 in range(2):
                            mm(pb[7], cqnT[:, c2, :], Wq2[:, c2, 512:1024], c2 == 0, c2 == 1, r=["cqnT", "Wq2"], w=["pb7"])
                        QHp = QH.rearrange("p (a two) d -> p a two d", two=2)
                        cp("dve", QHp[:, 0:8, 0, :], pb[6].rearrange("p (h d) -> p h d", h=8), r=["pb6"], w=["QH"])
                        cp("act", QHp[:, 0:8, 1, :], pb[7].rearrange("p (h d) -> p h d", h=8), r=["pb7"], w=["QH"])
                        for g in range(2):
                            cp("dve", QHp[:, 8:12, g, :], pq[:, 288 + g * 256:288 + (g + 1) * 256].rearrange("p (r d) -> p r d", r=4), r=["pqn"], w=["QH"])
                        yield
                        act(wabs, pq[:, 256:264], AF.Abs, scale=IDX_SCALE, r=["pqa"], w=["wabs"])
                        act(wsgn, pq[:, 256:264], AF.Sign, r=["pqa"], w=["wsgn"])
                        act(gat, pq[:, 264:288], AF.Exp, scale=-1.0, r=["pqa"], w=["gat"])
                        ts("dve", gat, gat, 1.0, None, ALU.add, r=["gat"], w=["gat"])
                        P.op("dve", lambda e: e.reciprocal(out=gat, in_=gat), r=["gat"], w=["gat"])
                        yield
                        tt("dve", QHsq, QH, QH, ALU.mult, r=["QH"], w=["QHsqa", "QHsqb"])
                        red(qss, QHsq, ALU.add, r=["QHsqa", "QHsqb"], w=["qss"])
                        rstd_from_ss(qss, 24, 1.0 / 64, "qss")
                        memset("dve", qss.rearrange("p (a two) -> p a two", two=2)[:, 0:8, 1], 1.0, w=["qss"])
                        tt("dve", QH, QH, qss.unsqueeze(2).to_broadcast([128, 24, 64]), ALU.mult, r=["QH", "qss"], w=["QH"])
                        tt("dve", QH, QH, GQ, ALU.mult, r=["QH", "GQ"], w=["QH"])
                        rope(QH, QHb, 24, ropeQ, ["QH", "ropeQ"], "QHb")
                        qhb2 = QHb.rearrange("p (a two) d -> p a (two d)", two=2)
                        for a in range(8):
                            tp(pbh[6][:, a * 128:(a + 1) * 128], qhb2[:, a, :], r=["QHb"], w=["pb6"])
                        for a in range(4):
                            tp(pbh[7][:, a * 128:(a + 1) * 128], qhb2[:, 8 + a, :], r=["QHb"], w=["pb7"])
                        cp("act", QTd[0:64], pbh[6][0:64].rearrange("p (a t) -> p a t", a=8), r=["pb6"], w=[kQd])
                        cp("dve", QTi[64:128], pbh[6][64:128].rearrange("p (a t) -> p a t", a=8), r=["pb6"], w=["QTi"])
                        cp("act", QTn[0][0:64], pbh[7][0:64, 0:512].rearrange("p (a t) -> p a t", a=4), r=["pb7"], w=[kQn])
                        cp("dve", QTn[1][64:128], pbh[7][64:128, 0:512].rearrange("p (a t) -> p a t", a=4), r=["pb7"], w=[kQn])
                        if "QT" in dbg_out and i == dbg_i:
                            P.barrier()
                            pass
                            P.barrier()

                        yield
                        for ck in range(nck):
                            k0 = ck * 512
                            wd = min(512, L - k0)
                            sc_ap = score[:, k0:k0 + wd]
                            for h in range(8):
                                sbk = 6 + h % 2
                                mm(pb[sbk][:, 0:wd], QTi[:, h, :], KT[:, 0, k0:k0 + wd], True, True, r=["QTi"], w=[f"pb{sbk}"])
                                act(rbuf[h % 2][:, 0:wd], pb[sbk][:, 0:wd], AF.Relu, scale=wabs[:, h:h + 1], r=[f"pb{sbk}", "wabs"], w=[f"rbuf{h % 2}"])
                                if h == 0:
                                    ts("dve", sc_ap, rbuf[h % 2][:, 0:wd], wsgn[:, 0:1], None, ALU.mult, r=[f"rbuf{h % 2}", "wsgn"], w=["score"])
                                else:
                                    P.op("dve", lambda e, sc_ap=sc_ap, h=h, wd=wd: e.scalar_tensor_tensor(out=sc_ap, in0=rbuf[h % 2][:, 0:wd], scalar=wsgn[:, h:h + 1], in1=sc_ap, op0=ALU.mult, op1=ALU.add),
                                         r=[f"rbuf{h % 2}", "wsgn", "score"], w=["score"])
                                yield
                        yield

                    def stageB():
                        def thr_steps(L=L):
                            P.op("dve", lambda e: e.tensor_reduce(out=lo, in_=score[:, 0:L], axis=AX.X, op=ALU.min), r=["score"], w=["lo"])
                            tt("dve", score[:, L - 256:L], score[:, L - 256:L], cmask, ALU.add, r=["score", "cmask"], w=["score"])
                            P.op("dve", lambda e: e.tensor_reduce(out=hi, in_=score[:, 0:L], axis=AX.X, op=ALU.max), r=["score"], w=["hi"])
                            tt("dve", d0, hi, lo, ALU.subtract, r=["hi", "lo"], w=["d0"])
                            ts("dve", Hh, pow2, d0, None, ALU.mult, r=["pow2", "d0"], w=["Hh"])
                            ts("dve", Hh2, Hh, 2.0, None, ALU.mult, r=["Hh"], w=["Hh2"])
                            tt("dve", mid, lo, Hh[:, 0:1], ALU.add, r=["lo", "Hh"], w=["mid"])
                            yield
                            wA = ((L // 3) // 512) * 512
                            LD = L - wA
                            nch = (LD + 2047) // 2048
                            thrA = TOPK - 0.5 - 0.5 * wA
                            for k in range(NIT):
                                if wA:
                                    act(junkA[:, 0:wA], score[:, LD:L], AF.Sign, scale=-1.0, bias=mid, accum_out=cntA, r=["score", "mid"], w=["QH", "cntA"])
                                for c4 in range(nch):
                                    c0 = c4 * 2048
                                    w4 = min(2048, LD - c0)
                                    seed = None if c4 == 0 else cnt[:, c4 - 1:c4]
                                    rk = ["score", "mid"] + ([f"cnt{c4 - 1}"] if c4 else [])
                                    ts("dve", mkj[:, 0:w4], score[:, c0:c0 + w4], mid, seed, ALU.is_ge, ALU.add, r=rk, w=[f"cnt{c4}", "xq_s"], accum=cnt[:, c4:c4 + 1])
                                ckey, cap, cthr = f"cnt{nch - 1}", cnt[:, nch - 1:nch], TOPK - 0.5
                                if wA:
                                    P.op("dve", lambda e, nch=nch: e.scalar_tensor_tensor(out=cnttot, in0=cntA, scalar=-0.5, in1=cnt[:, nch - 1:nch], op0=ALU.mult, op1=ALU.add),
                                         r=["cntA", f"cnt{nch - 1}"], w=["cnttot"])
                                    ckey, cap, cthr = "cnttot", cnttot, thrA
                                ts("dve", t1, cap, cthr, Hh2[:, k + 1:k + 2], ALU.is_ge, ALU.mult, r=[ckey, "Hh2"], w=["t1"])
                                P.op("dve", lambda e, k=k: e.scalar_tensor_tensor(out=mid, in0=t1, scalar=Hh[:, k + 1:k + 2], in1=mid, op0=ALU.subtract, op1=ALU.add),
                                     r=["t1", "Hh", "mid"], w=["mid"])
                                yield
                            tt("dve", lo, mid, Hh[:, NIT:NIT + 1], ALU.subtract, r=["mid", "Hh"], w=["lo"])
                            yield

                        def nsa_steps(i=i, L=L, nkb=nkb, nck=nck):
                            NB = (16 * i + 14) // 128 + 1
                            for nb in range(NB):
                                ts("dve", mcT[:, nb, :], cn[:, nb * 128:(nb + 1) * 128], tq, 256.0 * i, ALU.subtract, ALU.is_gt, r=["tq", "cn"], w=["mcT"])
                                ts("dve", mcT[:, nb, :], mcT[:, nb, :], MNEG, None, ALU.mult, r=["mcT"], w=["mcT"])
                            for g in range(2):
                                zero_bank(4)
                                zero_bank(5)
                                yield from attend(list(range(NB)), [g],
                                                  lambda kb, g_: KCT[:, kb * 128:(kb + 1) * 128],
                                                  lambda kb, g_: VC[:, kb, g_, 0:194],
                                                  lambda kb, g_: (mcT[:, kb, :], ["mcT"]),
                                                  lambda g_: QTn[g_],
                                                  lambda g_, r_: (pb[4 + r_ // 2][:, (r_ % 2) * 194:(r_ % 2) * 194 + 194], f"pb{4 + r_ // 2}"), kkeys=[kQn])
                                for b2 in range(2):
                                    accv = pb[4 + b2][:, 0:388].rearrange("p (h c) -> p h c", c=194)
                                    ts("dve", rsum[:, 0:2], accv[:, :, 64], 1e-30, None, ALU.max, r=[f"pb{4 + b2}"], w=["rsum"])
                                    P.op("dve", lambda e: e.reciprocal(out=rsum[:, 0:2], in_=rsum[:, 0:2]), r=["rsum"], w=["rsum"])
                                    h0 = g * 4 + 2 * b2
                                    gv = gat.rearrange("p (h b) -> p h b", b=3)[:, h0:h0 + 2, 0]
                                    tt("dve", cco[:, 0:2], rsum[:, 0:2], gv, ALU.mult, r=["rsum", "gat"], w=["cco"])
                                    tt("dve", Onsa[:, h0:h0 + 2, :], accv[:, :, 0:64], cco[:, 0:2].unsqueeze(2).to_broadcast([128, 2, 64]), ALU.mult,
                                       r=[f"pb{4 + b2}", "cco"], w=["Onsa"])
                                    for r2 in range(2):
                                        if b2 == 0 and r2 == 0:
                                            ts("dve", blk[g], accv[:, r2, 66:194], rsum[:, r2:r2 + 1], None, ALU.mult, r=[f"pb{4 + b2}", "rsum"], w=[f"blk{g}"])
                                        else:
                                            P.op("dve", lambda e, accv=accv, r2=r2, g=g: e.scalar_tensor_tensor(out=blk[g], in0=accv[:, r2, 66:194], scalar=rsum[:, r2:r2 + 1], in1=blk[g], op0=ALU.mult, op1=ALU.add),
                                                 r=[f"pb{4 + b2}", "rsum", f"blk{g}"], w=[f"blk{g}"])
                                yield
                            for g in range(2):
                                tt("dve", bs[g], blk[g], fb, ALU.add, r=[f"blk{g}", "fb"], w=[f"bs{g}"])
                                P.op("dve", lambda e, g=g: e.max(out=m8[:, 0:8], in_=bs[g]), r=[f"bs{g}"], w=["m8a"])
                                P.op("dve", lambda e, g=g: e.match_replace(out=bs2, in_to_replace=m8[:, 0:8], in_values=bs[g], imm_value=NEG), r=[f"bs{g}", "m8a"], w=["bs2"])
                                P.op("dve", lambda e: e.max(out=m8[:, 8:16], in_=bs2), r=["bs2"], w=["m8b"])
                                ts("dve", thr16, m8[:, 15:16], -1.0e29, None, ALU.max, r=["m8b"], w=["thr16"])
                                ts("dve", bm[g], bs[g], thr16, None, ALU.is_ge, r=[f"bs{g}", "thr16"], w=[f"bm{g}"])
                                ts("dve", bm[g], bm[g], -1.0, -MNEG, ALU.add, ALU.mult, r=[f"bm{g}"], w=[f"bm{g}"])
                            yield
                            zero_bank(4)
                            zero_bank(5)
                            def slc_mask(ck):
                                k0 = ck * 512
                                wd = min(512, L - k0)
                                nb4 = wd // 128
                                for g in range(2):
                                    me = mexp[g][ck % 2]
                                    cp("pool", me[:, 0:wd].rearrange("p (j s) -> p j s", s=64),
                                       bm[g][:, ck * 8:ck * 8 + 2 * nb4].unsqueeze(2).to_broadcast([128, 2 * nb4, 64]), r=[f"bm{g}"], w=[f"mexp{g}{ck % 2}"])
                                    for b4 in range(nb4):
                                        kb = ck * 4 + b4
                                        if kb >= nkb - 2:
                                            kbl = kb - (nkb - 2)
                                            tt("pool", me[:, b4 * 128:(b4 + 1) * 128], me[:, b4 * 128:(b4 + 1) * 128], causT[:, kbl, :], ALU.add,
                                               r=[f"mexp{g}{ck % 2}", "causT"], w=[f"mexp{g}{ck % 2}"])

                            slc_mask(0)
                            for ck in range(nck):
                                nb4 = min(512, L - ck * 512) // 128
                                if ck + 1 < nck:
                                    slc_mask(ck + 1)
                                yield from attend([ck * 4 + b4 for b4 in range(nb4)], [0, 1],
                                                  lambda kb, g_: KT[:, 1, kb * 128:(kb + 1) * 128],
                                                  lambda kb, g_: VS[:, kb, g_, 0:66],
                                                  lambda kb, g_, ck=ck: (mexp[g_][ck % 2][:, (kb - ck * 4) * 128:(kb - ck * 4 + 1) * 128], [f"mexp{g_}{ck % 2}"]),
                                                  lambda g_: QTn[g_],
                                                  lambda g_, r_: (pb[4 + g_][:, r_ * 66:r_ * 66 + 66], f"pb{4 + g_}"), kkeys=[kQn])
                            fold_branch([(4, 0, 4, 66), (5, 4, 4, 66)], 1, False)
                            yield
                            wlist = [w_ for w_ in range(6) if 2 * i - 4 + w_ >= 0]
                            kb0 = 2 * i - 4 + wlist[0]
                            nw = len(wlist)
                            dma(kw_t[:, 0:nw * 128], scrKw[:, kb0 * 128:(kb0 + nw) * 128], w=["kw_t"])
                            dma(vw_t[:, 0:nw].rearrange("p t g c -> p t (g c)"), scrVW[kb0:kb0 + nw].rearrange("t p c -> p t c"), w=["vw_t"])
                            zero_bank(4)
                            zero_bank(5)
                            yield from attend([2 * i - 4 + w_ for w_ in wlist], [0, 1],
                                              lambda kb, g_: kw_t[:, (kb - kb0) * 128:(kb - kb0 + 1) * 128],
                                              lambda kb, g_: vw_t[:, kb - kb0, g_, 0:66],
                                              lambda kb, g_: (winT[:, kb - (2 * i - 4), :], ["winT"]),
                                              lambda g_: QTn[g_],
                                              lambda g_, r_: (pb[4 + g_][:, r_ * 66:r_ * 66 + 66], f"pb{4 + g_}"),
                                              kkeys=["kw_t", kQn], vkeys=["vw_t"])
                            fold_branch([(4, 0, 4, 66), (5, 4, 4, 66)], 2, False)
                            yield

                        gen_t = thr_steps()
                        gen_n = nsa_steps()
                        ratio = max(1, (2 * nkb + 24 + NIT) // (NIT + 1))
                        alive_t = alive_n = True
                        while alive_t or alive_n:
                            if alive_t:
                                try:
                                    next(gen_t)
                                except StopIteration:
                                    alive_t = False
                            for _ in range(ratio if alive_t else 1000000):
                                if not alive_n:
                                    break
                                try:
                                    next(gen_n)
                                except StopIteration:
                                    alive_n = False
                        if "thr" in dbg_out and i == dbg_i:
                            P.barrier()
                            dump("thr", lo)
                            dump("blk", blk[0])
                            P.barrier()


                    def stageC():
                        zero_bank(4)
                        zero_bank(5)
                        for ck in range(nck):
                            k0 = ck * 512
                            wd = min(512, L - k0)
                            ts("dve", MR[:, k0:k0 + wd], score[:, k0:k0 + wd], lo, MNEG, ALU.is_lt, ALU.mult, r=["score", "lo"], w=["MR"])
                        yield
                        for ck in range(nck):
                            nb4 = min(512, L - ck * 512) // 128
                            yield from attend([ck * 4 + b4 for b4 in range(nb4)], [0, 1],
                                              lambda kb, hf: KT[:, 0, kb * 128:(kb + 1) * 128],
                                              lambda kb, hf: VD[:, kb, 0:66],
                                              lambda kb, hf: (MR[:, kb * 128:(kb + 1) * 128], ["MR"]),
                                              lambda hf: QTd[:, 4 * hf:4 * hf + 4, :],
                                              lambda hf, r_: (pb[4 + hf][:, r_ * 66:r_ * 66 + 66], f"pb{4 + hf}"),
                                              bank_fn=lambda ui, hf: ui % 4, kkeys=[kQd])
                        Ob3 = Ob.rearrange("p (h d) -> p h d", d=64)
                        for half in range(2):
                            accv = pb[4 + half][:, 0:264].rearrange("p (h c) -> p h c", c=66)
                            P.op("dve", lambda e, accv=accv: e.reciprocal(out=rsum[:, 0:4], in_=accv[:, :, 64]), r=[f"pb{4 + half}"], w=["rsum"])
                            tt("dve", Ob3[:, 4 * half:4 * half + 4, :], accv[:, :, 0:64], rsum[:, 0:4].unsqueeze(2).to_broadcast([128, 4, 64]), ALU.mult,
                               r=[f"pb{4 + half}", "rsum"], w=["Ob"])
                        cp("act", Ob[:, 512:1024], Onsa.rearrange("p h d -> p (h d)"), r=["Onsa"], w=["Ob"])
                        if "O" in dbg_out and i == dbg_i:
                            P.barrier()
                            dump("O", Ob)
                            P.barrier()
                        for a in range(8):
                            tp(pbh[4][:, a * 128:(a + 1) * 128], Ob[:, a * 128:(a + 1) * 128], r=["Ob"], w=["pb4"])
                        cp("act", OT, pbh[4], r=["pb4"], w=["OT"])
                        dma(scrO[i], OT, r=["OT"], w=[f"scrO{i}"])

                        yield

                    return stageA, stageB, stageC

            def run_gen(g_):
                for _ in g_:
                    pass

            def interleave(ga, gb):
                la = lb = True
                while la or lb:
                    if la:
                        try:
                            next(ga)
                        except StopIteration:
                            la = False
                    if lb:
                        try:
                            next(gb)
                        except StopIteration:
                            lb = False

            ilist = list(qlist) if qlist is not None else list(range(nqb))
            sA, sB, sC = make_stages(ilist[0])
            run_gen(sA())
            sB()
            prevC = sC
            for i_ in ilist[1:]:
                sA, sB, sC = make_stages(i_)
                interleave(prevC(), sA())
                sB()
                prevC = sC
            run_gen(prevC())
            P.barrier()
        A.release(mark_resident)

        if 3 in phases:
            A.release(mark_resident0)
            Wout = A.alloc(8 * 1024, BF16).rearrange("p (c n) -> p c n", c=8)
            Wxq = A.alloc(8 * 512, BF16).rearrange("p (c n) -> p c n", c=8)
            Wxo = A.alloc(4 * 1024, BF16).rearrange("p (c n) -> p c n", c=4)
            KxT = A.alloc(4 * 256, BF16).rearrange("p (h m) -> p h m", h=4)
            Vx = A.alloc(2 * 4 * 130, BF16).rearrange("p (t h c) -> p t h c", t=2, h=4)
            gx = A.alloc(24)
            gxq = A.alloc(128)
            gxk = A.alloc(128)
            m3 = A.mark()
            stg = A.alloc(4096 + 2048)
            stgA = stg[:, 0:4096]
            dma(gx[:, 0:8], W["norm2_g"].rearrange("(c p) -> p c", p=128), w=["gx"], allow_slow_non_contiguous=True)
            dma(gx[:, 8:16], W["mem_norm_g"].rearrange("(c p) -> p c", p=128), w=["gx"], allow_slow_non_contiguous=True)
            dma(gx[:, 16:24], W["norm3_g"].rearrange("(c p) -> p c", p=128), w=["gx"], allow_slow_non_contiguous=True)
            bvec(gxq, W["xq_norm_g"], 128, "gxq")
            bvec(gxk, W["xk_norm_g"], 128, "gxk")
            ts("dve", gxq, gxq, 128 ** -0.5, None, ALU.mult, r=["gxq"], w=["gxq"])
            castn = [0]

            def cast_rows(dst3, src3, nchunk, gcol0=None, skey="stg"):
                for c in range(nchunk):
                    castn[0] += 1
                    if gcol0 is None:
                        eng = ["act", "dve", "pool"][castn[0] % 3]
                        cp(eng, dst3[:, c, :], src3[:, c, :], r=[skey], w=[uk("W")])
                    elif castn[0] % 2 == 0:
                        act(dst3[:, c, :], src3[:, c, :], AF.Copy, scale=gx[:, gcol0 + c:gcol0 + c + 1], r=[skey, "gx"], w=[uk("W")])
                    else:
                        ts("dve", dst3[:, c, :], src3[:, c, :], gx[:, gcol0 + c:gcol0 + c + 1], None, ALU.mult, r=[skey, "gx"], w=[uk("W")])

            for hf in range(2):
                s3 = stgA.rearrange("p (c n) -> p c n", c=4)
                dma(s3, W["w_out"].rearrange("(c p) n -> p c n", p=128)[:, hf * 4:(hf + 1) * 4, :], w=["stg"])
                cast_rows(Wout[:, hf * 4:(hf + 1) * 4, :], s3, 4)
            s3 = stgA.rearrange("p (c n) -> p c n", c=8)
            dma(s3, W["w_xq"].rearrange("(c p) n -> p c n", p=128), w=["stg"])
            cast_rows(Wxq, s3, 8, 0)
            s3 = stgA.rearrange("p (c n) -> p c n", c=4)
            dma(s3, W["w_xo"].rearrange("(c p) n -> p c n", p=128), w=["stg"])
            cast_rows(Wxo, s3, 4)
            m3b = A.mark()
            Wxk = A.alloc(8 * 512, BF16).rearrange("p (c n) -> p c n", c=8)
            Wxv = A.alloc(8 * 512, BF16).rearrange("p (c n) -> p c n", c=8)
            ms = A.alloc(8 * 256).rearrange("p (c t) -> p c t", c=8)
            mb = A.alloc(8 * 256, BF16).rearrange("p (c t) -> p c t", c=8)
            msq = A.alloc(8 * 256, BF16).rearrange("p (c t) -> p c t", c=8)
            kx = A.alloc(512)
            kxsq = A.alloc(512)
            kxb = A.alloc(512, BF16)
            rm = A.alloc(2)
            kxss = A.alloc(4)
            s3 = stgA.rearrange("p (c n) -> p c n", c=8)
            dma(s3, W["w_xk"].rearrange("(c p) n -> p c n", p=128), w=["stg"])
            cast_rows(Wxk, s3, 8, 8)
            dma(s3, W["w_xv"].rearrange("(c p) n -> p c n", p=128), w=["stg"])
            cast_rows(Wxv, s3, 8, 8)
            dma(ms, memT.rearrange("(c p) t -> p c t", p=128), w=["ms"])
            cp("act", mb, ms, r=["ms"], w=["mb"])
            tt("dve", msq, ms, ms, ALU.mult, r=["ms"], w=["msq"])
            P.barrier()
            memset("pool", Vx[:, :, :, 128:130], 1.0, w=["Vxones"])
            for mt in range(2):
                for c in range(8):
                    mm(pb[7][:, 2 * mt:2 * mt + 2], msq[:, c, mt * 128:(mt + 1) * 128], ones_bf[:, 0:2], c == 0, c == 7, r=["msq", "ones"], w=["pb7"])
            cp("dve", rm, pb[7][:, 0:4:2], r=["pb7"], w=["rm"])
            rstd_from_ss(rm, 2, 1.0 / 1024, "rm")
            for mt in range(2):
                for c in range(8):
                    mm(pb[0], mb[:, c, mt * 128:(mt + 1) * 128], Wxk[:, c, :], c == 0, c == 7, r=["mb"], w=["pb0"])
                for c in range(8):
                    mm(pb[1], mb[:, c, mt * 128:(mt + 1) * 128], Wxv[:, c, :], c == 0, c == 7, r=["mb"], w=["pb1"])
                act(kx, pb[0], AF.Copy, scale=rm[:, mt:mt + 1], r=["pb0", "rm"], w=["kx"])
                act(Vx[:, mt, :, 0:128], pb[1].rearrange("p (h d) -> p h d", h=4), AF.Copy, scale=rm[:, mt:mt + 1], r=["pb1", "rm"], w=["Vx"])
                tt("dve", kxsq, kx, kx, ALU.mult, r=["kx"], w=["kxsq"])
                red(kxss, kxsq.rearrange("p (h d) -> p h d", h=4), ALU.add, r=["kxsq"], w=["kxss"])
                rstd_from_ss(kxss, 4, 1.0 / 128, "kxss")
                kx3 = kx.rearrange("p (h d) -> p h d", h=4)
                tt("dve", kx3, kx3, kxss.unsqueeze(2).to_broadcast([128, 4, 128]), ALU.mult, r=["kx", "kxss"], w=["kx"])
                tt("dve", kxb.rearrange("p (h d) -> p h d", h=4), kx3, gxk.unsqueeze(1).to_broadcast([128, 4, 128]), ALU.mult, r=["kx", "gxk"], w=["kxb"])
                for h in range(4):
                    tp(pbh[6][:, h * 128:(h + 1) * 128], kxb[:, h * 128:(h + 1) * 128], r=["kxb"], w=["pb6"])
                cp("act", KxT[:, :, mt * 128:(mt + 1) * 128], pbh[6][:, 0:512].rearrange("p (h m) -> p h m", h=4), r=["pb6"], w=["KxT"])
            P.barrier()
            A.release(m3)
            Wff1 = A.alloc(8 * 4096, BF16).rearrange("p (c n) -> p c n", c=8)
            Wff2 = A.alloc(32 * 1024, BF16).rearrange("p (c n) -> p c n", c=32)
            m3 = A.mark()
            stgs = [A.alloc(4096), A.alloc(4096)]
            for c in range(8):
                sk = c % 2
                s3 = stgs[sk].rearrange("p (c n) -> p c n", c=1)
                dma(s3, W["w_ff_in"].rearrange("(c p) n -> p c n", p=128)[:, c:c + 1, :], w=[f"stg{sk}"])
                for hq_ in range(2):
                    castn[0] += 1
                    cs_ = slice(hq_ * 2048, (hq_ + 1) * 2048)
                    if castn[0] % 2 == 0:
                        act(Wff1[:, c, cs_], s3[:, 0, cs_], AF.Copy, scale=gx[:, 16 + c:17 + c], r=[f"stg{sk}", "gx"], w=[uk("W")])
                    else:
                        ts("dve", Wff1[:, c, cs_], s3[:, 0, cs_], gx[:, 16 + c:17 + c], None, ALU.mult, r=[f"stg{sk}", "gx"], w=[uk("W")])
            for c8 in range(8):
                sk = c8 % 2
                s3 = stgs[sk].rearrange("p (c n) -> p c n", c=4)
                dma(s3, W["w_ff_out"].rearrange("(c p) n -> p c n", p=128)[:, c8 * 4:(c8 + 1) * 4, :], w=[f"stg{sk}"])
                cast_rows(Wff2[:, c8 * 4:(c8 + 1) * 4, :], s3, 4, skey=f"stg{sk}")
            P.barrier()
            A.release(m3)
            OTs = A.alloc(1024, BF16).rearrange("p (c t) -> p c t", c=8)
            x1s = [A.alloc(1024) for _ in range(2)]
            xbX = A.alloc(1024, BF16)
            hTX = A.alloc(1024, BF16).rearrange("p (c t) -> p c t", c=8)
            xbM = A.alloc(1024, BF16)
            hTM = A.alloc(1024, BF16).rearrange("p (c t) -> p c t", c=8)
            xblk = A.alloc(1024)
            qx = xblk[:, 0:512]
            qxb = xblk[:, 512:768].bitcast(BF16)
            qxT = xblk[:, 768:1024].bitcast(BF16).rearrange("p (h t) -> p h t", h=4)
            ex = A.alloc(512, BF16)
            oxb = A.alloc(512, BF16)
            oxb_f = xbX.bitcast(F32)
            oxT = A.alloc(512, BF16).rearrange("p (h t) -> p h t", h=4)
            uT = A.alloc(32 * 128, BF16).rearrange("p (f t) -> p f t", f=32)
            rlb = A.alloc(1024)
            rl = [rlb[:, 0:512], rlb[:, 512:1024]]
            r2 = [A.alloc(1) for _ in range(2)]
            qxss = A.alloc(4)
            rs4 = A.alloc(4)

            def norm_T(src, key_src, junk, junk_keys, xb_, kxb, hT_, khT, r2_, kr2):
                act(junk, src, AF.Square, accum_out=r2_, r=[key_src], w=list(junk_keys) + [kr2])
                rstd_from_ss(r2_, 1, 1.0 / 1024, kr2)
                ts("dve", xb_, src, r2_, None, ALU.mult, r=[key_src, kr2], w=[kxb])
                for a in range(8):
                    tp(pbh[6][:, a * 128:(a + 1) * 128], xb_[:, a * 128:(a + 1) * 128], r=[kxb], w=["pb6"])
                cp("act", hT_, pbh[6].rearrange("p (c t) -> p c t", c=8), r=["pb6"], w=[khT])

            def genX(i, slot):
                x1 = x1s[slot]
                kx = f"x1_{slot}"
                tsl = slice(i * 128, (i + 1) * 128)
                dma(OTs, scrO[i].rearrange("p (c t) -> p c t", c=8), w=["OTs"])
                dma(x1, xq[tsl, :], w=[kx])
                for hfo in range(2):
                    for c in range(8):
                        mm(pb[4 + hfo], OTs[:, c, :], Wout[:, c, hfo * 512:(hfo + 1) * 512], c == 0, c == 7, r=["OTs"], w=[f"pb{4 + hfo}"])
                    tt("dve", x1[:, hfo * 512:(hfo + 1) * 512], pb[4 + hfo], x1[:, hfo * 512:(hfo + 1) * 512], ALU.add, r=[f"pb{4 + hfo}", kx], w=[kx])
                yield
                if "x1" in dbg_out and i == dbg_i:
                    P.barrier()
                    dump("x1", x1)
                    P.barrier()
                norm_T(x1, kx, xblk, ["qx", "qxb", "qxT"], xbX, "xbX", hTX, "hTX", r2[0], "r2X")
                yield
                for c in range(8):
                    mm(pb[6], hTX[:, c, :], Wxq[:, c, :], c == 0, c == 7, r=["hTX"], w=["pb6"])
                cp("act", qx, pb[6], r=["pb6"], w=["qx"])
                yield
                tt("dve", oxb_f, qx, qx, ALU.mult, r=["qx"], w=["xbX"])
                red(qxss, oxb_f.rearrange("p (h d) -> p h d", h=4), ALU.add, r=["xbX"], w=["qxss"])
                rstd_from_ss(qxss, 4, 1.0 / 128, "qxss")
                qx3 = qx.rearrange("p (h d) -> p h d", h=4)
                tt("dve", qx3, qx3, qxss.unsqueeze(2).to_broadcast([128, 4, 128]), ALU.mult, r=["qx", "qxss"], w=["qx"])
                tt("dve", qxb.rearrange("p (h d) -> p h d", h=4), qx3, gxq.unsqueeze(1).to_broadcast([128, 4, 128]), ALU.mult, r=["qx", "gxq"], w=["qxb"])
                yield
                for h in range(4):
                    tp(pbh[6][:, h * 128:(h + 1) * 128], qxb[:, h * 128:(h + 1) * 128], r=["qxb"], w=["pb6"])
                cp("dve", qxT, pbh[6][:, 0:512].rearrange("p (h t) -> p h t", h=4), r=["pb6"], w=["qxT"])
                yield
                zero_bank(4)
                zero_bank(5)
                for mt in range(2):
                    for h in range(4):
                        mm(pb[7][:, h * 128:(h + 1) * 128], KxT[:, h, mt * 128:(mt + 1) * 128], qxT[:, h, :], True, True, r=["qxT", "KxT"], w=["pb7"])
                    act(ex, pb[7], AF.Exp, r=["pb7"], w=["ex"])
                    for h in range(4):
                        mm(pb[4 + h // 2][:, (h % 2) * 130:(h % 2) * 130 + 130], ex[:, h * 128:(h + 1) * 128], Vx[:, mt, h, 0:130], False, mt == 1,
                           r=["ex", "Vx"], w=[f"pb{4 + h // 2}"], skip_group_check=True)
                for b2 in range(2):
                    accv = pb[4 + b2][:, 0:260].rearrange("p (h c) -> p h c", c=130)
                    P.op("dve", lambda e, accv=accv: e.reciprocal(out=rs4[:, 0:2], in_=accv[:, :, 128]), r=[f"pb{4 + b2}"], w=["rs4"])
                    tt("dve", oxb[:, b2 * 256:(b2 + 1) * 256].rearrange("p (h d) -> p h d", h=2), accv[:, :, 0:128],
                       rs4[:, 0:2].unsqueeze(2).to_broadcast([128, 2, 128]), ALU.mult, r=[f"pb{4 + b2}", "rs4"], w=["oxb"])
                yield
                for h in range(4):
                    tp(pbh[6][:, h * 128:(h + 1) * 128], oxb[:, h * 128:(h + 1) * 128], r=["oxb"], w=["pb6"])
                cp("act", oxT, pbh[6][:, 0:512].rearrange("p (h t) -> p h t", h=4), r=["pb6"], w=["oxT"])
                yield
                for hfo in range(2):
                    for h in range(4):
                        mm(pb[4 + hfo], oxT[:, h, :], Wxo[:, h, hfo * 512:(hfo + 1) * 512], h == 0, h == 3, r=["oxT"], w=[f"pb{4 + hfo}"])
                    tt("dve", x1[:, hfo * 512:(hfo + 1) * 512], pb[4 + hfo], x1[:, hfo * 512:(hfo + 1) * 512], ALU.add, r=[f"pb{4 + hfo}", kx], w=[kx])
                if "x2" in dbg_out and i == dbg_i:
                    P.barrier()
                    dump("x2", x1)
                    P.barrier()
                yield

            def genM(i, slot):
                x1 = x1s[slot]
                kx = f"x1_{slot}"
                tsl = slice(i * 128, (i + 1) * 128)
                norm_T(x1, kx, rlb, ["rl0", "rl1"], xbM, "xbM", hTM, "hTM", r2[1], "r2M")
                yield
                for f4 in range(8):
                    bk = 2 + f4 % 2
                    for fi in range(4):
                        fc = f4 * 4 + fi
                        for c in range(8):
                            mm(pb[bk][:, fi * 128:(fi + 1) * 128], Wff1[:, c, fc * 128:(fc + 1) * 128], hTM[:, c, :], c == 0, c == 7, r=["hTM"], w=[f"pb{bk}"])
                    act(rl[f4 % 2], pb[bk], AF.Relu, r=[f"pb{bk}"], w=[f"rl{f4 % 2}"])
                    tt("pool" if f4 % 2 else "dve", uT[:, f4 * 4:(f4 + 1) * 4, :].rearrange("p f t -> p (f t)"), rl[f4 % 2], rl[f4 % 2], ALU.mult, r=[f"rl{f4 % 2}"], w=[f"uT{f4}"])
                    yield
                for fc in range(32):
                    for hfo in range(2):
                        mm(pb[hfo], uT[:, fc, :], Wff2[:, fc, hfo * 512:(hfo + 1) * 512], fc == 0, fc == 31, r=[f"uT{fc // 4}"], w=[f"pb{hfo}"])
                    if fc % 4 == 3:
                        yield
                for hfo in range(2):
                    tt("dve", x1[:, hfo * 512:(hfo + 1) * 512], pb[hfo], x1[:, hfo * 512:(hfo + 1) * 512], ALU.add, r=[f"pb{hfo}", kx], w=[kx])
                dma(out[tsl, :], x1, r=[kx], w=[f"out{i}"])
                yield

            def run_gen3(g_):
                for _ in g_:
                    pass

            def interleave3(ga, gb):
                la = lb = True
                while la or lb:
                    if la:
                        try:
                            next(ga)
                        except StopIteration:
                            la = False
                    if lb:
                        try:
                            next(gb)
                        except StopIteration:
                            lb = False

            ilist3 = list(qlist) if qlist is not None else list(range(nqb))
            run_gen3(genX(ilist3[0], 0))
            for k3 in range(len(ilist3)):
                if k3 + 1 < len(ilist3):
                    interleave3(genM(ilist3[k3], k3 % 2), genX(ilist3[k3 + 1], (k3 + 1) % 2))
                else:
                    run_gen3(genM(ilist3[k3], k3 % 2))
        P.barrier()
        print("arena peak (KiB/partition):", A.peak * 4 / 1024, "ops:", len(P.ops))
        P.emit()
    return nc


def host_prepare(inputs):
    f32 = np.float32
    x = np.asarray(inputs["x"], dtype=f32)
    mem = np.asarray(inputs["mem"], dtype=f32)
    wts = {}
    for k, v in inputs.items():
        if k in ("x", "mem"):
            continue
        v = np.asarray(v, dtype=f32)
        wts[k] = np.ascontiguousarray(v.reshape(v.shape[1:]))
    half = 32
    inv = (f32(10000.0) ** (-(np.arange(half, dtype=f32) / f32(half)))).astype(f32)

    def rope_tab(pos):
        ang = pos.astype(f32)[:, None] * inv[None, :]
        return np.concatenate([np.cos(ang), np.sin(ang)], axis=1).astype(f32)
    ropeK = rope_tab(np.arange(SEQ))
    ropeC = rope_tab(np.minimum(np.arange(512) * 16 + 31, 16 * 511 + 31))
    n = np.arange(512)
    j = np.arange(128)
    ovl = ((n[:, None] * 16 < j[None, :] * 64 + 64) & (n[:, None] * 16 + 32 > j[None, :] * 64)).astype(f32)
    ovl[511] = 0.0
    cn = np.tile((16 * np.arange(512) + 31).astype(f32)[None, :], (128, 1))
    pow2 = np.tile((2.0 ** -(np.arange(NIT + 1) + 1)).astype(f32)[None, :], (128, 1))
    ident = np.eye(128, dtype=f32)
    xT_b = [np.ascontiguousarray(x[b].T) for b in range(4)]
    memT_b = [np.ascontiguousarray(mem[b].T) for b in range(4)]
    maps = []
    m = np.arange(128)
    for c in range(8):
        b, p = c // 2, c % 2
        tok = (np.arange(NQB)[:, None] * 2 + p) * 128 + m[None, :]
        tok = tok.reshape(-1)
        xq = np.ascontiguousarray(x[b][tok])
        xqT = np.ascontiguousarray(xq.T)
        sp = np.arange(256)
        cmask = np.where(sp[None, :] <= (128 * p + m)[:, None], 0.0, NEG).astype(f32)
        s = np.arange(128)
        causT = np.zeros((128, 2, 128), f32)
        for kbl in range(2):
            causT[:, kbl, :] = (128 * kbl + s[None, :] <= 128 * p + m[:, None])
        winT = np.zeros((128, 6, 128), f32)
        for w in range(6):
            rel = 128 * (w - 4 - p) + s[None, :] - m[:, None]
            winT[:, w, :] = (rel <= 0) & (rel > -512)
        tq = (128 * p + m).astype(f32)[:, None].copy()
        fb = np.zeros((NQB, 128, 128), f32)
        for i in range(NQB):
            t = (2 * i + p) * 128 + m
            tb = t // 64
            forced = (j[None, :] == 0) | (j[None, :] == tb[:, None]) | (j[None, :] == tb[:, None] - 1)
            adm = j[None, :] * 64 <= t[:, None]
            fb[i] = np.where(adm, np.where(forced, 1e4, 0.0), NEG)
        d = dict(xT=xT_b[b], xqT=xqT, xq=xq, memT=memT_b[b], c_ident=ident, c_ropeK=ropeK, c_ropeQ=ropeK[tok],
                 c_ropeC=ropeC, c_cmask=cmask, c_causT=causT.reshape(128, 256), c_winT=winT.reshape(128, 768),
                 c_tq=tq, c_cn=cn, c_fbias=fb.reshape(NQB * 128, 128), c_ovl=ovl, c_pow2=pow2)
        d.update(wts)
        maps.append(d)
    return maps


def kernel(**inputs):
    maps = host_prepare(inputs)
    nc = build_program()
    res = run_bass_kernel_spmd(nc, maps, core_ids=list(range(8)))
    outp = np.zeros((4, SEQ, D_MODEL), np.float32)
    for c in range(8):
        b, p = c // 2, c % 2
        o = np.asarray(res.results[c]["out"]).reshape(NQB, 128, D_MODEL)
        for i in range(NQB):
            qb = 2 * i + p
            outp[b, qb * 128:(qb + 1) * 128] = o[i]
    return outp
```

```python
import numpy as np
from contextlib import ExitStack
import concourse.bass as bass
import concourse.mybir as mybir
from concourse.bass_utils import run_bass_kernel_spmd

F32 = mybir.dt.float32
BF16 = mybir.dt.bfloat16
ALU = mybir.AluOpType
AF = mybir.ActivationFunctionType
AX = mybir.AxisListType

SAME_ENGINE_SYNC = True
SEM_CHUNK = 16000
NSLOT = 8

D_MODEL = 1024
SEQ = 8192
NQB = 32
NKT = 64
NST = 16
NIT = 16
NEG = -1.0e30
EPS = 1e-6
IDX_SCALE = 8 ** -0.5 * 64 ** -0.5
TOPK = 256


class Prog:
    ENGS = ["pe", "act", "dve", "pool", "sp"]
    BLK = {"pe": "tensor", "act": "scalar", "dve": "vector", "pool": "gpsimd", "sp": "sync"}

    def __init__(self, nc):
        self.nc = nc
        self.ops = []
        self.lastw = {}
        self.readers = {}

    def op(self, eng, fn, r=(), w=(), dma=False):
        i = len(self.ops)
        w = list(w) + [k for k in r if k.startswith("pb") and k not in w]
        deps = set()
        for k in r:
            if k in self.lastw:
                deps.add(self.lastw[k])
        for k in w:
            if k in self.lastw:
                deps.add(self.lastw[k])
            for x in self.readers.get(k, {}).values():
                deps.add(x)
        for k in w:
            self.lastw[k] = i
            self.readers[k] = {}
        for k in r:
            if k not in w:
                d = self.readers.setdefault(k, {})
                d[("dma", i) if dma else eng] = i
        fd = set()
        for d in deps:
            o = self.ops[d]
            if (not dma) and (not o["dma"]) and o["eng"] == eng:
                if eng == "pe" or not SAME_ENGINE_SYNC:
                    continue
            fd.add(d)
        self.ops.append(dict(eng=eng, fn=fn, deps=fd, dma=dma, sig=None))
        return i

    def barrier(self):
        last = {}
        dmas = []
        for i, o in enumerate(self.ops):
            if o["fn"] is None:
                continue
            if o["dma"]:
                dmas.append(i)
            else:
                last[o["eng"]] = i
        deps = set(last.values()) | set(dmas)
        for e in self.ENGS:
            self.ops.append(dict(eng=e, fn=None, deps=set(deps), dma=False, sig=None))
        self.lastw = {}
        self.readers = {}

    def emit(self):
        nc = self.nc
        ops = self.ops
        needs = [False] * len(ops)
        for o in ops:
            for d in o["deps"]:
                needs[d] = True
        cnt = {e: 0 for e in self.ENGS}
        dcnt = {e: 0 for e in self.ENGS}
        slot_uses = {}
        for i, o in enumerate(ops):
            e = o["eng"]
            if o["fn"] is None:
                continue
            if o["dma"]:
                k = dcnt[e]
                dcnt[e] += 1
                slot = (e, k % NSLOT)
                slot_uses[slot] = slot_uses.get(slot, 0) + 1
                o["sig"] = ("d", slot, 16 * slot_uses[slot])
            elif needs[i]:
                cnt[e] += 1
                c = cnt[e] - 1
                o["sig"] = ("c", (e, c // SEM_CHUNK), c % SEM_CHUNK + 1)
        with ExitStack() as st:
            sems = {}
            for o in ops:
                if o["sig"] is not None:
                    kk = (o["sig"][0], o["sig"][1])
                    if kk not in sems:
                        sems[kk] = st.enter_context(nc.semaphore(f"s{kk[0]}_{kk[1][0]}_{kk[1][1]}"))
            block = st.enter_context(nc.Block())
            for e in self.ENGS:
                myops = [o for o in ops if o["eng"] == e]

                def body(eng, e=e, myops=myops):
                    waited = {}

                    def wait(sig):
                        kind, key, val = sig
                        kk = (kind, key)
                        if waited.get(kk, 0) >= val:
                            return
                        eng.wait_ge(sems[kk], val)
                        waited[kk] = val
                    for o in myops:
                        for d in sorted(o["deps"]):
                            wait(ops[d]["sig"])
                        if o["fn"] is None:
                            continue
                        if o["dma"] and o["sig"][2] > 16:
                            wait(("d", o["sig"][1], o["sig"][2] - 16))
                        ins = o["fn"](eng)
                        if o["sig"] is not None:
                            ins.then_inc(sems[(o["sig"][0], o["sig"][1])], 16 if o["dma"] else 1)
                    if e == "sp":
                        for slot, uses in slot_uses.items():
                            wait(("d", slot, 16 * uses))
                        for ee in self.ENGS:
                            if cnt[ee] > 0:
                                c = cnt[ee] - 1
                                wait(("c", (ee, c // SEM_CHUNK), c % SEM_CHUNK + 1))
                getattr(block, self.BLK[e])(body)


class Arena:
    def __init__(self, ap_f32, size_f32):
        self.ap = ap_f32
        self.size = size_f32
        self.off = 0
        self.peak = 0

    def alloc(self, n, dt=F32):
        nb = n * (4 if dt == F32 else 2)
        nf = (nb + 3) // 4
        assert self.off + nf <= self.size, f"SBUF arena overflow: {self.off}+{nf}>{self.size}"
        v = self.ap[:, self.off:self.off + nf]
        self.off += nf
        self.peak = max(self.peak, self.off)
        if dt != F32:
            v = v.bitcast(dt)
        return v

    def mark(self):
        return self.off

    def release(self, m):
        self.off = m


C_CQ = (0, 256)
C_CKV = (256, 384)
C_KIDX = (384, 448)
C_WIDX = (448, 456)
C_QNSA = (456, 968)
C_CMPK = (968, 1096)
C_CMPV = (1096, 1224)
C_SLCK = (1224, 1352)
C_SLCV = (1352, 1480)
C_WINK = (1480, 1608)
C_WINV = (1608, 1736)
C_GNSA = (1736, 1760)
W1Q_GROUPS = [C_CQ, C_WIDX, C_GNSA, C_QNSA]
W1K_GROUPS = [C_SLCK, C_SLCV, C_WINK, C_WINV, C_CKV, C_KIDX]
W1C_GROUPS = [C_CMPK, C_CMPV]

import os
LEVEL = float(os.environ.get('K1_LEVEL', '99'))
K1_NST = int(os.environ.get('K1_NST', '16'))


def build_program(dbg=None, nqb=NQB, phases=(0, 1, 2, 3), qlist=None, dbg_i=-1):
    dbg = dbg or {}
    nc = bass.Bass("TRN2", target_bir_lowering=False)

    def din(name, shape, dt=F32):
        return nc.dram_tensor(name, list(shape), dt, kind="ExternalInput").ap()

    xT = din("xT", [1024, SEQ])
    xqT = din("xqT", [1024, NQB * 128])
    xq = din("xq", [NQB * 128, 1024])
    memT = din("memT", [1024, 256])
    W = {}
    for nm, shp in [("norm1_g", [1024]), ("w_in", [1024, 1760]), ("dsa_cq_g", [256]), ("dsa_ckv_g", [128]),
                    ("w_dsa_uq", [256, 512]), ("w_dsa_ukv", [128, 128]), ("w_idx_q", [256, 512]),
                    ("idx_k_ln_g", [64]), ("idx_k_ln_b", [64]), ("dsa_qn_g", [64]), ("dsa_kn_g", [64]),
                    ("nsa_cmp_pe", [2, 32, 64]), ("w_nsa_cmp", [2, 32, 64, 64]), ("nsa_qn_g", [64]),
                    ("nsa_kn_g", [3, 64]), ("w_out", [1024, 1024]), ("norm2_g", [1024]), ("mem_norm_g", [1024]),
                    ("w_xq", [1024, 512]), ("w_xk", [1024, 512]), ("w_xv", [1024, 512]), ("xq_norm_g", [128]),
                    ("xk_norm_g", [128]), ("w_xo", [512, 1024]), ("norm3_g", [1024]), ("w_ff_in", [1024, 4096]),
                    ("w_ff_out", [4096, 1024])]:
        W[nm] = din(nm, shp)
    c_ident = din("c_ident", [128, 128])
    c_ropeK = din("c_ropeK", [SEQ, 64])
    c_ropeQ = din("c_ropeQ", [NQB * 128, 64])
    c_ropeC = din("c_ropeC", [512, 64])
    c_cmask = din("c_cmask", [128, 256])
    c_causT = din("c_causT", [128, 256])
    c_winT = din("c_winT", [128, 768])
    c_tq = din("c_tq", [128, 1])
    c_cn = din("c_cn", [128, 512])
    c_fbias = din("c_fbias", [NQB * 128, 128])
    c_ovl = din("c_ovl", [512, 128])
    c_pow2 = din("c_pow2", [128, NIT + 1])
    out = nc.dram_tensor("out", [NQB * 128, 1024], F32, kind="ExternalOutput").ap()
    scrO = nc.dram_tensor("scrO", [NQB, 128, 1024], BF16, kind="Internal").ap()
    scrKw = nc.dram_tensor("scrKw", [128, SEQ], BF16, kind="Internal").ap()
    scrVW = nc.dram_tensor("scrVW", [NKT, 128, 132], BF16, kind="Internal").ap()
    dbg_out = {}
    for nm, shp in dbg.items():
        dbg_out[nm] = nc.dram_tensor("dbg_" + nm, list(shp), F32, kind="ExternalOutput").ap()

    P = Prog(nc)
    with ExitStack() as st:
        ARENA_F32 = 53000
        arena_t = st.enter_context(nc.sbuf_tensor("arena", [128, ARENA_F32], F32))
        A = Arena(arena_t[:], ARENA_F32)
        pb = [st.enter_context(nc.psum_tensor(f"pb{k}", [128, 512], F32))[:] for k in range(8)]
        pbh = [p.bitcast(BF16) for p in pb]

        def dma(out_, in_, r=(), w=(), eng="sp", **kw):
            P.op(eng, lambda e: e.dma_start(out=out_, in_=in_, **kw), r=r, w=w, dma=True)

        def tt(eng, out_, in0, in1, op, r=(), w=()):
            P.op(eng, lambda e: e.tensor_tensor(out=out_, in0=in0, in1=in1, op=op), r=r, w=w)

        def ts(eng, out_, in0, s1, s2, op0, op1=None, r=(), w=(), accum=None):
            kw = {}
            if op1 is not None:
                kw["op1"] = op1
            if accum is not None:
                kw["accum_out"] = accum
            P.op(eng, lambda e: e.tensor_scalar(out=out_, in0=in0, scalar1=s1, scalar2=s2, op0=op0, **kw), r=r, w=w)

        def cp(eng, out_, in_, r=(), w=()):
            if eng == "act":
                P.op(eng, lambda e: e.copy(out=out_, in_=in_), r=r, w=w)
            else:
                P.op(eng, lambda e: e.tensor_copy(out=out_, in_=in_), r=r, w=w)

        def act(out_, in_, func, r=(), w=(), **kw):
            P.op("act", lambda e: e.activation(out=out_, in_=in_, func=func, **kw), r=r, w=w)

        def mm(out_, lhsT, rhs, start, stop, r=(), w=(), **kw):
            P.op("pe", lambda e: e.matmul(out_, lhsT=lhsT, rhs=rhs, start=start, stop=stop, **kw), r=r, w=w)

        def tp(out_, in_, r=(), w=()):
            P.op("pe", lambda e: e.transpose(out=out_, in_=in_, identity=ident_bf), r=list(r) + ["ident"], w=w)

        def memset(eng, ap, val, w=()):
            P.op(eng, lambda e: e.memset(ap, val), w=w)

        def red(out_, in_, op, r=(), w=()):
            P.op("dve", lambda e: e.tensor_reduce(out=out_, in_=in_, axis=AX.X, op=op), r=r, w=w)

        use_pow = [False]

        def rstd_from_ss(ss_ap, n, inv_n, key, eng_r=()):
            ts("dve", ss_ap, ss_ap, inv_n, EPS, ALU.mult, ALU.add, r=[key], w=[key])
            if use_pow[0]:
                tt("pool", ss_ap, ss_ap, neghalf[:, 0:n], ALU.pow, r=[key, "neghalf"], w=[key])
            else:
                act(ss_ap, ss_ap, AF.Sqrt, r=[key], w=[key])
                P.op("dve", lambda e: e.reciprocal(out=ss_ap, in_=ss_ap), r=[key], w=[key])

        def dump(name, ap_sb):
            if name not in dbg_out:
                return
            dst = dbg_out[name]
            n = ap_sb.shape[1]
            CH = 256
            for ci, off in enumerate(range(0, n, CH)):
                w_ = min(CH, n - off)
                tmpb = dbg_tmp[ci % 2]
                cp("dve", tmpb[:, 0:w_], ap_sb[:, off:off + w_], r=["*"], w=[f"dbgtmp{ci % 2}"])
                dma(dst[:, off:off + w_], tmpb[:, 0:w_], r=[f"dbgtmp{ci % 2}"])

        uid = [0]

        def uk(prefix="j"):
            uid[0] += 1
            return f"{prefix}{uid[0]}"

        def zero_bank(k):
            mm(pb[k], zeros_bf[0:1, 0:128], zeros_bf[0:1, 0:512], True, False, r=["zeros"], w=[f"pb{k}"], skip_group_check=True)

        ident_f = A.alloc(128)
        ident_bf = A.alloc(128, BF16)
        ones_bf = A.alloc(128, BF16)
        zeros_bf = A.alloc(512, BF16)
        neghalf = A.alloc(32)
        dbg_tmp = [A.alloc(256) for _ in range(2)] if dbg else None
        dma(ident_f, c_ident, w=["identf"])
        cp("dve", ident_bf, ident_f, r=["identf"], w=["ident"])
        memset("pool", ones_bf, 1.0, w=["ones"])
        memset("pool", zeros_bf, 0.0, w=["zeros"])
        memset("pool", neghalf, -0.5, w=["neghalf"])

        mark_resident0 = A.mark()
        KT = A.alloc(2 * SEQ, BF16).rearrange("p (a t) -> p a t", a=2)
        VD = A.alloc(NKT * 66, BF16).rearrange("p (t c) -> p t c", c=66)
        VS = A.alloc(NKT * 132, BF16).rearrange("p (t g c) -> p t g c", g=2, c=66)
        KCT = A.alloc(512, BF16)
        VC = A.alloc(4 * 2 * 194, BF16).rearrange("p (t g c) -> p t g c", g=2, c=194)
        mark_resident = A.mark()

        m0 = A.mark()
        W1k = A.alloc(8 * 704, BF16).rearrange("p (c n) -> p c n", c=8)
        W1c = A.alloc(8 * 256, BF16).rearrange("p (c n) -> p c n", c=8)
        Wukv = A.alloc(128, BF16)
        Wc = A.alloc(2 * 32 * 64, BF16).rearrange("p (k l e) -> p k l e", k=2, l=32)
        biasKV = A.alloc(128)
        G6 = A.alloc(6 * 64).rearrange("p (h d) -> p h d", h=6)
        B1 = A.alloc(64)
        GC = A.alloc(64)
        g1 = A.alloc(8)
        gckv = A.alloc(1)
        m1 = A.mark()
        stg = A.alloc(4 * 1760).rearrange("p (c n) -> p c n", c=4)
        stg2 = stg.rearrange("p c n -> p (c n)")

        def bvec(dst, src_vec, n, key):
            dma(dst, src_vec.rearrange("(o n) -> o n", o=1).partition_broadcast(128), w=[key])

        w_in_v = W["w_in"].rearrange("(c p) n -> p c n", p=128)

        def load_w_in(groups, dstW, key, gvec, gkey):
            for hf in range(2):
                col = 0
                for (a, b) in groups:
                    dma(stg[:, :, col:col + (b - a)], w_in_v[:, hf * 4:(hf + 1) * 4, a:b], w=["stg"])
                    col += b - a
                for c4 in range(4):
                    c = hf * 4 + c4
                    if c % 2 == 0:
                        act(dstW[:, c, :], stg[:, c4, 0:col], AF.Copy, scale=gvec[:, c:c + 1], r=["stg", gkey], w=[key])
                    else:
                        ts("dve", dstW[:, c, :], stg[:, c4, 0:col], gvec[:, c:c + 1], None, ALU.mult, r=["stg", gkey], w=[key])

        dma(g1, W["norm1_g"].rearrange("(c p) -> p c", p=128), w=["g1"], allow_slow_non_contiguous=True)
        dma(gckv, W["dsa_ckv_g"].rearrange("(c p) -> p c", p=128), w=["gckv"], allow_slow_non_contiguous=True)
        load_w_in(W1K_GROUPS, W1k, "W1k", g1, "g1")
        load_w_in(W1C_GROUPS, W1c, "W1c", g1, "g1")
        P.barrier()
        s_ukv = stg2[:, 2048:2176]
        dma(s_ukv, W["w_dsa_ukv"], w=["sukv"])
        ts("dve", Wukv, s_ukv, gckv[:, 0:1], None, ALU.mult, r=["sukv", "gckv"], w=["Wukv"])
        s_wc = stg2[:, 2176:2176 + 4096].rearrange("p (k l e) -> p k l e", k=2, l=32)
        wc_src = W["w_nsa_cmp"].rearrange("k l d e -> d k l e")
        for hf in range(2):
            for kv in range(2):
                dma(s_wc[hf * 64:(hf + 1) * 64, kv], wc_src[:, kv], w=["swc"])
        cp("act", Wc, s_wc, r=["swc"], w=["Wc"])
        s_pe = stg2[:, 6400:6464]
        dma(s_pe[0:64, :], W["nsa_cmp_pe"].rearrange("k l d -> (k l) d"), w=["spe"])
        pe_bf = A.alloc(64, BF16)
        cp("dve", pe_bf[0:64, :], s_pe[0:64, :], r=["spe"], w=["pebf"])
        P.op("pe", lambda e: e.transpose(out=pbh[6][0:64, 0:64], in_=pe_bf[0:64, 0:64], identity=ident_bf[0:64, 0:64]), r=["pebf", "ident"], w=["pb6"])
        peT = A.alloc(64, BF16)
        cp("act", peT[0:64, :], pbh[6][0:64, 0:64], r=["pb6"], w=["peT"])
        peB = A.alloc(64 * 128, BF16).rearrange("p (a m) -> p a m", a=64)
        cp("dve", peB[0:64], peT[0:64, :].unsqueeze(2).to_broadcast([64, 64, 128]), r=["peT"], w=["peB"])
        for kv in range(2):
            for l in range(32):
                mm(pb[7][:, kv * 64:(kv + 1) * 64], peB[0:64, kv * 32 + l, :], Wc[0:64, kv, l, :], l == 0, l == 31,
                   r=["peB", "Wc"], w=["pb7"])
        cp("dve", biasKV, pb[7][:, 0:128], r=["pb7"], w=["biasKV"])
        bvec(G6[:, 0, :], W["dsa_kn_g"], 64, "G6")
        bvec(G6[:, 1, :], W["idx_k_ln_g"], 64, "G6")
        for hh, row in [(2, 1), (3, 1), (4, 2), (5, 2)]:
            dma(G6[:, hh, :], W["nsa_kn_g"][row:row + 1, :].partition_broadcast(128), w=["G6"])
        dma(GC, W["nsa_kn_g"][0:1, :].partition_broadcast(128), w=["GC"])
        bvec(B1, W["idx_k_ln_b"], 64, "B1")
        memset("pool", VD[:, :, 64:66], 1.0, w=["VDones"])
        memset("pool", VS[:, :, :, 64:66], 1.0, w=["VSones"])
        memset("pool", VC[:, :, :, 64:66], 1.0, w=["VCones"])
        P.barrier()
        A.release(m1)

        cmpT = A.alloc(2 * (SEQ + 32), BF16).rearrange("p (k t) -> p k t", k=2)
        memset("pool", cmpT[:, :, SEQ:SEQ + 32], 0.0, w=["cmpTpad"])
        m1b = A.mark()
        xs = [A.alloc(8 * 512).rearrange("p (c t) -> p c t", c=8)] * 2
        hT = [A.alloc(8 * 512, BF16).rearrange("p (c t) -> p c t", c=8)] * 2
        xsq = [A.alloc(8 * 512, BF16).rearrange("p (c t) -> p c t", c=8)] * 2
        rbc = A.alloc(512)
        rtok = A.alloc(4)
        pk = [A.alloc(704) for _ in range(2)]
        ropeK = [A.alloc(64) for _ in range(2)]
        KH = [A.alloc(6 * 64).rearrange("p (h d) -> p h d", h=6) for _ in range(2)]
        KHsq2 = [A.alloc(6 * 64).rearrange("p (h d) -> p h d", h=6) for _ in range(2)]
        KHsq = KHsq2[0]
        KHb = [A.alloc(6 * 64, BF16).rearrange("p (h d) -> p h d", h=6) for _ in range(2)]
        kss = A.alloc(8)
        kss2 = [A.alloc(8) for _ in range(2)]
        kmu2 = [A.alloc(1) for _ in range(2)]
        ckvss2 = [A.alloc(1) for _ in range(2)]
        ckvn2 = [A.alloc(128, BF16) for _ in range(2)]
        ckvnT2 = [A.alloc(128, BF16) for _ in range(2)]
        rt = [A.alloc(6 * 32).rearrange("p (h d) -> p h d", h=6) for _ in range(4)]
        kwst = [A.alloc(128, BF16) for _ in range(2)]
        vwst = [A.alloc(132, BF16).rearrange("p (g c) -> p g c", g=2) for _ in range(2)]
        for q_ in range(2):
            memset("pool", vwst[q_][:, :, 64:66], 1.0, w=[f"vwones{q_}"])
        xT_v = xT.rearrange("(c p) t -> p c t", p=128)

        for stI in range(min(NST, K1_NST) if 1 in phases else 0):
            b = 0
            t0 = stI * 512
            dma(xs[b], xT_v[:, :, t0:t0 + 512], w=[f"xs{b}"])
            cp("act", hT[b], xs[b], r=[f"xs{b}"], w=[f"hT{b}"])
            tt("pool", xsq[b], xs[b], xs[b], ALU.mult, r=[f"xs{b}"], w=[f"xsq{b}"])
            for c in range(8):
                mm(pb[7], ones_bf, xsq[b][:, c, :], c == 0, c == 7, r=["ones", f"xsq{b}"], w=["pb7"])
            ts("dve", rbc, pb[7], 1.0 / 1024, EPS, ALU.mult, ALU.add, r=["pb7"], w=["rbc"])
            act(rbc, rbc, AF.Sqrt, r=["rbc"], w=["rbc"])
            P.op("dve", lambda e: e.reciprocal(out=rbc, in_=rbc), r=["rbc"], w=["rbc"])
            for j in range(4):
                for c in range(8):
                    mm(pb[6][:, 2 * j:2 * j + 2], xsq[b][:, c, j * 128:(j + 1) * 128], ones_bf[:, 0:2], c == 0, c == 7,
                       r=["ones", f"xsq{b}"], w=["pb6"])
            ts("dve", rtok, pb[6][:, 0:8:2], 1.0 / 1024, EPS, ALU.mult, ALU.add, r=["pb6"], w=["rtok"])
            act(rtok, rtok, AF.Sqrt, r=["rtok"], w=["rtok"])
            P.op("dve", lambda e: e.reciprocal(out=rtok, in_=rtok), r=["rtok"], w=["rtok"])
            for kv in range(2 if LEVEL >= 2 else 0):
                for c in range(8):
                    mm(pb[4 + kv], W1c[:, c, kv * 128:(kv + 1) * 128], hT[b][:, c, :], c == 0, c == 7,
                       r=["W1c", f"hT{b}"], w=[f"pb{4 + kv}"])
                tt("dve", cmpT[:, kv, t0:t0 + 512], pb[4 + kv], rbc, ALU.mult, r=[f"pb{4 + kv}", "rbc"], w=["cmpT"])
            def ktile(stI, j, q, b=b):
                tile = stI * 4 + j
                tsl = slice(j * 128, (j + 1) * 128)
                gsl = slice(tile * 128, (tile + 1) * 128)
                ubank = 2 if q == 0 else 5
                sq_, css_, cn_, cT_, mu_, ss_ = KHsq2[q], ckvss2[q], ckvn2[q], ckvnT2[q], kmu2[q], kss2[q]
                ta, tb = rt[2 * q], rt[2 * q + 1]
                dma(ropeK[q], c_ropeK[gsl, :], w=[f"ropeK{q}"])
                for c in range(8):
                    mm(pb[0], hT[b][:, c, tsl], W1k[:, c, 0:512], c == 0, c == 7, r=["W1k", f"hT{b}"], w=["pb0"])
                for c in range(8):
                    mm(pb[1][:, 0:192], hT[b][:, c, tsl], W1k[:, c, 512:704], c == 0, c == 7, r=["W1k", f"hT{b}"], w=["pb1"])
                act(pk[q][:, 0:512], pb[0], AF.Copy, scale=rtok[:, j:j + 1], r=["pb0", "rtok"], w=[f"pk{q}"])
                act(pk[q][:, 512:704], pb[1][:, 0:192], AF.Copy, scale=rtok[:, j:j + 1], r=["pb1", "rtok"], w=[f"pk{q}"])
                yield
                act(sq_.rearrange("p h d -> p (h d)")[:, 0:128], pk[q][:, 512:640], AF.Square, accum_out=css_, r=[f"pk{q}"], w=[f"KHsq{q}", f"ckvss{q}"])
                rstd_from_ss(css_, 1, 1.0 / 128, f"ckvss{q}")
                ts("dve", cn_, pk[q][:, 512:640], css_[:, 0:1], None, ALU.mult, r=[f"pk{q}", f"ckvss{q}"], w=[f"ckvn{q}"])
                yield
                tp(pbh[6][:, 128:256], cn_, r=[f"ckvn{q}"], w=["pb6"])
                cp("act", cT_, pbh[6][:, 128:256], r=["pb6"], w=[f"ckvnT{q}"])
                yield
                mm(pb[ubank][:, 0:128], cT_, Wukv, True, True, r=[f"ckvnT{q}", "Wukv"], w=[f"pb{ubank}"])
                kh = KH[q]
                cp("dve", kh[:, 0, :], pb[ubank][:, 0:64], r=[f"pb{ubank}"], w=[f"KH{q}"])
                cp("act", VD[:, tile, 0:64], pb[ubank][:, 64:128], r=[f"pb{ubank}"], w=[f"VD{tile}"])
                yield
                cp("act", kh[:, 2:4, :], pk[q][:, 0:128].rearrange("p (h d) -> p h d", h=2), r=[f"pk{q}"], w=[f"KH{q}"])
                cp("act", kh[:, 4:6, :], pk[q][:, 256:384].rearrange("p (h d) -> p h d", h=2), r=[f"pk{q}"], w=[f"KH{q}"])
                cp("act", VS[:, tile, :, 0:64], pk[q][:, 128:256].rearrange("p (g d) -> p g d", g=2), r=[f"pk{q}"], w=[f"VS{tile}"])
                cp("act", vwst[q][:, :, 0:64], pk[q][:, 384:512].rearrange("p (g d) -> p g d", g=2), r=[f"pk{q}", f"vwones{q}"], w=[f"vwst{q}"])
                dma(scrVW[tile], vwst[q].rearrange("p g c -> p (g c)"), r=[f"vwst{q}"], w=[f"scrVW{tile}"])
                red(mu_, pk[q][:, 640:704], ALU.add, r=[f"pk{q}"], w=[f"kmu{q}"])
                ts("dve", mu_, mu_, -1.0 / 64, None, ALU.mult, r=[f"kmu{q}"], w=[f"kmu{q}"])
                ts("dve", kh[:, 1, :], pk[q][:, 640:704], mu_[:, 0:1], None, ALU.add, r=[f"kmu{q}", f"pk{q}"], w=[f"KH{q}"])
                yield
                tt("dve", sq_, kh, kh, ALU.mult, r=[f"KH{q}"], w=[f"KHsq{q}"])
                red(ss_[:, 0:6], sq_, ALU.add, r=[f"KHsq{q}"], w=[f"kss{q}"])
                rstd_from_ss(ss_[:, 0:6], 6, 1.0 / 64, f"kss{q}")
                yield
                tt("dve", kh, kh, ss_[:, 0:6].unsqueeze(2).to_broadcast([128, 6, 64]), ALU.mult, r=[f"KH{q}", f"kss{q}"], w=[f"KH{q}"])
                tt("dve", kh, kh, G6, ALU.mult, r=[f"KH{q}", "G6"], w=[f"KH{q}"])
                tt("dve", kh[:, 1, :], kh[:, 1, :], B1, ALU.add, r=[f"KH{q}", "B1"], w=[f"KH{q}"])
                yield
                cosb = ropeK[q][:, 0:32].unsqueeze(1).to_broadcast([128, 6, 32])
                sinb = ropeK[q][:, 32:64].unsqueeze(1).to_broadcast([128, 6, 32])
                x1 = kh[:, :, 0:32]
                x2 = kh[:, :, 32:64]
                tt("dve", ta, x1, cosb, ALU.mult, r=[f"KH{q}", f"ropeK{q}"], w=[f"rta{q}"])
                tt("dve", tb, x2, sinb, ALU.mult, r=[f"KH{q}", f"ropeK{q}"], w=[f"rtb{q}"])
                tt("dve", KHb[q][:, :, 0:32], ta, tb, ALU.subtract, r=[f"rta{q}", f"rtb{q}"], w=[f"KHb{q}"])
                yield
                tt("dve", ta, x1, sinb, ALU.mult, r=[f"KH{q}", f"ropeK{q}"], w=[f"rta{q}"])
                tt("dve", tb, x2, cosb, ALU.mult, r=[f"KH{q}", f"ropeK{q}"], w=[f"rtb{q}"])
                tt("dve", KHb[q][:, :, 32:64], ta, tb, ALU.add, r=[f"rta{q}", f"rtb{q}"], w=[f"KHb{q}"])
                yield
                khb2 = KHb[q].rearrange("p (a two) d -> p a (two d)", two=2)
                for a3 in range(3):
                    tp(pbh[3][:, a3 * 128:(a3 + 1) * 128], khb2[:, a3, :], r=[f"KHb{q}"], w=["pb3"])
                cp("act", KT[:, :, gsl], pbh[3][:, 0:256].rearrange("p (a t) -> p a t", a=2), r=["pb3"], w=[f"KT{tile}"])
                cp("dve", kwst[q], pbh[3][:, 256:384], r=["pb3"], w=[f"kwst{q}"])
                dma(scrKw[:, gsl], kwst[q], r=[f"kwst{q}"], w=[f"scrKw{tile}"])
                yield

            if LEVEL >= 3:
                for jp in range(2):
                    ga_, gb_ = ktile(stI, 2 * jp, 0), ktile(stI, 2 * jp + 1, 1)
                    la_ = lb_ = True
                    while la_ or lb_:
                        if la_:
                            try:
                                next(ga_)
                            except StopIteration:
                                la_ = False
                        if lb_:
                            try:
                                next(gb_)
                            except StopIteration:
                                lb_ = False

        if 1 in phases and LEVEL >= 7:
            P.barrier()
            A.release(m1b)
            rt = [A.alloc(6 * 32).rearrange("p (h d) -> p h d", h=6) for _ in range(4)]
            kss = A.alloc(8)
            ropeC = A.alloc(4 * 64).rearrange("p (t d) -> p t d", t=4)
            dma(ropeC, c_ropeC.rearrange("(t p) d -> p t d", p=128), w=["ropeC"])
            ovl_f = A.alloc(4 * 128).rearrange("p (t j) -> p t j", t=4)
            dma(ovl_f, c_ovl.rearrange("(t p) j -> p t j", p=128), w=["ovlf"])
            for g in range(2):
                cp("dve", VC[:, :, g, 66:194], ovl_f, r=["ovlf"], w=["VC"])
            kc = A.alloc(2 * 64).rearrange("p (g d) -> p g d", g=2)
            kcsq = A.alloc(2 * 64).rearrange("p (g d) -> p g d", g=2)
            kcb = A.alloc(2 * 64, BF16)
            for nb in range(4):
                for kv in range(2):
                    for g in range(2):
                        bank = pb[0] if g == 0 else pb[2]
                        for l in range(32):
                            s0 = nb * 2048 + l
                            mm(bank[:, kv * 64:(kv + 1) * 64], cmpT[g * 64:(g + 1) * 64, kv, s0:s0 + 16 * 127 + 1:16],
                               Wc[g * 64:(g + 1) * 64, kv, l, :], l == 0, l == 31, r=["cmpT", "cmpTpad", "Wc"], w=["pb0" if g == 0 else "pb2"])
                bk = biasKV.rearrange("p (k e) -> p k e", k=2)
                for g in range(2):
                    bank = pb[0] if g == 0 else pb[2]
                    bkey = "pb0" if g == 0 else "pb2"
                    tt("dve", kc[:, g, :], bank[:, 0:64], bk[:, 0, :], ALU.add, r=[bkey, "biasKV"], w=["kc"])
                    tt("dve", VC[:, nb, g, 0:64], bank[:, 64:128], bk[:, 1, :], ALU.add, r=[bkey, "biasKV"], w=["VC"])
                tt("dve", kcsq, kc, kc, ALU.mult, r=["kc"], w=["kcsq"])
                red(kss[:, 6:8], kcsq, ALU.add, r=["kcsq"], w=["kss2"])
                ts("dve", kss[:, 6:8], kss[:, 6:8], 1.0 / 64, EPS, ALU.mult, ALU.add, r=["kss2"], w=["kss2"])
                act(kss[:, 6:8], kss[:, 6:8], AF.Sqrt, r=["kss2"], w=["kss2"])
                P.op("dve", lambda e: e.reciprocal(out=kss[:, 6:8], in_=kss[:, 6:8]), r=["kss2"], w=["kss2"])
                tt("dve", kc, kc, kss[:, 6:8].unsqueeze(2).to_broadcast([128, 2, 64]), ALU.mult, r=["kc", "kss2"], w=["kc"])
                tt("dve", kc, kc, GC.unsqueeze(1).to_broadcast([128, 2, 64]), ALU.mult, r=["kc", "GC"], w=["kc"])
                cosb = ropeC[:, nb, 0:32].unsqueeze(1).to_broadcast([128, 2, 32])
                sinb = ropeC[:, nb, 32:64].unsqueeze(1).to_broadcast([128, 2, 32])
                kcb3 = kcb.rearrange("p (g d) -> p g d", g=2)
                tt("dve", rt[0][:, 0:2, :], kc[:, :, 0:32], cosb, ALU.mult, r=["kc", "ropeC"], w=["rt0"])
                tt("dve", rt[1][:, 0:2, :], kc[:, :, 32:64], sinb, ALU.mult, r=["kc", "ropeC"], w=["rt1"])
                tt("dve", rt[2][:, 0:2, :], kc[:, :, 0:32], sinb, ALU.mult, r=["kc", "ropeC"], w=["rt2"])
                tt("dve", rt[3][:, 0:2, :], kc[:, :, 32:64], cosb, ALU.mult, r=["kc", "ropeC"], w=["rt3"])
                tt("dve", kcb3[:, :, 0:32], rt[0][:, 0:2, :], rt[1][:, 0:2, :], ALU.subtract, r=["rt0", "rt1"], w=["kcb"])
                tt("dve", kcb3[:, :, 32:64], rt[2][:, 0:2, :], rt[3][:, 0:2, :], ALU.add, r=["rt2", "rt3"], w=["kcb"])
                tp(pbh[6][:, 0:128], kcb, r=["kcb"], w=["pb6"])
                cp("act", KCT[:, nb * 128:(nb + 1) * 128], pbh[6][:, 0:128], r=["pb6"], w=["KCT"])
            P.barrier()
            dump("KT", KT.rearrange("p a t -> p (a t)"))
            dump("VD", VD.rearrange("p t c -> p (t c)"))
            dump("VS", VS.rearrange("p t g c -> p (t g c)"))
            dump("KCT", KCT)
            dump("VC", VC.rearrange("p t g c -> p (t g c)"))
        P.barrier()
        A.release(m0)

        use_pow[0] = True
        memset("pool", neghalf, -0.5, w=["neghalf"])
        if 2 in phases:
            W1q = A.alloc(8 * 800, BF16).rearrange("p (c n) -> p c n", c=8)
            Wq2 = A.alloc(2 * 1024, BF16).rearrange("p (c n) -> p c n", c=2)
            GQ = A.alloc(24 * 64).rearrange("p (h d) -> p h d", h=24)
            g1 = A.alloc(8)
            gcq = A.alloc(2)
            cmask = A.alloc(256)
            causT = A.alloc(256, BF16).rearrange("p (k q) -> p k q", k=2)
            winT = A.alloc(768, BF16).rearrange("p (k q) -> p k q", k=6)
            tq = A.alloc(1)
            cn = A.alloc(512)
            pow2 = A.alloc(NIT + 1)
            m2 = A.mark()
            stg = A.alloc(4 * 1760).rearrange("p (c n) -> p c n", c=4)
            stg2 = stg.rearrange("p c n -> p (c n)")
            dma(g1, W["norm1_g"].rearrange("(c p) -> p c", p=128), w=["g1"], allow_slow_non_contiguous=True)
            dma(gcq, W["dsa_cq_g"].rearrange("(c p) -> p c", p=128), w=["gcq"], allow_slow_non_contiguous=True)
            load_w_in(W1Q_GROUPS, W1q, "W1q", g1, "g1")
            P.barrier()
            s_q2 = stg2[:, 0:2048].rearrange("p (c n) -> p c n", c=2)
            dma(s_q2[:, :, 0:512], W["w_dsa_uq"].rearrange("(c p) n -> p c n", p=128), w=["sq2"])
            dma(s_q2[:, :, 512:1024], W["w_idx_q"].rearrange("(c p) n -> p c n", p=128), w=["sq2"])
            for c in range(2):
                ts("dve", Wq2[:, c, :], s_q2[:, c, :], gcq[:, c:c + 1], None, ALU.mult, r=["sq2", "gcq"], w=["Wq2"])
            s_c = stg2[:, 2048:2048 + 1024]
            dma(s_c[:, 0:256], c_causT, w=["sc"])
            dma(s_c[:, 256:1024], c_winT, w=["sc"])
            ts("dve", causT.rearrange("p k q -> p (k q)"), s_c[:, 0:256], -1.0, 30000.0, ALU.add, ALU.mult, r=["sc"], w=["causT"])
            ts("dve", winT.rearrange("p k q -> p (k q)"), s_c[:, 256:1024], -1.0, 30000.0, ALU.add, ALU.mult, r=["sc"], w=["winT"])
            dma(cmask, c_cmask, w=["cmask"])
            dma(tq, c_tq, w=["tq"])
            dma(cn, c_cn, w=["cn"])
            dma(pow2, c_pow2, w=["pow2"])
            for pr in range(8):
                bvec(GQ[:, 2 * pr, :], W["dsa_qn_g"], 64, "GQ")
                memset("pool", GQ[:, 2 * pr + 1, :], 1.0, w=["GQ"])
            for hh in range(16, 24):
                bvec(GQ[:, hh, :], W["nsa_qn_g"], 64, "GQ")
            GQp = GQ.rearrange("p (a two) d -> p a two d", two=2)
            ts("dve", GQp[:, 0:8, 0, :], GQp[:, 0:8, 0, :], 0.125, None, ALU.mult, r=["GQ"], w=["GQ"])
            ts("dve", GQ[:, 16:24, :], GQ[:, 16:24, :], 0.125, None, ALU.mult, r=["GQ"], w=["GQ"])
            P.barrier()
            A.release(m2)

            score = A.alloc(SEQ)
            MR = A.alloc(SEQ, BF16)
            xq_flat = A.alloc(8 * 128)
            xq_s = xq_flat.rearrange("p (c t) -> p c t", c=8)
            mkj = xq_flat.bitcast(BF16)
            hq = A.alloc(8 * 128, BF16).rearrange("p (c t) -> p c t", c=8)
            xsqq = A.alloc(8 * 128, BF16).rearrange("p (c t) -> p c t", c=8)
            pq = A.alloc(800)
            cqn = A.alloc(256, BF16)
            cqnT = A.alloc(256, BF16).rearrange("p (c t) -> p c t", c=2)
            QH = A.alloc(24 * 64).rearrange("p (h d) -> p h d", h=24)
            QHsq = A.alloc(24 * 64).rearrange("p (h d) -> p h d", h=24)
            QHb = A.alloc(24 * 64, BF16).rearrange("p (h d) -> p h d", h=24)
            QTd2 = [A.alloc(8 * 128, BF16).rearrange("p (a t) -> p a t", a=8) for _ in range(2)]
            QTi = A.alloc(8 * 128, BF16).rearrange("p (a t) -> p a t", a=8)
            QTn2 = [[A.alloc(4 * 128, BF16).rearrange("p (a t) -> p a t", a=4) for _ in range(2)] for _ in range(2)]
            for par_ in range(2):
                memset("pool", QTd2[par_][64:128], 0.0, w=[f"QTd{par_}"])
                memset("pool", QTn2[par_][0][64:128], 0.0, w=[f"QTn{par_}"])
                memset("pool", QTn2[par_][1][0:64], 0.0, w=[f"QTn{par_}"])
            memset("pool", QTi[0:64], 0.0, w=["QTi"])
            ropeQ = A.alloc(64)
            rbuf = [A.alloc(512, BF16) for _ in range(2)]
            pbuf = [A.alloc(512, BF16) for _ in range(4)]
            mexp = [[A.alloc(512, BF16) for _ in range(2)] for _ in range(2)]
            mcT = A.alloc(4 * 128, BF16).rearrange("p (k q) -> p k q", k=4)
            bm = [A.alloc(128, BF16) for _ in range(2)]
            bs = [A.alloc(128) for _ in range(2)]
            bs2 = A.alloc(128)
            blk = [A.alloc(128) for _ in range(2)]
            fb = A.alloc(128)
            Onsa = A.alloc(8 * 64).rearrange("p (h d) -> p h d", h=8)
            tmpO = A.alloc(4 * 64).rearrange("p (h d) -> p h d", h=4)
            Ob = A.alloc(1024, BF16)
            OT = A.alloc(1024, BF16)
            kw_t = A.alloc(768, BF16)
            vw_t = A.alloc(6 * 132, BF16).rearrange("p (t g c) -> p t g c", t=6, g=2)
            sm = A.alloc(64)
            rq, cqss, wabs, wsgn, gat, qss = sm[:, 0:1], sm[:, 1:2], sm[:, 2:10], sm[:, 10:18], sm[:, 18:42], A.alloc(24)
            lo, hi, d0, mid, t1, cnttot, thr16 = (A.alloc(1) for _ in range(7))
            cnt = A.alloc(4)
            cntA = A.alloc(1)
            junkA = QH.rearrange("p h d -> p (h d)").bitcast(BF16)
            Hh = A.alloc(NIT + 1)
            Hh2 = A.alloc(NIT + 1)
            m8 = A.alloc(16)
            rsum = A.alloc(8)
            cco = A.alloc(8)
            xqT_v = xqT.rearrange("(c p) t -> p c t", p=128)
            print("phase2 arena used (KiB):", A.off * 4 / 1024)
            def rope(src, dstb, nh, ropet, keys_r, key_w):
                cosb = ropet[:, 0:32].unsqueeze(1).to_broadcast([128, nh, 32])
                sinb = ropet[:, 32:64].unsqueeze(1).to_broadcast([128, nh, 32])
                ta = QHsq[:, 0:nh, 0:32]
                tb = QHsq[:, 0:nh, 32:64]
                x1 = src[:, :, 0:32]
                x2 = src[:, :, 32:64]
                tt("dve", ta, x1, cosb, ALU.mult, r=keys_r, w=["QHsqa"])
                tt("pool", tb, x2, sinb, ALU.mult, r=keys_r, w=["QHsqb"])
                tt("dve", dstb[:, :, 0:32], ta, tb, ALU.subtract, r=["QHsqa", "QHsqb"], w=[key_w])
                tt("dve", ta, x1, sinb, ALU.mult, r=keys_r, w=["QHsqa"])
                tt("pool", tb, x2, cosb, ALU.mult, r=keys_r, w=["QHsqb"])
                tt("dve", dstb[:, :, 32:64], ta, tb, ALU.add, r=["QHsqa", "QHsqb"], w=[key_w])

            def fold_branch(banks, br, first):
                for (bk, h0, nh, stride) in banks:
                    accv = pb[bk][:, 0:nh * stride].rearrange("p (h c) -> p h c", c=stride)
                    ts("dve", rsum[:, 0:nh], accv[:, :, 64], 1e-30, None, ALU.max, r=[f"pb{bk}"], w=["rsum"])
                    P.op("dve", lambda e, nh=nh: e.reciprocal(out=rsum[:, 0:nh], in_=rsum[:, 0:nh]), r=["rsum"], w=["rsum"])
                    gv = gat.rearrange("p (h b) -> p h b", b=3)[:, h0:h0 + nh, br]
                    tt("dve", cco[:, 0:nh], rsum[:, 0:nh], gv, ALU.mult, r=["rsum", "gat"], w=["cco"])
                    cb = cco[:, 0:nh].unsqueeze(2).to_broadcast([128, nh, 64])
                    if first:
                        tt("dve", Onsa[:, h0:h0 + nh, :], accv[:, :, 0:64], cb, ALU.mult, r=[f"pb{bk}", "cco"], w=["Onsa"])
                    else:
                        tt("dve", tmpO[:, 0:nh, :], accv[:, :, 0:64], cb, ALU.mult, r=[f"pb{bk}", "cco"], w=["tmpO"])
                        tt("dve", Onsa[:, h0:h0 + nh, :], Onsa[:, h0:h0 + nh, :], tmpO[:, 0:nh, :], ALU.add, r=["tmpO", "Onsa"], w=["Onsa"])

            MNEG = -30000.0
            NPB = 4
            PVLAG = 2
            pctr = [0]

            def attend(kb_list, g_list, kplane_fn, v_fn, mask_fn, qsel, acc_fn, kkeys=(), vkeys=(), bank_fn=None):
                units = [(kb, g) for kb in kb_list for g in g_list]
                pend = []
                for ui, (kb, g) in enumerate(units):
                    bank = bank_fn(ui, g) if bank_fn else 2 * g + (ui // len(g_list)) % 2
                    sb = pb[bank]
                    kT = kplane_fn(kb, g)
                    mk_ap, mk_keys = mask_fn(kb, g)
                    mm(sb, kT, qsel(g), True, False, r=list(kkeys), w=[f"pb{bank}"], skip_group_check=True)
                    mm(sb, mk_ap, ident_bf.unsqueeze(1).to_broadcast([128, 4, 128]), False, True, r=["ident"] + mk_keys, w=[f"pb{bank}"], skip_group_check=True)
                    pi = pctr[0] % NPB
                    pctr[0] += 1
                    pbf = pbuf[pi]
                    act(pbf, sb, AF.Exp, r=[f"pb{bank}"], w=[f"pbuf{pi}"])
                    if len(pend) >= PVLAG:
                        pend.pop(0)()
                    def do_pv(kb=kb, g=g, pbf=pbf, pi=pi, ui=ui):
                        vv = v_fn(kb, g)
                        for r_ in range(4):
                            o_ap, okey = acc_fn(g, r_)
                            mm(o_ap, pbf[:, r_ * 128:(r_ + 1) * 128], vv, False, ui == len(units) - 1, r=[f"pbuf{pi}"] + list(vkeys), w=[okey], skip_group_check=True)
                    pend.append(do_pv)
                    yield
                while pend:
                    pend.pop(0)()
                yield

            def make_stages(i):
                    L = 256 * (i + 1)
                    nkb = L // 128
                    nck = (L + 511) // 512
                    tsl = slice(i * 128, (i + 1) * 128)
                    par = i % 2
                    QTd = QTd2[par]
                    QTn = QTn2[par]
                    kQd = f"QTd{par}"
                    kQn = f"QTn{par}"

                    def stageA():
                        dma(xq_s, xqT_v[:, :, tsl], w=["xq_s"])
                        dma(ropeQ, c_ropeQ[tsl, :], w=["ropeQ"])
                        dma(fb, c_fbias[tsl, :], w=["fb"])
                        cp("act", hq, xq_s, r=["xq_s"], w=["hq"])
                        tt("pool", xsqq, xq_s, xq_s, ALU.mult, r=["xq_s"], w=["xsqq"])
                        for c in range(8):
                            mm(pb[7][:, 0:2], xsqq[:, c, :], ones_bf[:, 0:2], c == 0, c == 7, r=["ones", "xsqq"], w=["pb7"])
                        cp("dve", rq, pb[7][:, 0:1], r=["pb7"], w=["rq"])
                        rstd_from_ss(rq, 1, 1.0 / 1024, "rq")
                        for c in range(8):
                            mm(pb[6], hq[:, c, :], W1q[:, c, 288:800], c == 0, c == 7, r=["hq", "W1q"], w=["pb6"])
                        for c in range(8):
                            mm(pb[7][:, 0:288], hq[:, c, :], W1q[:, c, 0:288], c == 0, c == 7, r=["hq", "W1q"], w=["pb7"])
                        act(pq[:, 288:800], pb[6], AF.Copy, scale=rq, r=["pb6", "rq"], w=["pqn"])
                        act(pq[:, 0:288], pb[7][:, 0:288], AF.Copy, scale=rq, r=["pb7", "rq"], w=["pqa"])
                        yield
                        act(QHsq.rearrange("p h d -> p (h d)")[:, 0:256], pq[:, 0:256], AF.Square, accum_out=cqss, r=["pqa"], w=["QHsqa", "QHsqb", "cqss"])
                        rstd_from_ss(cqss, 1, 1.0 / 256, "cqss")
                        ts("dve", cqn, pq[:, 0:256], cqss, None, ALU.mult, r=["pqa", "cqss"], w=["cqn"])
                        for c2 in range(2):
                            tp(pbh[6][:, c2 * 128:(c2 + 1) * 128], cqn[:, c2 * 128:(c2 + 1) * 128], r=["cqn"], w=["pb6"])
                        cp("act", cqnT, pbh[6][:, 0:256].rearrange("p (c t) -> p c t", c=2), r=["pb6"], w=["cqnT"])
                        for c2 in range(2):
                            mm(pb[6], cqnT[:, c2, :], Wq2[:, c2, 0:512], c2 == 0, c2 == 1, r=["cqnT", "Wq2"], w=["pb6"])
                        for c2 in range(2):
                            mm(pb[7], cqnT[:, c2, :], Wq2[:, c2, 512:1024], c2 == 0, c2 == 1, r=["cqnT", "Wq2"], w=["pb7"])
                        QHp = QH.rearrange("p (a two) d -> p a two d", two=2)
                        cp("dve", QHp[:, 0:8, 0, :], pb[6].rearrange("p (h d) -> p h d", h=8), r=["pb6"], w=["QH"])
                        cp("act", QHp[:, 0:8, 1, :], pb[7].rearrange("p (h d) -> p h d", h=8), r=["pb7"], w=["QH"])
                        for g in range(2):
                            cp("dve", QHp[:, 8:12, g, :], pq[:, 288 + g * 256:288 + (g + 1) * 256].rearrange("p (r d) -> p r d", r=4), r=["pqn"], w=["QH"])
                        yield
                        act(wabs, pq[:, 256:264], AF.Abs, scale=IDX_SCALE, r=["pqa"], w=["wabs"])
                        act(wsgn, pq[:, 256:264], AF.Sign, r=["pqa"], w=["wsgn"])
                        act(gat, pq[:, 264:288], AF.Exp, scale=-1.0, r=["pqa"], w=["gat"])
                        ts("dve", gat, gat, 1.0, None, ALU.add, r=["gat"], w=["gat"])
                        P.op("dve", lambda e: e.reciprocal(out=gat, in_=gat), r=["gat"], w=["gat"])
                        yield
                        tt("dve", QHsq, QH, QH, ALU.mult, r=["QH"], w=["QHsqa", "QHsqb"])
                        red(qss, QHsq, ALU.add, r=["QHsqa", "QHsqb"], w=["qss"])
                        rstd_from_ss(qss, 24, 1.0 / 64, "qss")
                        memset("dve", qss.rearrange("p (a two) -> p a two", two=2)[:, 0:8, 1], 1.0, w=["qss"])
                        tt("dve", QH, QH, qss.unsqueeze(2).to_broadcast([128, 24, 64]), ALU.mult, r=["QH", "qss"], w=["QH"])
                        tt("dve", QH, QH, GQ, ALU.mult, r=["QH", "GQ"], w=["QH"])
                        rope(QH, QHb, 24, ropeQ, ["QH", "ropeQ"], "QHb")
                        qhb2 = QHb.rearrange("p (a two) d -> p a (two d)", two=2)
                        for a in range(8):
                            tp(pbh[6][:, a * 128:(a + 1) * 128], qhb2[:, a, :], r=["QHb"], w=["pb6"])
                        for a in range(4):
                            tp(pbh[7][:, a * 128:(a + 1) * 128], qhb2[:, 8 + a, :], r=["QHb"], w=["pb7"])
                        cp("act", QTd[0:64], pbh[6][0:64].rearrange("p (a t) -> p a t", a=8), r=["pb6"], w=[kQd])
                        cp("dve", QTi[64:128], pbh[6][64:128].rearrange("p (a t) -> p a t", a=8), r=["pb6"], w=["QTi"])
                        cp("act", QTn[0][0:64], pbh[7][0:64, 0:512].rearrange("p (a t) -> p a t", a=4), r=["pb7"], w=[kQn])
                        cp("dve", QTn[1][64:128], pbh[7][64:128, 0:512].rearrange("p (a t) -> p a t", a=4), r=["pb7"], w=[kQn])
                        if "QT" in dbg_out and i == dbg_i:
                            P.barrier()
                            pass
                            P.barrier()

                        yield
                        for ck in range(nck):
                            k0 = ck * 512
                            wd = min(512, L - k0)
                            sc_ap = score[:, k0:k0 + wd]
                            for h in range(8):
                                sbk = 6 + h % 2
                                mm(pb[sbk][:, 0:wd], QTi[:, h, :], KT[:, 0, k0:k0 + wd], True, True, r=["QTi"], w=[f"pb{sbk}"])
                                act(rbuf[h % 2][:, 0:wd], pb[sbk][:, 0:wd], AF.Relu, scale=wabs[:, h:h + 1], r=[f"pb{sbk}", "wabs"], w=[f"rbuf{h % 2}"])
                                if h == 0:
                                    ts("dve", sc_ap, rbuf[h % 2][:, 0:wd], wsgn[:, 0:1], None, ALU.mult, r=[f"rbuf{h % 2}", "wsgn"], w=["score"])
                                else:
                                    P.op("dve", lambda e, sc_ap=sc_ap, h=h, wd=wd: e.scalar_tensor_tensor(out=sc_ap, in0=rbuf[h % 2][:, 0:wd], scalar=wsgn[:, h:h + 1], in1=sc_ap, op0=ALU.mult, op1=ALU.add),
                                         r=[f"rbuf{h % 2}", "wsgn", "score"], w=["score"])
                                yield
                        yield

                    def stageB():
                        def thr_steps(L=L):
                            P.op("dve", lambda e: e.tensor_reduce(out=lo, in_=score[:, 0:L], axis=AX.X, op=ALU.min), r=["score"], w=["lo"])
                            tt("dve", score[:, L - 256:L], score[:, L - 256:L], cmask, ALU.add, r=["score", "cmask"], w=["score"])
                            P.op("dve", lambda e: e.tensor_reduce(out=hi, in_=score[:, 0:L], axis=AX.X, op=ALU.max), r=["score"], w=["hi"])
                            tt("dve", d0, hi, lo, ALU.subtract, r=["hi", "lo"], w=["d0"])
                            ts("dve", Hh, pow2, d0, None, ALU.mult, r=["pow2", "d0"], w=["Hh"])
                            ts("dve", Hh2, Hh, 2.0, None, ALU.mult, r=["Hh"], w=["Hh2"])
                            tt("dve", mid, lo, Hh[:, 0:1], ALU.add, r=["lo", "Hh"], w=["mid"])
                            yield
                            wA = ((L // 3) // 512) * 512
                            LD = L - wA
                            nch = (LD + 2047) // 2048
                            thrA = TOPK - 0.5 - 0.5 * wA
                            for k in range(NIT):
                                if wA:
                                    act(junkA[:, 0:wA], score[:, LD:L], AF.Sign, scale=-1.0, bias=mid, accum_out=cntA, r=["score", "mid"], w=["QH", "cntA"])
                                for c4 in range(nch):
                                    c0 = c4 * 2048
                                    w4 = min(2048, LD - c0)
                                    seed = None if c4 == 0 else cnt[:, c4 - 1:c4]
                                    rk = ["score", "mid"] + ([f"cnt{c4 - 1}"] if c4 else [])
                                    ts("dve", mkj[:, 0:w4], score[:, c0:c0 + w4], mid, seed, ALU.is_ge, ALU.add, r=rk, w=[f"cnt{c4}", "xq_s"], accum=cnt[:, c4:c4 + 1])
                                ckey, cap, cthr = f"cnt{nch - 1}", cnt[:, nch - 1:nch], TOPK - 0.5
                                if wA:
                                    P.op("dve", lambda e, nch=nch: e.scalar_tensor_tensor(out=cnttot, in0=cntA, scalar=-0.5, in1=cnt[:, nch - 1:nch], op0=ALU.mult, op1=ALU.add),
                                         r=["cntA", f"cnt{nch - 1}"], w=["cnttot"])
                                    ckey, cap, cthr = "cnttot", cnttot, thrA
                                ts("dve", t1, cap, cthr, Hh2[:, k + 1:k + 2], ALU.is_ge, ALU.mult, r=[ckey, "Hh2"], w=["t1"])
                                P.op("dve", lambda e, k=k: e.scalar_tensor_tensor(out=mid, in0=t1, scalar=Hh[:, k + 1:k + 2], in1=mid, op0=ALU.subtract, op1=ALU.add),
                                     r=["t1", "Hh", "mid"], w=["mid"])
                                yield
                            tt("dve", lo, mid, Hh[:, NIT:NIT + 1], ALU.subtract, r=["mid", "Hh"], w=["lo"])
                            yield

                        def nsa_steps(i=i, L=L, nkb=nkb, nck=nck):
                            NB = (16 * i + 14) // 128 + 1
                            for nb in range(NB):
                                ts("dve", mcT[:, nb, :], cn[:, nb * 128:(nb + 1) * 128], tq, 256.0 * i, ALU.subtract, ALU.is_gt, r=["tq", "cn"], w=["mcT"])
                                ts("dve", mcT[:, nb, :], mcT[:, nb, :], MNEG, None, ALU.mult, r=["mcT"], w=["mcT"])
                            for g in range(2):
                                zero_bank(4)
                                zero_bank(5)
                                yield from attend(list(range(NB)), [g],
                                                  lambda kb, g_: KCT[:, kb * 128:(kb + 1) * 128],
                                                  lambda kb, g_: VC[:, kb, g_, 0:194],
                                                  lambda kb, g_: (mcT[:, kb, :], ["mcT"]),
                                                  lambda g_: QTn[g_],
                                                  lambda g_, r_: (pb[4 + r_ // 2][:, (r_ % 2) * 194:(r_ % 2) * 194 + 194], f"pb{4 + r_ // 2}"), kkeys=[kQn])
                                for b2 in range(2):
                                    accv = pb[4 + b2][:, 0:388].rearrange("p (h c) -> p h c", c=194)
                                    ts("dve", rsum[:, 0:2], accv[:, :, 64], 1e-30, None, ALU.max, r=[f"pb{4 + b2}"], w=["rsum"])
                                    P.op("dve", lambda e: e.reciprocal(out=rsum[:, 0:2], in_=rsum[:, 0:2]), r=["rsum"], w=["rsum"])
                                    h0 = g * 4 + 2 * b2
                                    gv = gat.rearrange("p (h b) -> p h b", b=3)[:, h0:h0 + 2, 0]
                                    tt("dve", cco[:, 0:2], rsum[:, 0:2], gv, ALU.mult, r=["rsum", "gat"], w=["cco"])
                                    tt("dve", Onsa[:, h0:h0 + 2, :], accv[:, :, 0:64], cco[:, 0:2].unsqueeze(2).to_broadcast([128, 2, 64]), ALU.mult,
                                       r=[f"pb{4 + b2}", "cco"], w=["Onsa"])
                                    for r2 in range(2):
                                        if b2 == 0 and r2 == 0:
                                            ts("dve", blk[g], accv[:, r2, 66:194], rsum[:, r2:r2 + 1], None, ALU.mult, r=[f"pb{4 + b2}", "rsum"], w=[f"blk{g}"])
                                        else:
                                            P.op("dve", lambda e, accv=accv, r2=r2, g=g: e.scalar_tensor_tensor(out=blk[g], in0=accv[:, r2, 66:194], scalar=rsum[:, r2:r2 + 1], in1=blk[g], op0=ALU.mult, op1=ALU.add),
                                                 r=[f"pb{4 + b2}", "rsum", f"blk{g}"], w=[f"blk{g}"])
                                yield
                            for g in range(2):
                                tt("dve", bs[g], blk[g], fb, ALU.add, r=[f"blk{g}", "fb"], w=[f"bs{g}"])
                                P.op("dve", lambda e, g=g: e.max(out=m8[:, 0:8], in_=bs[g]), r=[f"bs{g}"], w=["m8a"])
                                P.op("dve", lambda e, g=g: e.match_replace(out=bs2, in_to_replace=m8[:, 0:8], in_values=bs[g], imm_value=NEG), r=[f"bs{g}", "m8a"], w=["bs2"])
                                P.op("dve", lambda e: e.max(out=m8[:, 8:16], in_=bs2), r=["bs2"], w=["m8b"])
                                ts("dve", thr16, m8[:, 15:16], -1.0e29, None, ALU.max, r=["m8b"], w=["thr16"])
                                ts("dve", bm[g], bs[g], thr16, None, ALU.is_ge, r=[f"bs{g}", "thr16"], w=[f"bm{g}"])
                                ts("dve", bm[g], bm[g], -1.0, -MNEG, ALU.add, ALU.mult, r=[f"bm{g}"], w=[f"bm{g}"])
                            yield
                            zero_bank(4)
                            zero_bank(5)
                            def slc_mask(ck):
                                k0 = ck * 512
                                wd = min(512, L - k0)
                                nb4 = wd // 128
                                for g in range(2):
                                    me = mexp[g][ck % 2]
                                    cp("pool", me[:, 0:wd].rearrange("p (j s) -> p j s", s=64),
                                       bm[g][:, ck * 8:ck * 8 + 2 * nb4].unsqueeze(2).to_broadcast([128, 2 * nb4, 64]), r=[f"bm{g}"], w=[f"mexp{g}{ck % 2}"])
                                    for b4 in range(nb4):
                                        kb = ck * 4 + b4
                                        if kb >= nkb - 2:
                                            kbl = kb - (nkb - 2)
                                            tt("pool", me[:, b4 * 128:(b4 + 1) * 128], me[:, b4 * 128:(b4 + 1) * 128], causT[:, kbl, :], ALU.add,
                                               r=[f"mexp{g}{ck % 2}", "causT"], w=[f"mexp{g}{ck % 2}"])

                            slc_mask(0)
                            for ck in range(nck):
                                nb4 = min(512, L - ck * 512) // 128
                                if ck + 1 < nck:
                                    slc_mask(ck + 1)
                                yield from attend([ck * 4 + b4 for b4 in range(nb4)], [0, 1],
                                                  lambda kb, g_: KT[:, 1, kb * 128:(kb + 1) * 128],
                                                  lambda kb, g_: VS[:, kb, g_, 0:66],
                                                  lambda kb, g_, ck=ck: (mexp[g_][ck % 2][:, (kb - ck * 4) * 128:(kb - ck * 4 + 1) * 128], [f"mexp{g_}{ck % 2}"]),
                                                  lambda g_: QTn[g_],
                                                  lambda g_, r_: (pb[4 + g_][:, r_ * 66:r_ * 66 + 66], f"pb{4 + g_}"), kkeys=[kQn])
                            fold_branch([(4, 0, 4, 66), (5, 4, 4, 66)], 1, False)
                            yield
                            wlist = [w_ for w_ in range(6) if 2 * i - 4 + w_ >= 0]
                            kb0 = 2 * i - 4 + wlist[0]
                            nw = len(wlist)
                            dma(kw_t[:, 0:nw * 128], scrKw[:, kb0 * 128:(kb0 + nw) * 128], w=["kw_t"])
                            dma(vw_t[:, 0:nw].rearrange("p t g c -> p t (g c)"), scrVW[kb0:kb0 + nw].rearrange("t p c -> p t c"), w=["vw_t"])
                            zero_bank(4)
                            zero_bank(5)
                            yield from attend([2 * i - 4 + w_ for w_ in wlist], [0, 1],
                                              lambda kb, g_: kw_t[:, (kb - kb0) * 128:(kb - kb0 + 1) * 128],
                                              lambda kb, g_: vw_t[:, kb - kb0, g_, 0:66],
                                              lambda kb, g_: (winT[:, kb - (2 * i - 4), :], ["winT"]),
                                              lambda g_: QTn[g_],
                                              lambda g_, r_: (pb[4 + g_][:, r_ * 66:r_ * 66 + 66], f"pb{4 + g_}"),
                                              kkeys=["kw_t", kQn], vkeys=["vw_t"])
                            fold_branch([(4, 0, 4, 66), (5, 4, 4, 66)], 2, False)
                            yield

                        gen_t = thr_steps()
                        gen_n = nsa_steps()
                        ratio = max(1, (2 * nkb + 24 + NIT) // (NIT + 1))
                        alive_t = alive_n = True
                        while alive_t or alive_n:
                            if alive_t:
                                try:
                                    next(gen_t)
                                except StopIteration:
                                    alive_t = False
                            for _ in range(ratio if alive_t else 1000000):
                                if not alive_n:
                                    break
                                try:
                                    next(gen_n)
                                except StopIteration:
                                    alive_n = False
                        if "thr" in dbg_out and i == dbg_i:
                            P.barrier()
                            dump("thr", lo)
                            dump("blk", blk[0])
                            P.barrier()


                    def stageC():
                        zero_bank(4)
                        zero_bank(5)
                        for ck in range(nck):
                            k0 = ck * 512
                            wd = min(512, L - k0)
                            ts("dve", MR[:, k0:k0 + wd], score[:, k0:k0 + wd], lo, MNEG, ALU.is_lt, ALU.mult, r=["score", "lo"], w=["MR"])
                        yield
                        for ck in range(nck):
                            nb4 = min(512, L - ck * 512) // 128
                            yield from attend([ck * 4 + b4 for b4 in range(nb4)], [0, 1],
                                              lambda kb, hf: KT[:, 0, kb * 128:(kb + 1) * 128],
                                              lambda kb, hf: VD[:, kb, 0:66],
                                              lambda kb, hf: (MR[:, kb * 128:(kb + 1) * 128], ["MR"]),
                                              lambda hf: QTd[:, 4 * hf:4 * hf + 4, :],
                                              lambda hf, r_: (pb[4 + hf][:, r_ * 66:r_ * 66 + 66], f"pb{4 + hf}"),
                                              bank_fn=lambda ui, hf: ui % 4, kkeys=[kQd])
                        Ob3 = Ob.rearrange("p (h d) -> p h d", d=64)
                        for half in range(2):
                            accv = pb[4 + half][:, 0:264].rearrange("p (h c) -> p h c", c=66)
                            P.op("dve", lambda e, accv=accv: e.reciprocal(out=rsum[:, 0:4], in_=accv[:, :, 64]), r=[f"pb{4 + half}"], w=["rsum"])
                            tt("dve", Ob3[:, 4 * half:4 * half + 4, :], accv[:, :, 0:64], rsum[:, 0:4].unsqueeze(2).to_broadcast([128, 4, 64]), ALU.mult,
                               r=[f"pb{4 + half}", "rsum"], w=["Ob"])
                        cp("act", Ob[:, 512:1024], Onsa.rearrange("p h d -> p (h d)"), r=["Onsa"], w=["Ob"])
                        if "O" in dbg_out and i == dbg_i:
                            P.barrier()
                            dump("O", Ob)
                            P.barrier()
                        for a in range(8):
                            tp(pbh[4][:, a * 128:(a + 1) * 128], Ob[:, a * 128:(a + 1) * 128], r=["Ob"], w=["pb4"])
                        cp("act", OT, pbh[4], r=["pb4"], w=["OT"])
                        dma(scrO[i], OT, r=["OT"], w=[f"scrO{i}"])

                        yield

                    return stageA, stageB, stageC

            def run_gen(g_):
                for _ in g_:
                    pass

            def interleave(ga, gb):
                la = lb = True
                while la or lb:
                    if la:
                        try:
                            next(ga)
                        except StopIteration:
                            la = False
                    if lb:
                        try:
                            next(gb)
                        except StopIteration:
                            lb = False

            ilist = list(qlist) if qlist is not None else list(range(nqb))
            sA, sB, sC = make_stages(ilist[0])
            run_gen(sA())
            sB()
            prevC = sC
            for i_ in ilist[1:]:
                sA, sB, sC = make_stages(i_)
                interleave(prevC(), sA())
                sB()
                prevC = sC
            run_gen(prevC())
            P.barrier()
        A.release(mark_resident)

        if 3 in phases:
            A.release(mark_resident0)
            Wout = A.alloc(8 * 1024, BF16).rearrange("p (c n) -> p c n", c=8)
            Wxq = A.alloc(8 * 512, BF16).rearrange("p (c n) -> p c n", c=8)
            Wxo = A.alloc(4 * 1024, BF16).rearrange("p (c n) -> p c n", c=4)
            KxT = A.alloc(4 * 256, BF16).rearrange("p (h m) -> p h m", h=4)
            Vx = A.alloc(2 * 4 * 130, BF16).rearrange("p (t h c) -> p t h c", t=2, h=4)
            gx = A.alloc(24)
            gxq = A.alloc(128)
            gxk = A.alloc(128)
            m3 = A.mark()
            stg = A.alloc(4096 + 2048)
            stgA = stg[:, 0:4096]
            dma(gx[:, 0:8], W["norm2_g"].rearrange("(c p) -> p c", p=128), w=["gx"], allow_slow_non_contiguous=True)
            dma(gx[:, 8:16], W["mem_norm_g"].rearrange("(c p) -> p c", p=128), w=["gx"], allow_slow_non_contiguous=True)
            dma(gx[:, 16:24], W["norm3_g"].rearrange("(c p) -> p c", p=128), w=["gx"], allow_slow_non_contiguous=True)
            bvec(gxq, W["xq_norm_g"], 128, "gxq")
            bvec(gxk, W["xk_norm_g"], 128, "gxk")
            ts("dve", gxq, gxq, 128 ** -0.5, None, ALU.mult, r=["gxq"], w=["gxq"])
            castn = [0]

            def cast_rows(dst3, src3, nchunk, gcol0=None, skey="stg"):
                for c in range(nchunk):
                    castn[0] += 1
                    if gcol0 is None:
                        eng = ["act", "dve", "pool"][castn[0] % 3]
                        cp(eng, dst3[:, c, :], src3[:, c, :], r=[skey], w=[uk("W")])
                    elif castn[0] % 2 == 0:
                        act(dst3[:, c, :], src3[:, c, :], AF.Copy, scale=gx[:, gcol0 + c:gcol0 + c + 1], r=[skey, "gx"], w=[uk("W")])
                    else:
                        ts("dve", dst3[:, c, :], src3[:, c, :], gx[:, gcol0 + c:gcol0 + c + 1], None, ALU.mult, r=[skey, "gx"], w=[uk("W")])

            for hf in range(2):
                s3 = stgA.rearrange("p (c n) -> p c n", c=4)
                dma(s3, W["w_out"].rearrange("(c p) n -> p c n", p=128)[:, hf * 4:(hf + 1) * 4, :], w=["stg"])
                cast_rows(Wout[:, hf * 4:(hf + 1) * 4, :], s3, 4)
            s3 = stgA.rearrange("p (c n) -> p c n", c=8)
            dma(s3, W["w_xq"].rearrange("(c p) n -> p c n", p=128), w=["stg"])
            cast_rows(Wxq, s3, 8, 0)
            s3 = stgA.rearrange("p (c n) -> p c n", c=4)
            dma(s3, W["w_xo"].rearrange("(c p) n -> p c n", p=128), w=["stg"])
            cast_rows(Wxo, s3, 4)
            m3b = A.mark()
            Wxk = A.alloc(8 * 512, BF16).rearrange("p (c n) -> p c n", c=8)
            Wxv = A.alloc(8 * 512, BF16).rearrange("p (c n) -> p c n", c=8)
            ms = A.alloc(8 * 256).rearrange("p (c t) -> p c t", c=8)
            mb = A.alloc(8 * 256, BF16).rearrange("p (c t) -> p c t", c=8)
            msq = A.alloc(8 * 256, BF16).rearrange("p (c t) -> p c t", c=8)
            kx = A.alloc(512)
            kxsq = A.alloc(512)
            kxb = A.alloc(512, BF16)
            rm = A.alloc(2)
            kxss = A.alloc(4)
            s3 = stgA.rearrange("p (c n) -> p c n", c=8)
            dma(s3, W["w_xk"].rearrange("(c p) n -> p c n", p=128), w=["stg"])
            cast_rows(Wxk, s3, 8, 8)
            dma(s3, W["w_xv"].rearrange("(c p) n -> p c n", p=128), w=["stg"])
            cast_rows(Wxv, s3, 8, 8)
            dma(ms, memT.rearrange("(c p) t -> p c t", p=128), w=["ms"])
            cp("act", mb, ms, r=["ms"], w=["mb"])
            tt("dve", msq, ms, ms, ALU.mult, r=["ms"], w=["msq"])
            P.barrier()
            memset("pool", Vx[:, :, :, 128:130], 1.0, w=["Vxones"])
            for mt in range(2):
                for c in range(8):
                    mm(pb[7][:, 2 * mt:2 * mt + 2], msq[:, c, mt * 128:(mt + 1) * 128], ones_bf[:, 0:2], c == 0, c == 7, r=["msq", "ones"], w=["pb7"])
            cp("dve", rm, pb[7][:, 0:4:2], r=["pb7"], w=["rm"])
            rstd_from_ss(rm, 2, 1.0 / 1024, "rm")
            for mt in range(2):
                for c in range(8):
                    mm(pb[0], mb[:, c, mt * 128:(mt + 1) * 128], Wxk[:, c, :], c == 0, c == 7, r=["mb"], w=["pb0"])
                for c in range(8):
                    mm(pb[1], mb[:, c, mt * 128:(mt + 1) * 128], Wxv[:, c, :], c == 0, c == 7, r=["mb"], w=["pb1"])
                act(kx, pb[0], AF.Copy, scale=rm[:, mt:mt + 1], r=["pb0", "rm"], w=["kx"])
                act(Vx[:, mt, :, 0:128], pb[1].rearrange("p (h d) -> p h d", h=4), AF.Copy, scale=rm[:, mt:mt + 1], r=["pb1", "rm"], w=["Vx"])
                tt("dve", kxsq, kx, kx, ALU.mult, r=["kx"], w=["kxsq"])
                red(kxss, kxsq.rearrange("p (h d) -> p h d", h=4), ALU.add, r=["kxsq"], w=["kxss"])
                rstd_from_ss(kxss, 4, 1.0 / 128, "kxss")
                kx3 = kx.rearrange("p (h d) -> p h d", h=4)
                tt("dve", kx3, kx3, kxss.unsqueeze(2).to_broadcast([128, 4, 128]), ALU.mult, r=["kx", "kxss"], w=["kx"])
                tt("dve", kxb.rearrange("p (h d) -> p h d", h=4), kx3, gxk.unsqueeze(1).to_broadcast([128, 4, 128]), ALU.mult, r=["kx", "gxk"], w=["kxb"])
                for h in range(4):
                    tp(pbh[6][:, h * 128:(h + 1) * 128], kxb[:, h * 128:(h + 1) * 128], r=["kxb"], w=["pb6"])
                cp("act", KxT[:, :, mt * 128:(mt + 1) * 128], pbh[6][:, 0:512].rearrange("p (h m) -> p h m", h=4), r=["pb6"], w=["KxT"])
            P.barrier()
            A.release(m3)
            Wff1 = A.alloc(8 * 4096, BF16).rearrange("p (c n) -> p c n", c=8)
            Wff2 = A.alloc(32 * 1024, BF16).rearrange("p (c n) -> p c n", c=32)
            m3 = A.mark()
            stgs = [A.alloc(4096), A.alloc(4096)]
            for c in range(8):
                sk = c % 2
                s3 = stgs[sk].rearrange("p (c n) -> p c n", c=1)
                dma(s3, W["w_ff_in"].rearrange("(c p) n -> p c n", p=128)[:, c:c + 1, :], w=[f"stg{sk}"])
                for hq_ in range(2):
                    castn[0] += 1
                    cs_ = slice(hq_ * 2048, (hq_ + 1) * 2048)
                    if castn[0] % 2 == 0:
                        act(Wff1[:, c, cs_], s3[:, 0, cs_], AF.Copy, scale=gx[:, 16 + c:17 + c], r=[f"stg{sk}", "gx"], w=[uk("W")])
                    else:
                        ts("dve", Wff1[:, c, cs_], s3[:, 0, cs_], gx[:, 16 + c:17 + c], None, ALU.mult, r=[f"stg{sk}", "gx"], w=[uk("W")])
            for c8 in range(8):
                sk = c8 % 2
                s3 = stgs[sk].rearrange("p (c n) -> p c n", c=4)
                dma(s3, W["w_ff_out"].rearrange("(c p) n -> p c n", p=128)[:, c8 * 4:(c8 + 1) * 4, :], w=[f"stg{sk}"])
                cast_rows(Wff2[:, c8 * 4:(c8 + 1) * 4, :], s3, 4, skey=f"stg{sk}")
            P.barrier()
            A.release(m3)
            OTs = A.alloc(1024, BF16).rearrange("p (c t) -> p c t", c=8)
            x1s = [A.alloc(1024) for _ in range(2)]
            xbX = A.alloc(1024, BF16)
            hTX = A.alloc(1024, BF16).rearrange("p (c t) -> p c t", c=8)
            xbM = A.alloc(1024, BF16)
            hTM = A.alloc(1024, BF16).rearrange("p (c t) -> p c t", c=8)
            xblk = A.alloc(1024)
            qx = xblk[:, 0:512]
            qxb = xblk[:, 512:768].bitcast(BF16)
            qxT = xblk[:, 768:1024].bitcast(BF16).rearrange("p (h t) -> p h t", h=4)
            ex = A.alloc(512, BF16)
            oxb = A.alloc(512, BF16)
            oxb_f = xbX.bitcast(F32)
            oxT = A.alloc(512, BF16).rearrange("p (h t) -> p h t", h=4)
            uT = A.alloc(32 * 128, BF16).rearrange("p (f t) -> p f t", f=32)
            rlb = A.alloc(1024)
            rl = [rlb[:, 0:512], rlb[:, 512:1024]]
            r2 = [A.alloc(1) for _ in range(2)]
            qxss = A.alloc(4)
            rs4 = A.alloc(4)

            def norm_T(src, key_src, junk, junk_keys, xb_, kxb, hT_, khT, r2_, kr2):
                act(junk, src, AF.Square, accum_out=r2_, r=[key_src], w=list(junk_keys) + [kr2])
                rstd_from_ss(r2_, 1, 1.0 / 1024, kr2)
                ts("dve", xb_, src, r2_, None, ALU.mult, r=[key_src, kr2], w=[kxb])
                for a in range(8):
                    tp(pbh[6][:, a * 128:(a + 1) * 128], xb_[:, a * 128:(a + 1) * 128], r=[kxb], w=["pb6"])
                cp("act", hT_, pbh[6].rearrange("p (c t) -> p c t", c=8), r=["pb6"], w=[khT])

            def genX(i, slot):
                x1 = x1s[slot]
                kx = f"x1_{slot}"
                tsl = slice(i * 128, (i + 1) * 128)
                dma(OTs, scrO[i].rearrange("p (c t) -> p c t", c=8), w=["OTs"])
                dma(x1, xq[tsl, :], w=[kx])
                for hfo in range(2):
                    for c in range(8):
                        mm(pb[4 + hfo], OTs[:, c, :], Wout[:, c, hfo * 512:(hfo + 1) * 512], c == 0, c == 7, r=["OTs"], w=[f"pb{4 + hfo}"])
                    tt("dve", x1[:, hfo * 512:(hfo + 1) * 512], pb[4 + hfo], x1[:, hfo * 512:(hfo + 1) * 512], ALU.add, r=[f"pb{4 + hfo}", kx], w=[kx])
                yield
                if "x1" in dbg_out and i == dbg_i:
                    P.barrier()
                    dump("x1", x1)
                    P.barrier()
                norm_T(x1, kx, xblk, ["qx", "qxb", "qxT"], xbX, "xbX", hTX, "hTX", r2[0], "r2X")
                yield
                for c in range(8):
                    mm(pb[6], hTX[:, c, :], Wxq[:, c, :], c == 0, c == 7, r=["hTX"], w=["pb6"])
                cp("act", qx, pb[6], r=["pb6"], w=["qx"])
                yield
                tt("dve", oxb_f, qx, qx, ALU.mult, r=["qx"], w=["xbX"])
                red(qxss, oxb_f.rearrange("p (h d) -> p h d", h=4), ALU.add, r=["xbX"], w=["qxss"])
                rstd_from_ss(qxss, 4, 1.0 / 128, "qxss")
                qx3 = qx.rearrange("p (h d) -> p h d", h=4)
                tt("dve", qx3, qx3, qxss.unsqueeze(2).to_broadcast([128, 4, 128]), ALU.mult, r=["qx", "qxss"], w=["qx"])
                tt("dve", qxb.rearrange("p (h d) -> p h d", h=4), qx3, gxq.unsqueeze(1).to_broadcast([128, 4, 128]), ALU.mult, r=["qx", "gxq"], w=["qxb"])
                yield
                for h in range(4):
                    tp(pbh[6][:, h * 128:(h + 1) * 128], qxb[:, h * 128:(h + 1) * 128], r=["qxb"], w=["pb6"])
                cp("dve", qxT, pbh[6][:, 0:512].rearrange("p (h t) -> p h t", h=4), r=["pb6"], w=["qxT"])
                yield
                zero_bank(4)
                zero_bank(5)
                for mt in range(2):
                    for h in range(4):
                        mm(pb[7][:, h * 128:(h + 1) * 128], KxT[:, h, mt * 128:(mt + 1) * 128], qxT[:, h, :], True, True, r=["qxT", "KxT"], w=["pb7"])
                    act(ex, pb[7], AF.Exp, r=["pb7"], w=["ex"])
                    for h in range(4):
                        mm(pb[4 + h // 2][:, (h % 2) * 130:(h % 2) * 130 + 130], ex[:, h * 128:(h + 1) * 128], Vx[:, mt, h, 0:130], False, mt == 1,
                           r=["ex", "Vx"], w=[f"pb{4 + h // 2}"], skip_group_check=True)
                for b2 in range(2):
                    accv = pb[4 + b2][:, 0:260].rearrange("p (h c) -> p h c", c=130)
                    P.op("dve", lambda e, accv=accv: e.reciprocal(out=rs4[:, 0:2], in_=accv[:, :, 128]), r=[f"pb{4 + b2}"], w=["rs4"])
                    tt("dve", oxb[:, b2 * 256:(b2 + 1) * 256].rearrange("p (h d) -> p h d", h=2), accv[:, :, 0:128],
                       rs4[:, 0:2].unsqueeze(2).to_broadcast([128, 2, 128]), ALU.mult, r=[f"pb{4 + b2}", "rs4"], w=["oxb"])
                yield
                for h in range(4):
                    tp(pbh[6][:, h * 128:(h + 1) * 128], oxb[:, h * 128:(h + 1) * 128], r=["oxb"], w=["pb6"])
                cp("act", oxT, pbh[6][:, 0:512].rearrange("p (h t) -> p h t", h=4), r=["pb6"], w=["oxT"])
                yield
                for hfo in range(2):
                    for h in range(4):
                        mm(pb[4 + hfo], oxT[:, h, :], Wxo[:, h, hfo * 512:(hfo + 1) * 512], h == 0, h == 3, r=["oxT"], w=[f"pb{4 + hfo}"])
                    tt("dve", x1[:, hfo * 512:(hfo + 1) * 512], pb[4 + hfo], x1[:, hfo * 512:(hfo + 1) * 512], ALU.add, r=[f"pb{4 + hfo}", kx], w=[kx])
                if "x2" in dbg_out and i == dbg_i:
                    P.barrier()
                    dump("x2", x1)
                    P.barrier()
                yield

            def genM(i, slot):
                x1 = x1s[slot]
                kx = f"x1_{slot}"
                tsl = slice(i * 128, (i + 1) * 128)
                norm_T(x1, kx, rlb, ["rl0", "rl1"], xbM, "xbM", hTM, "hTM", r2[1], "r2M")
                yield
                for f4 in range(8):
                    bk = 2 + f4 % 2
                    for fi in range(4):
                        fc = f4 * 4 + fi
                        for c in range(8):
                            mm(pb[bk][:, fi * 128:(fi + 1) * 128], Wff1[:, c, fc * 128:(fc + 1) * 128], hTM[:, c, :], c == 0, c == 7, r=["hTM"], w=[f"pb{bk}"])
                    act(rl[f4 % 2], pb[bk], AF.Relu, r=[f"pb{bk}"], w=[f"rl{f4 % 2}"])
                    tt("pool" if f4 % 2 else "dve", uT[:, f4 * 4:(f4 + 1) * 4, :].rearrange("p f t -> p (f t)"), rl[f4 % 2], rl[f4 % 2], ALU.mult, r=[f"rl{f4 % 2}"], w=[f"uT{f4}"])
                    yield
                for fc in range(32):
                    for hfo in range(2):
                        mm(pb[hfo], uT[:, fc, :], Wff2[:, fc, hfo * 512:(hfo + 1) * 512], fc == 0, fc == 31, r=[f"uT{fc // 4}"], w=[f"pb{hfo}"])
                    if fc % 4 == 3:
                        yield
                for hfo in range(2):
                    tt("dve", x1[:, hfo * 512:(hfo + 1) * 512], pb[hfo], x1[:, hfo * 512:(hfo + 1) * 512], ALU.add, r=[f"pb{hfo}", kx], w=[kx])
                dma(out[tsl, :], x1, r=[kx], w=[f"out{i}"])
                yield

            def run_gen3(g_):
                for _ in g_:
                    pass

            def interleave3(ga, gb):
                la = lb = True
                while la or lb:
                    if la:
                        try:
                            next(ga)
                        except StopIteration:
                            la = False
                    if lb:
                        try:
                            next(gb)
                        except StopIteration:
                            lb = False

            ilist3 = list(qlist) if qlist is not None else list(range(nqb))
            run_gen3(genX(ilist3[0], 0))
            for k3 in range(len(ilist3)):
                if k3 + 1 < len(ilist3):
                    interleave3(genM(ilist3[k3], k3 % 2), genX(ilist3[k3 + 1], (k3 + 1) % 2))
                else:
                    run_gen3(genM(ilist3[k3], k3 % 2))
        P.barrier()
        print("arena peak (KiB/partition):", A.peak * 4 / 1024, "ops:", len(P.ops))
        P.emit()
    return nc


def host_prepare(inputs):
    f32 = np.float32
    x = np.asarray(inputs["x"], dtype=f32)
    mem = np.asarray(inputs["mem"], dtype=f32)
    wts = {}
    for k, v in inputs.items():
        if k in ("x", "mem"):
            continue
        v = np.asarray(v, dtype=f32)
        wts[k] = np.ascontiguousarray(v.reshape(v.shape[1:]))
    half = 32
    inv = (f32(10000.0) ** (-(np.arange(half, dtype=f32) / f32(half)))).astype(f32)

    def rope_tab(pos):
        ang = pos.astype(f32)[:, None] * inv[None, :]
        return np.concatenate([np.cos(ang), np.sin(ang)], axis=1).astype(f32)
    ropeK = rope_tab(np.arange(SEQ))
    ropeC = rope_tab(np.minimum(np.arange(512) * 16 + 31, 16 * 511 + 31))
    n = np.arange(512)
    j = np.arange(128)
    ovl = ((n[:, None] * 16 < j[None, :] * 64 + 64) & (n[:, None] * 16 + 32 > j[None, :] * 64)).astype(f32)
    ovl[511] = 0.0
    cn = np.tile((16 * np.arange(512) + 31).astype(f32)[None, :], (128, 1))
    pow2 = np.tile((2.0 ** -(np.arange(NIT + 1) + 1)).astype(f32)[None, :], (128, 1))
    ident = np.eye(128, dtype=f32)
    xT_b = [np.ascontiguousarray(x[b].T) for b in range(4)]
    memT_b = [np.ascontiguousarray(mem[b].T) for b in range(4)]
    maps = []
    m = np.arange(128)
    for c in range(8):
        b, p = c // 2, c % 2
        tok = (np.arange(NQB)[:, None] * 2 + p) * 128 + m[None, :]
        tok = tok.reshape(-1)
        xq = np.ascontiguousarray(x[b][tok])
        xqT = np.ascontiguousarray(xq.T)
        sp = np.arange(256)
        cmask = np.where(sp[None, :] <= (128 * p + m)[:, None], 0.0, NEG).astype(f32)
        s = np.arange(128)
        causT = np.zeros((128, 2, 128), f32)
        for kbl in range(2):
            causT[:, kbl, :] = (128 * kbl + s[None, :] <= 128 * p + m[:, None])
        winT = np.zeros((128, 6, 128), f32)
        for w in range(6):
            rel = 128 * (w - 4 - p) + s[None, :] - m[:, None]
            winT[:, w, :] = (rel <= 0) & (rel > -512)
        tq = (128 * p + m).astype(f32)[:, None].copy()
        fb = np.zeros((NQB, 128, 128), f32)
        for i in range(NQB):
            t = (2 * i + p) * 128 + m
            tb = t // 64
            forced = (j[None, :] == 0) | (j[None, :] == tb[:, None]) | (j[None, :] == tb[:, None] - 1)
            adm = j[None, :] * 64 <= t[:, None]
            fb[i] = np.where(adm, np.where(forced, 1e4, 0.0), NEG)
        d = dict(xT=xT_b[b], xqT=xqT, xq=xq, memT=memT_b[b], c_ident=ident, c_ropeK=ropeK, c_ropeQ=ropeK[tok],
                 c_ropeC=ropeC, c_cmask=cmask, c_causT=causT.reshape(128, 256), c_winT=winT.reshape(128, 768),
                 c_tq=tq, c_cn=cn, c_fbias=fb.reshape(NQB * 128, 128), c_ovl=ovl, c_pow2=pow2)
        d.update(wts)
        maps.append(d)
    return maps


def kernel(**inputs):
    maps = host_prepare(inputs)
    nc = build_program()
    res = run_bass_kernel_spmd(nc, maps, core_ids=list(range(8)))
    outp = np.zeros((4, SEQ, D_MODEL), np.float32)
    for c in range(8):
        b, p = c // 2, c % 2
        o = np.asarray(res.results[c]["out"]).reshape(NQB, 128, D_MODEL)
        for i in range(NQB):
            qb = 2 * i + p
            outp[b, qb * 128:(qb + 1) * 128] = o[i]
    return outp
```
